# Optimizing a Trainium2 kernel written in Bass

```python
import math
import jax, jax.numpy as jnp
from jax import lax
import numpy as np

D_MODEL = 1024
BATCH = 1
SEQ = 16384
DEPTH = 2
DEC_BATCH = 32
DEC_SEQ = 16
PAST_LEN = 2048

CHUNK = 64
Q_BLOCK = 128
CONV_A_CH = 512
CONV_A_WIDTH = 31
GDN_HEADS = 4
GDN_DK = 128
GDN_DV = 128
GDN_CONV_WIDTH = 4
GDN_QKV_WIDTH = 2 * GDN_HEADS * GDN_DK + GDN_HEADS * GDN_DV
MLA_HEADS = 4
MLA_Q_LORA = 384
MLA_KV_LORA = 256
MLA_NOPE_DIM = 128
MLA_ROPE_DIM = 64
MLA_V_DIM = 128
ROPE_THETA = 10000.0
N_BRANCH = 3
BRANCH_WIDTH = 512
D_FF = 2816
FFN_CONV_WIDTH = 3
DN_ALPHA = (2 * DEPTH) ** 0.25
DN_BETA = (8 * DEPTH) ** -0.25
LN_EPS = 1e-5
RMS_EPS = 1e-6

IN_SIZES = (
    2 * CONV_A_CH,
    GDN_QKV_WIDTH,
    GDN_HEADS * GDN_DV,
    GDN_HEADS,
    GDN_HEADS,
    MLA_Q_LORA,
    MLA_KV_LORA,
    MLA_ROPE_DIM,
    N_BRANCH * D_MODEL,
)
IN_WIDTH = sum(IN_SIZES)

kernel_name = 'hybrid_streaming_encoder_step'


def split_columns(z, sizes):
    offsets = np.cumsum(np.array(sizes))[:-1].tolist()
    return jnp.split(z, offsets, axis=-1)


def layer_norm(x, g, b):
    xf = x.astype(jnp.float32)
    mu = jnp.mean(xf, axis=-1, keepdims=True)
    var = jnp.mean(jnp.square(xf - mu), axis=-1, keepdims=True)
    return ((xf - mu) * lax.rsqrt(var + LN_EPS) * g.astype(jnp.float32) + b.astype(jnp.float32)).astype(x.dtype)


def rms_norm(x, g):
    xf = x.astype(jnp.float32)
    return (xf * lax.rsqrt(jnp.mean(xf * xf, axis=-1, keepdims=True) + RMS_EPS) * g.astype(jnp.float32)).astype(x.dtype)


def l2_normalize(x):
    return x * lax.rsqrt(jnp.sum(x * x, axis=-1, keepdims=True) + RMS_EPS)


def causal_dwconv(x, hist, w):
    k = w.shape[0]
    xp = jnp.concatenate([hist.astype(x.dtype), x], axis=1)
    y = lax.conv_general_dilated(xp, w[:, None, :].astype(x.dtype), window_strides=(1,), padding='VALID',
                                 dimension_numbers=('NWC', 'WIO', 'NWC'), feature_group_count=x.shape[-1])
    return y, xp[:, xp.shape[1] - (k - 1):]


def rope_tables(pos):
    half = MLA_ROPE_DIM // 2
    inv_freq = ROPE_THETA ** (-jnp.arange(half, dtype=jnp.float32) / half)
    ang = pos.astype(jnp.float32)[:, None] * inv_freq[None, :]
    return jnp.cos(ang), jnp.sin(ang)


def apply_rope(x, cos, sin):
    x1, x2 = jnp.split(x.astype(jnp.float32), 2, axis=-1)
    return jnp.concatenate([x1 * cos - x2 * sin, x2 * cos + x1 * sin], axis=-1).astype(x.dtype)


def chunk_attention(q, k, v, q_pos, k_pos):
    b, tq, h, dq = q.shape
    dv = v.shape[-1]
    scale = dq ** -0.5
    k_chunk = k_pos // CHUNK

    def attend(qb, qp):
        s = jnp.einsum('bqhd,bkhd->bhqk', qb, k, preferred_element_type=jnp.float32) * scale
        mask = k_chunk[None, :] <= (qp // CHUNK)[:, None]
        p = jax.nn.softmax(jnp.where(mask, s, -jnp.inf), axis=-1)
        return jnp.einsum('bhqk,bkhd->bqhd', p.astype(v.dtype), v)

    if tq > Q_BLOCK and tq % Q_BLOCK == 0:
        nb = tq // Q_BLOCK
        qb = jnp.moveaxis(q.reshape(b, nb, Q_BLOCK, h, dq), 1, 0)
        qp = q_pos.reshape(nb, Q_BLOCK)
        o = lax.map(lambda a: attend(a[0], a[1]), (qb, qp))
        return jnp.moveaxis(o, 0, 1).reshape(b, tq, h, dv)
    return attend(q, q_pos)


def gated_delta_rule(q, k, v, g, beta, state):
    b, t, h, dk = q.shape
    dv = v.shape[-1]
    c = min(CHUNK, t)
    n = t // c

    def chunks(a):
        return jnp.moveaxis(a.reshape((b, n, c, h) + a.shape[3:]), (1, 3), (0, 2))

    q, k, v, g, beta = chunks(q), chunks(k), chunks(v), chunks(g), chunks(beta)
    g = jnp.cumsum(g, axis=-1)
    tri = jnp.tril(jnp.ones((c, c), dtype=bool))
    strict = jnp.tril(jnp.ones((c, c), dtype=bool), -1)
    diff = g[..., :, None] - g[..., None, :]
    decay = jnp.where(tri, jnp.exp(jnp.where(tri, diff, 0.0)), 0.0)
    kk = jnp.einsum('nbhid,nbhjd->nbhij', k, k)
    lmat = jnp.where(strict, beta[..., :, None] * kk * decay, 0.0)
    rhs = jnp.concatenate([v * beta[..., None], k * (beta * jnp.exp(g))[..., None]], axis=-1)
    sol = lax.linalg.triangular_solve(jnp.eye(c, dtype=jnp.float32) + lmat, rhs,
                                      left_side=True, lower=True, unit_diagonal=True)
    u, w = sol[..., :dv], sol[..., dv:]
    qk = jnp.where(tri, jnp.einsum('nbhid,nbhjd->nbhij', q, k) * decay, 0.0)
    q_dec = q * jnp.exp(g)[..., None]
    g_last = g[..., -1]
    k_dec = k * jnp.exp(g_last[..., None] - g)[..., None]

    def step(s, xs):
        u_c, w_c, qk_c, qd_c, kd_c, gl_c = xs
        v_new = u_c - jnp.einsum('bhck,bhkv->bhcv', w_c, s)
        o = jnp.einsum('bhck,bhkv->bhcv', qd_c, s) + jnp.einsum('bhij,bhjv->bhiv', qk_c, v_new)
        s = s * jnp.exp(gl_c)[..., None, None] + jnp.einsum('bhck,bhcv->bhkv', kd_c, v_new)
        return s, o

    state, o = lax.scan(step, state, (u, w, qk, q_dec, k_dec, g_last))
    o = jnp.moveaxis(o, (0, 2), (1, 3)).reshape(b, t, h, dv)
    return o, state


def conv_module(pre, hist, w, bias, ln_g, ln_b):
    a, gate = jnp.split(pre, 2, axis=-1)
    u = a * jax.nn.sigmoid(gate)
    y, hist = causal_dwconv(u, hist, w)
    return jax.nn.silu(layer_norm(y + bias, ln_g, ln_b)), hist


def gated_deltanet(qkv, z, beta_pre, dec_pre, hist, state, conv_w, a_log, dt_bias, norm_g):
    b, t, _ = qkv.shape
    qkv, hist = causal_dwconv(qkv, hist, conv_w)
    qkv = jax.nn.silu(qkv.astype(jnp.float32))
    nk = GDN_HEADS * GDN_DK
    q = l2_normalize(qkv[..., :nk].reshape(b, t, GDN_HEADS, GDN_DK)) * (GDN_DK ** -0.5)
    k = l2_normalize(qkv[..., nk:2 * nk].reshape(b, t, GDN_HEADS, GDN_DK))
    v = qkv[..., 2 * nk:].reshape(b, t, GDN_HEADS, GDN_DV)
    beta = jax.nn.sigmoid(beta_pre.astype(jnp.float32))
    g = -jnp.exp(a_log.astype(jnp.float32)) * jax.nn.softplus(dec_pre.astype(jnp.float32) + dt_bias.astype(jnp.float32))
    o, state = gated_delta_rule(q, k, v, g, beta, state.astype(jnp.float32))
    o = rms_norm(o, norm_g) * jax.nn.silu(z.astype(jnp.float32).reshape(b, t, GDN_HEADS, GDN_DV))
    return o.reshape(b, t, GDN_HEADS * GDN_DV), hist, state


def latent_attention(q_lat, kv_lat, k_pe, cache_lat, cache_kpe, q_norm_g, kv_norm_g, w_uq, w_ukv):
    b, t, _ = q_lat.shape
    past = cache_lat.shape[1]
    tk = past + t
    q_pos = past + jnp.arange(t, dtype=jnp.int32)
    k_pos = jnp.arange(tk, dtype=jnp.int32)
    cos, sin = rope_tables(q_pos)
    q = (rms_norm(q_lat, q_norm_g) @ w_uq).reshape(b, t, MLA_HEADS, MLA_NOPE_DIM + MLA_ROPE_DIM)
    q = jnp.concatenate([q[..., :MLA_NOPE_DIM],
                         apply_rope(q[..., MLA_NOPE_DIM:], cos[:, None, :], sin[:, None, :])], axis=-1)
    lat_new = rms_norm(kv_lat, kv_norm_g)
    kpe_new = apply_rope(k_pe, cos, sin)
    lat_all = jnp.concatenate([cache_lat.astype(lat_new.dtype), lat_new], axis=1)
    kpe_all = jnp.concatenate([cache_kpe.astype(kpe_new.dtype), kpe_new], axis=1)
    kv = (lat_all @ w_ukv).reshape(b, tk, MLA_HEADS, MLA_NOPE_DIM + MLA_V_DIM)
    k = jnp.concatenate([kv[..., :MLA_NOPE_DIM],
                         jnp.broadcast_to(kpe_all[:, :, None, :], (b, tk, MLA_HEADS, MLA_ROPE_DIM))], axis=-1)
    o = chunk_attention(q, k.astype(q.dtype), kv[..., MLA_NOPE_DIM:], q_pos, k_pos)
    return o.reshape(b, t, MLA_HEADS * MLA_V_DIM), lat_new, kpe_new


def run_layer(x, c, cache_lat, cache_kpe, hist_a, hist_b, state_b, hist_f, p):
    b, t, _ = x.shape
    mod = jax.nn.silu(c) @ p['w_ada'] + p['b_ada']
    sh1, sc1, g1, sh2, sc2, g2 = jnp.split(mod[:, None, :], 6, axis=-1)
    h = x * (1 + sc1) + sh1
    (pre_a, qkv_b, z_b, beta_b, dec_b, q_lat, kv_lat, k_pe, gate_pre) = split_columns(h @ p['w_in'], IN_SIZES)
    y_a, hist_a = conv_module(pre_a, hist_a, p['conv_a_w'], p['conv_a_b'], p['ln_a_g'], p['ln_a_b'])
    y_b, hist_b, state_b = gated_deltanet(qkv_b, z_b, beta_b, dec_b, hist_b, state_b, p['gdn_conv_w'],
                                          p['gdn_a_log'], p['gdn_dt_bias'], p['gdn_norm_g'])
    y_c, lat_new, kpe_new = latent_attention(q_lat, kv_lat, k_pe, cache_lat, cache_kpe, p['mla_q_norm_g'],
                                             p['mla_kv_norm_g'], p['mla_w_uq'], p['mla_w_ukv'])
    branches = jnp.stack([y_a.astype(x.dtype), y_b.astype(x.dtype), y_c.astype(x.dtype)], axis=0)
    proj = jnp.einsum('nbtc,ncd->btnd', branches, p['w_branch'])
    gates = jax.nn.sigmoid(gate_pre.astype(jnp.float32)).reshape(b, t, N_BRANCH, D_MODEL)
    merged = jnp.sum(gates * proj, axis=2).astype(x.dtype)
    x = layer_norm(DN_ALPHA * x + (1 + g1) * (merged @ p['w_out']), p['ln1_g'], p['ln1_b'])
    h = x * (1 + sc2) + sh2
    a, v = jnp.split(h @ p['w_up'], 2, axis=-1)
    a, hist_f = causal_dwconv(a, hist_f, p['ffn_conv_w'])
    y = (jax.nn.silu(a + p['ffn_conv_b']) * v) @ p['w_down']
    x = layer_norm(DN_ALPHA * x + (1 + g2) * y, p['ln2_g'], p['ln2_b'])
    return x, (lat_new, kpe_new, hist_a, hist_b, state_b, hist_f)


def setup_inputs(seed: int = 0) -> dict:
    key = jax.random.key(seed)
    ks = iter(jax.random.split(key, 64))

    def nrm(shape, scale):
        return scale * jax.random.normal(next(ks), shape, jnp.float32)

    def gain(shape):
        return 1.0 + nrm(shape, 0.02)

    log_dt = jax.random.uniform(next(ks), (DEPTH, GDN_HEADS), jnp.float32, math.log(1e-3), math.log(1e-1))
    dt = jnp.exp(log_dt)
    a_init = jax.random.uniform(next(ks), (DEPTH, GDN_HEADS), jnp.float32, 1.0, 16.0)
    return {
        'x_prompt': nrm((BATCH, SEQ, D_MODEL), 1.0),
        'x_sample': nrm((DEC_BATCH, DEC_SEQ, D_MODEL), 1.0),
        'cache_mla_latent': nrm((DEPTH, DEC_BATCH, PAST_LEN, MLA_KV_LORA), 1.0),
        'cache_mla_kpe': nrm((DEPTH, DEC_BATCH, PAST_LEN, MLA_ROPE_DIM), 1.0),
        'state_conv_a': nrm((DEPTH, DEC_BATCH, CONV_A_WIDTH - 1, CONV_A_CH), 0.5),
        'state_gdn_conv': nrm((DEPTH, DEC_BATCH, GDN_CONV_WIDTH - 1, GDN_QKV_WIDTH), 1.0),
        'state_gdn': nrm((DEPTH, DEC_BATCH, GDN_HEADS, GDN_DK, GDN_DV), 0.1),
        'state_ffn_conv': nrm((DEPTH, DEC_BATCH, FFN_CONV_WIDTH - 1, D_FF), 1.0),
        'c_prompt': nrm((BATCH, D_MODEL), 1.0),
        'c_sample': nrm((DEC_BATCH, D_MODEL), 1.0),
        'ln0_g': gain((D_MODEL,)),
        'ln0_b': nrm((D_MODEL,), 0.02),
        'w_ada': nrm((DEPTH, D_MODEL, 6 * D_MODEL), 0.2 * D_MODEL ** -0.5),
        'b_ada': nrm((DEPTH, 6 * D_MODEL), 0.01),
        'w_in': nrm((DEPTH, D_MODEL, IN_WIDTH), D_MODEL ** -0.5),
        'conv_a_w': nrm((DEPTH, CONV_A_WIDTH, CONV_A_CH), CONV_A_WIDTH ** -0.5),
        'conv_a_b': nrm((DEPTH, CONV_A_CH), 0.02),
        'ln_a_g': gain((DEPTH, CONV_A_CH)),
        'ln_a_b': nrm((DEPTH, CONV_A_CH), 0.02),
        'gdn_conv_w': nrm((DEPTH, GDN_CONV_WIDTH, GDN_QKV_WIDTH), GDN_CONV_WIDTH ** -0.5),
        'gdn_a_log': jnp.log(a_init),
        'gdn_dt_bias': dt + jnp.log(-jnp.expm1(-dt)),
        'gdn_norm_g': gain((DEPTH, GDN_DV)),
        'mla_q_norm_g': gain((DEPTH, MLA_Q_LORA)),
        'mla_kv_norm_g': gain((DEPTH, MLA_KV_LORA)),
        'mla_w_uq': nrm((DEPTH, MLA_Q_LORA, MLA_HEADS * (MLA_NOPE_DIM + MLA_ROPE_DIM)), MLA_Q_LORA ** -0.5),
        'mla_w_ukv': nrm((DEPTH, MLA_KV_LORA, MLA_HEADS * (MLA_NOPE_DIM + MLA_V_DIM)), MLA_KV_LORA ** -0.5),
        'w_branch': nrm((DEPTH, N_BRANCH, BRANCH_WIDTH, D_MODEL), BRANCH_WIDTH ** -0.5),
        'w_out': nrm((DEPTH, D_MODEL, D_MODEL), DN_BETA * D_MODEL ** -0.5),
        'ln1_g': gain((DEPTH, D_MODEL)),
        'ln1_b': nrm((DEPTH, D_MODEL), 0.02),
        'w_up': nrm((DEPTH, D_MODEL, 2 * D_FF), D_MODEL ** -0.5),
        'ffn_conv_w': nrm((DEPTH, FFN_CONV_WIDTH, D_FF), FFN_CONV_WIDTH ** -0.5),
        'ffn_conv_b': nrm((DEPTH, D_FF), 0.02),
        'w_down': nrm((DEPTH, D_FF, D_MODEL), DN_BETA * D_FF ** -0.5),
        'ln2_g': gain((DEPTH, D_MODEL)),
        'ln2_b': nrm((DEPTH, D_MODEL), 0.02),
    }


def reference(x_prompt, x_sample, cache_mla_latent, cache_mla_kpe, state_conv_a, state_gdn_conv, state_gdn,
              state_ffn_conv, c_prompt, c_sample, ln0_g, ln0_b, w_ada, b_ada, w_in, conv_a_w, conv_a_b,
              ln_a_g, ln_a_b, gdn_conv_w, gdn_a_log, gdn_dt_bias, gdn_norm_g, mla_q_norm_g, mla_kv_norm_g,
              mla_w_uq, mla_w_ukv, w_branch, w_out, ln1_g, ln1_b, w_up, ffn_conv_w, ffn_conv_b, w_down,
              ln2_g, ln2_b):
    def layer_params(l):
        return {'w_ada': w_ada[l], 'b_ada': b_ada[l], 'w_in': w_in[l], 'conv_a_w': conv_a_w[l],
                'conv_a_b': conv_a_b[l], 'ln_a_g': ln_a_g[l], 'ln_a_b': ln_a_b[l], 'gdn_conv_w': gdn_conv_w[l],
                'gdn_a_log': gdn_a_log[l], 'gdn_dt_bias': gdn_dt_bias[l], 'gdn_norm_g': gdn_norm_g[l],
                'mla_q_norm_g': mla_q_norm_g[l], 'mla_kv_norm_g': mla_kv_norm_g[l], 'mla_w_uq': mla_w_uq[l],
                'mla_w_ukv': mla_w_ukv[l], 'w_branch': w_branch[l], 'w_out': w_out[l], 'ln1_g': ln1_g[l],
                'ln1_b': ln1_b[l], 'w_up': w_up[l], 'ffn_conv_w': ffn_conv_w[l], 'ffn_conv_b': ffn_conv_b[l],
                'w_down': w_down[l], 'ln2_g': ln2_g[l], 'ln2_b': ln2_b[l]}

    bp = x_prompt.shape[0]
    dt = x_prompt.dtype
    xp = layer_norm(x_prompt, ln0_g, ln0_b)
    xs = layer_norm(x_sample, ln0_g, ln0_b)
    p_states, s_states = [], []
    for l in range(DEPTH):
        p = layer_params(l)
        xp, st_p = run_layer(xp, c_prompt,
                             jnp.zeros((bp, 0, MLA_KV_LORA), dt), jnp.zeros((bp, 0, MLA_ROPE_DIM), dt),
                             jnp.zeros((bp, CONV_A_WIDTH - 1, CONV_A_CH), dt),
                             jnp.zeros((bp, GDN_CONV_WIDTH - 1, GDN_QKV_WIDTH), dt),
                             jnp.zeros((bp, GDN_HEADS, GDN_DK, GDN_DV), jnp.float32),
                             jnp.zeros((bp, FFN_CONV_WIDTH - 1, D_FF), dt), p)
        xs, st_s = run_layer(xs, c_sample, cache_mla_latent[l], cache_mla_kpe[l], state_conv_a[l],
                             state_gdn_conv[l], state_gdn[l], state_ffn_conv[l], p)
        p_states.append(st_p)
        s_states.append(st_s)
    p_lat, p_kpe, p_conv_a, p_gdn_conv, p_gdn, p_ffn = [jnp.stack(z, axis=0) for z in zip(*p_states)]
    s_lat, s_kpe, s_conv_a, s_gdn_conv, s_gdn, s_ffn = [jnp.stack(z, axis=0) for z in zip(*s_states)]
    return (xp, xs, p_lat, p_kpe, p_conv_a, p_gdn_conv, p_gdn, p_ffn,
            s_lat, s_kpe, s_conv_a, s_gdn_conv, s_gdn, s_ffn)
```

```python
from contextlib import ExitStack
import numpy as np
import concourse.bass as bass
import concourse.mybir as mybir

F32 = mybir.dt.float32
BF16 = mybir.dt.bfloat16
I32 = mybir.dt.int32
AF = mybir.ActivationFunctionType
ALU = mybir.AluOpType

ENGS = ("pe", "act", "dve", "pool", "sp")
NDMA = 12


class Trk:
    __slots__ = ("w", "r", "name", "multi", "serial")

    def __init__(self, name="", multi=False):
        self.w = {}
        self.r = {}
        self.name = name
        self.multi = multi
        self.serial = False


class T:
    def __init__(self, handle, name):
        self.h = handle
        self.k = Trk(name)
        self.shape = tuple(handle.shape)

    def __getitem__(self, key):
        return V(self.h[key] if not isinstance(key, tuple) or True else None, [self.k])

    def ap(self):
        return V(self.h.ap() if hasattr(self.h, "ap") else self.h[:], [self.k])


class V:
    def __init__(self, ap, ks):
        self.ap = ap
        self.ks = ks

    def __getitem__(self, key):
        return V(self.ap[key], self.ks)

    def rearrange(self, s, **kw):
        return V(self.ap.rearrange(s, **kw), self.ks)

    def bitcast(self, dt):
        return V(self.ap.bitcast(dt), self.ks)

    def broadcast_to(self, shape):
        return V(self.ap.broadcast_to(shape), self.ks)

    def to_broadcast(self, shape):
        return V(self.ap.to_broadcast(shape), self.ks)

    def partition_broadcast(self, n):
        return V(self.ap.partition_broadcast(n), self.ks)

    def unsqueeze(self, a):
        return V(self.ap.unsqueeze(a), self.ks)

    def bc(self, n):
        sh = list(self.ap.shape)
        return V(self.ap.unsqueeze(len(sh)).to_broadcast(sh + [n]), self.ks)

    @property
    def shape(self):
        return tuple(self.ap.shape)


def _ap(x):
    return x.ap if isinstance(x, V) else x


class Prog:
    def __init__(self, nc):
        self.nc = nc
        self.es = ExitStack()
        self.q = {e: [] for e in ENGS}
        self.cnt = {e: 0 for e in ENGS}
        self.seen = {e: {} for e in ENGS}
        self.sem = {}
        for e in ENGS:
            self.sem[("c", e)] = self.es.enter_context(nc.semaphore("s_" + e))
        self.dtot = {}
        for qn in ("sp", "pool"):
            for i in range(NDMA):
                k = ("d", qn, i)
                self.sem[k] = self.es.enter_context(nc.semaphore("d_%s%d" % (qn, i)))
                self.dtot[k] = 0
        self.drr = {"sp": 0, "pool": 0}
        self.ntile = 0
        self.psum_free = None

    def sb(self, shape, dt=F32, name=None, multi=False):
        self.ntile += 1
        name = name or "t%d" % self.ntile
        h = self.es.enter_context(self.nc.sbuf_tensor(name, list(shape), dt))
        t = T(h, name)
        t.k.multi = multi
        return t

    def ps(self, shape, dt=F32, name=None):
        self.ntile += 1
        name = name or "p%d" % self.ntile
        h = self.es.enter_context(self.nc.psum_tensor(name, list(shape), dt))
        t = T(h, name)
        t.k.serial = True
        return t

    def dram(self, name, shape, dt=F32, kind=None, multi=False):
        if kind is None:
            h = self.nc.dram_tensor(name, list(shape), dt)
        else:
            h = self.nc.dram_tensor(name, list(shape), dt, kind=kind)
        t = T(h, name)
        t.k.multi = multi
        return t

    def emit(self, eng, fn, reads, writes, dma=False, accum_pe=False):
        rk, wk = [], []
        for x in reads:
            if x is None:
                continue
            rk.extend(x.ks if isinstance(x, V) else [x.k])
        for x in writes:
            rk_ = x.ks if isinstance(x, V) else [x.k]
            wk.extend(rk_)
        deps = {}

        def need(ev):
            if ev is None:
                return
            k, v = ev
            if deps.get(k, 0) < v:
                deps[k] = v

        myk = ("c", eng)
        for k in rk:
            for sk, v in k.w.items():
                need((sk, v))
            if k.serial:
                for sk, v in k.r.items():
                    if sk != myk:
                        need((sk, v))
        for k in wk:
            if not k.multi:
                for sk, v in k.w.items():
                    if accum_pe and sk == myk:
                        continue
                    need((sk, v))
            for sk, v in k.r.items():
                if sk == myk and not dma:
                    continue
                need((sk, v))
        if dma:
            i = self.drr[eng]
            self.drr[eng] = (i + 1) % NDMA
            dk = ("d", eng, i)
            if self.dtot[dk] > 0:
                need((dk, self.dtot[dk]))
            self.dtot[dk] += 16
            ev = (dk, self.dtot[dk])
            inc = (self.sem[dk], 16)
        else:
            self.cnt[eng] += 1
            ev = (myk, self.cnt[eng])
            inc = (self.sem[myk], 1)
        waits = []
        seen = self.seen[eng]
        for k, v in deps.items():
            if seen.get(k, 0) >= v:
                continue
            seen[k] = v
            waits.append((self.sem[k], v))
        self.q[eng].append((waits, fn, inc))
        for k in wk:
            if k.multi:
                if k.w.get(ev[0], 0) < ev[1]:
                    k.w[ev[0]] = ev[1]
            else:
                k.w = {ev[0]: ev[1]}
                k.r = {}
        for k in rk:
            if k in wk:
                continue
            if k.r.get(ev[0], 0) < ev[1]:
                k.r[ev[0]] = ev[1]
        return ev

    def wait_all(self, eng="sp"):
        waits = []
        for e in ENGS:
            if self.cnt[e] > 0:
                waits.append((self.sem[("c", e)], self.cnt[e]))
        for k, v in self.dtot.items():
            if v > 0:
                waits.append((self.sem[k], v))
        self.q[eng].append((waits, None, None))

    def finalize(self):
        nc = self.nc
        q = self.q

        def run(engname):
            def body(eng):
                for waits, fn, inc in q[engname]:
                    for s, v in waits:
                        eng.wait_ge(s, v)
                    if fn is not None:
                        ins = fn(eng)
                        ins.then_inc(inc[0], inc[1])
            return body

        with nc.Block() as block:
            block.tensor(run("pe"))
            block.scalar(run("act"))
            block.vector(run("dve"))
            block.gpsimd(run("pool"))
            block.sync(run("sp"))
        self.es.close()

    def dma(self, out, in_, q="sp", **kw):
        return self.emit(q, lambda e: e.dma_start(out=_ap(out), in_=_ap(in_), **kw), [in_], [out], dma=True)

    def mm(self, out, lhsT, rhs, start=True, stop=True):
        return self.emit("pe", lambda e: e.matmul(_ap(out), _ap(lhsT), _ap(rhs), start=start, stop=stop),
                         [lhsT, rhs], [out], accum_pe=True)

    def tr(self, out, in_, ident):
        return self.emit("pe", lambda e: e.transpose(_ap(out), _ap(in_), _ap(ident)), [in_, ident], [out],
                         accum_pe=True)

    def act(self, out, in_, func, bias=None, scale=None, eng="act", accum_out=None):
        kw = {}
        rd = [in_]
        if bias is not None:
            kw["bias"] = _ap(bias)
            if isinstance(bias, (V, T)):
                rd.append(bias)
        if scale is not None:
            kw["scale"] = _ap(scale)
            if isinstance(scale, (V, T)):
                rd.append(scale)
        wr = [out]
        if accum_out is not None:
            kw["accum_out"] = _ap(accum_out)
            wr.append(accum_out)
        return self.emit("act", lambda e: e.activation(_ap(out), _ap(in_), func, **kw), rd, wr)

    def tt(self, out, a, b, op, eng="dve"):
        return self.emit(eng, lambda e: e.tensor_tensor(_ap(out), _ap(a), _ap(b), op), [a, b], [out])

    def ts(self, out, a, s1, s2=None, op0=ALU.mult, op1=None, eng="dve"):
        rd = [a] + [s for s in (s1, s2) if isinstance(s, (V, T))]
        if op1 is None:
            return self.emit(eng, lambda e: e.tensor_scalar(_ap(out), _ap(a), _ap(s1), None, op0), rd, [out])
        return self.emit(eng, lambda e: e.tensor_scalar(_ap(out), _ap(a), _ap(s1), _ap(s2), op0, op1), rd, [out])

    def stt(self, out, a, s, b, op0, op1):
        rd = [a, b] + ([s] if isinstance(s, (V, T)) else [])
        return self.emit("dve", lambda e: e.scalar_tensor_tensor(_ap(out), _ap(a), _ap(s), _ap(b), op0, op1),
                         rd, [out])

    def copy(self, out, in_, eng="dve"):
        if eng == "act":
            return self.act(out, in_, AF.Copy)
        return self.emit(eng, lambda e: e.tensor_copy(_ap(out), _ap(in_)), [in_], [out])

    def memset(self, out, val, eng="dve"):
        return self.emit(eng, lambda e: e.memset(_ap(out), val), [], [out])

    def recip(self, out, in_):
        return self.emit("dve", lambda e: e.reciprocal(_ap(out), _ap(in_)), [in_], [out])


import math
import numpy as np

D = 1024
KC_D = 8
IN_W = 6856
NH = 4
DFF = 2816
KC_F = 22
O_AV, O_AG, O_Q, O_Z, O_BETA, O_DEC, O_QL, O_KVL, O_KPE, O_GATE = 0, 512, 1024, 2560, 3072, 3076, 3080, 3464, 3720, 3784
ALPHA = 4 ** 0.25
LN_EPS = 1e-5
RMS_EPS = 1e-6
SCALE_Q = 192 ** -0.5


def build_program(cfg, debug=()):
    SEQ, NS, PAST, TB, L = cfg["SEQ"], cfg["NS"], cfg["PAST"], cfg["TB"], cfg["L"]
    TS = 16
    NB = SEQ // TB
    G = 1 + NS
    nc = bass.Bass("TRN2", target_bir_lowering=False)
    P = Prog(nc)
    dbg_out = {}

    def din(name, shape, dt=F32):
        return P.dram(name, shape, dt, kind="ExternalInput")

    def dout(name, shape, dt=F32):
        return P.dram(name, shape, dt, kind="ExternalOutput")

    xpT = din("xpT", [D, SEQ]); xsT = din("xsT", [D, NS * TS]); cT = din("cT", [D, G])
    ln0p = din("ln0p", [128, 2, 8])
    w_ada = din("w_ada", [L, D, 6 * D]); b_ada_p = din("b_ada_p", [L, 128, 48])
    w_in = din("w_in", [L, D, IN_W]); w_kpp = din("w_kpp", [L, D, 64])
    cva_w = din("cva_w", [L, 128, 4, 31]); cva_v = din("cva_v", [L, 128, 3, 4])
    gcv_w = din("gcv_w", [L, 128, 12, 4]); gdn_s = din("gdn_s", [L, 4, 2]); gdn_ng = din("gdn_ng", [L, 128, 1])
    qn_g = din("qn_g", [L, 128, 3]); kvn_g = din("kvn_g", [L, 128, 2])
    w_uqn = din("w_uqn", [L, 384, 512]); w_uqr = din("w_uqr", [L, 384, 256]); w_uqp = din("w_uqp", [L, 384, 256])
    w_uk = din("w_uk", [L, 256, 512]); w_uv = din("w_uv", [L, 256, 512])
    w_br = din("w_br", [L, 3, 512, D]); w_out = din("w_out", [L, D, D])
    lnv = din("lnv", [L, 128, 4, 8])
    w_up = din("w_up", [L, D, 2 * DFF]); ffn_w = din("ffn_w", [L, 128, KC_F, 3]); ffn_b = din("ffn_b", [L, 128, KC_F])
    w_dn = din("w_dn", [L, DFF, D])
    clatT = din("clatT", [L, NS, 256, PAST]); ckpeT = din("ckpeT", [L, NS, 64, PAST])
    hA = din("hA", [L, NS, 512, 30]); hB = din("hB", [L, NS, 1536, 3]); sG = din("sG", [L, NS, 4, 128, 128])
    hF = din("hF", [L, NS, DFF, 2])
    c_ident = din("c_ident", [128, 128]); c_triu = din("c_triu", [128, 128])
    c_mtri = din("c_mtri", [128, 128]); c_mtriT = din("c_mtriT", [128, 128]); c_mstr = din("c_mstr", [128, 128])
    ropeP = din("ropeP", [2, 64, SEQ]); ropeS = din("ropeS", [2, 64, TS])
    ypT = dout("ypT", [D, SEQ]); ysT = dout("ysT", [D, NS * TS])
    o_platT = dout("o_platT", [L, 256, SEQ]); o_pkpeT = dout("o_pkpeT", [L, 64, SEQ])
    o_pcaT = dout("o_pcaT", [L, 512, 30]); o_pgcT = dout("o_pgcT", [L, 1536, 3]); o_pg = dout("o_pg", [L, 4, 128, 128])
    o_pfT = dout("o_pfT", [L, DFF, 2])
    o_slatT = dout("o_slatT", [L, NS, 256, TS]); o_skpeT = dout("o_skpeT", [L, NS, 64, TS])
    o_scaT = dout("o_scaT", [L, NS, 512, 30]); o_sgcT = dout("o_sgcT", [L, NS, 1536, 3]); o_sg = dout("o_sg", [L, NS, 4, 128, 128])
    o_sfT = dout("o_sfT", [L, NS, DFF, 2])

    class StopBuild(Exception):
        pass

    def milestone(name):
        if cfg.get("stop") == name:
            raise StopBuild()

    def dbg(name, v, shape, dt=F32):
        if name in debug and name not in dbg_out:
            t = dout("dbg_" + name, list(shape), dt)
            dbg_out[name] = t
            P.dma(t[:], v)

    def main_body():
        ident = P.sb([128, 128]); triu = P.sb([128, 128]); mtri = P.sb([128, 128]); mtriT = P.sb([128, 128]); mstr = P.sb([128, 128])
        ones_f = P.sb([128, 128]); ones_b = P.sb([128, 128], BF16)
        for t, s in ((ident, c_ident), (triu, c_triu), (mtri, c_mtri), (mtriT, c_mtriT), (mstr, c_mstr)):
            P.dma(t[:], s[:])
        P.memset(ones_f[:], 1.0); P.memset(ones_b[:], 1.0)
        def b4(t, C):
            return t[0:C, 0:C].unsqueeze(1).to_broadcast([C, 4, C])
        ln0s = P.sb([128, 2, 8]); P.dma(ln0s[:], ln0p[:])

        PSN = 6
        pspool = [P.ps([128, 512], name="psg%d" % i) for i in range(PSN)]
        ps_o = P.ps([128, 512], name="ps_o"); ps_s = P.ps([128, 512], name="ps_s")
        psi = [0]

        def pst():
            t = pspool[psi[0] % PSN]
            psi[0] += 1
            return t

        W = {}
        stg_f = []
        stg_b = []
        prep_i = [0]

        def prep_w(name, src, K, N, Mc=128):
            KC = K // 128
            NM = (N + Mc - 1) // Mc
            scr = P.dram("wc_" + name, [NM, 128, KC * Mc], BF16)
            for mi in range(NM):
                m0 = mi * Mc
                mc = min(Mc, N - m0)
                i = prep_i[0]; prep_i[0] += 1
                sf = stg_f[i % 2]; sbf = stg_b[i % 2]
                q = "sp" if i % 2 == 0 else "pool"
                P.dma(sf[:, 0:KC * mc].rearrange("p (kc m) -> p kc m", kc=KC),
                      src[:, m0:m0 + mc].rearrange("(kc p) m -> p kc m", p=128), q=q)
                ce = ("dve", "act")[i % 2]
                P.copy(sbf[:, 0:KC * mc], sf[:, 0:KC * mc], eng=ce)
                P.dma(scr[mi, :, 0:KC * mc], sbf[:, 0:KC * mc], q=q)
            W[name] = (scr, KC, Mc, N)

        wbufs = [P.sb([128, 8 * 128], BF16, name="wbuf%d" % i) for i in range(4)]
        wbi = [0]

        def load_w(name, mi):
            scr, KC, Mc, N = W[name]
            mc = min(Mc, N - mi * Mc)
            i = wbi[0]; wbi[0] += 1
            wb = wbufs[i % 4]
            P.dma(wb[:, 0:KC * mc], scr[mi, :, 0:KC * mc], q=("sp", "pool")[i % 2])
            return wb[:, 0:KC * mc].rearrange("p (kc m) -> p kc m", kc=KC), KC, mc

        def lin(name, mi, rhs, nt):
            parts = name if isinstance(name, list) else [(name, 0)]
            ps = pst()
            np_ = len(parts)
            for pi, (nm, ko) in enumerate(parts):
                wv, KC, mc = load_w(nm, mi)
                for kc in range(KC):
                    P.mm(ps[0:mc, 0:nt], wv[:, kc, :], rhs[:, ko + kc, 0:nt], start=(pi == 0 and kc == 0),
                         stop=(pi == np_ - 1 and kc == KC - 1))
            return ps[0:mc, 0:nt]

        X = P.sb([128, 8, TB]); X1 = P.sb([128, 8, TB]); Hb = P.sb([128, 8, TB], BF16)
        SQ = P.sb([128, 8, TB])
        U = P.sb([128, 4, 30 + TB]); QKVP = P.sb([128, 12, 3 + TB]); QKV = P.sb([128, 12, TB])
        Zs = P.sb([128, 4, TB]); G3 = P.sb([128, 24, TB], BF16)
        BE = P.sb([4, TB]); GG = P.sb([4, TB])
        YA = P.sb([128, 4, TB], BF16); YB = P.sb([128, 4, TB], BF16); YC = P.sb([128, 4, TB], BF16)
        QLn = P.sb([128, 3, TB], BF16); QLf = P.sb([128, 3, TB])
        LAT = P.sb([128, 2, TB]); LATb = P.sb([128, 2, TB], BF16)
        KPE = P.sb([64, TB]); KPEb = P.sb([65, TB], BF16)
        Qn = P.sb([128, 4, TB], BF16); QR = P.sb([65, 4, TB], BF16); QNf = P.sb([128, 4, TB]); QRf = P.sb([64, 4, TB])
        KNf = P.sb([128, TB]); KNb = P.sb([128, 4, TB], BF16); VVb = P.sb([128, 4 * max(1, TB // 128), 128], BF16)
        Mg = P.sb([128, 8, TB], BF16); Gf = G3
        Aj = [P.sb([128, 2 + TB]) for _ in range(2)]
        tA = [P.sb([128, TB]) for _ in range(4)]
        tAi = [0]

        def tmp():
            t = tA[tAi[0] % 4]
            tAi[0] += 1
            return t
        rowk = P.sb([1, 4])
        ropeT = P.sb([64, 2, TB])
        knt = [P.sb([128, 512], BF16) for _ in range(2)]; krt = [P.sb([65, 512], BF16) for _ in range(2)]
        vts = [P.sb([128, 4, 128], BF16) for _ in range(2)]; pts = [P.sb([128, 512], BF16) for _ in range(3)]
        avi = [0]

        stg_f.extend([QKVP[:, :, :].rearrange("p a b -> p (a b)"), QKV[:, :, :].rearrange("p a b -> p (a b)")])
        stg_b.extend([Mg[:, :, :].rearrange("p a b -> p (a b)"), G3[:, :, :].rearrange("p a b -> p (a b)")])
        for l in range(L):
            prep_w("ada%d" % l, w_ada[l], D, 6 * D)
            prep_w("inA%d" % l, w_in[l][:, 0:3072], D, 3072)
            prep_w("inbeta%d" % l, w_in[l][:, O_BETA:O_BETA + 4], D, 4, Mc=4)
            prep_w("indec%d" % l, w_in[l][:, O_DEC:O_DEC + 4], D, 4, Mc=4)
            prep_w("inql%d" % l, w_in[l][:, O_QL:O_QL + 384], D, 384)
            prep_w("inkvl%d" % l, w_in[l][:, O_KVL:O_KVL + 256], D, 256)
            prep_w("inkpe%d" % l, w_in[l][:, O_KPE:O_KPE + 64], D, 64, Mc=64)
            prep_w("inkpp%d" % l, w_kpp[l], D, 64, Mc=64)
            prep_w("ingate%d" % l, w_in[l][:, O_GATE:O_GATE + 3072], D, 3072)
            prep_w("uqn%d" % l, w_uqn[l], 384, 512)
            prep_w("uqr%d" % l, w_uqr[l], 384, 256, Mc=64)
            prep_w("uqp%d" % l, w_uqp[l], 384, 256, Mc=64)
            prep_w("uk%d" % l, w_uk[l], 256, 512)
            for n in range(3):
                prep_w("br%d_%d" % (l, n), w_br[l, n], 512, D)
            prep_w("out%d" % l, w_out[l], D, D)
            prep_w("up%d" % l, w_up[l], D, 2 * DFF)
            prep_w("dna%d" % l, w_dn[l][0:1024, :], 1024, D)
            prep_w("dnb%d" % l, w_dn[l][1024:2048, :], 1024, D)
            prep_w("dnc%d" % l, w_dn[l][2048:2816, :], 768, D)

        milestone("prep")
        cvaw = [P.sb([128, 4, 31]) for _ in range(L)]; cvav = [P.sb([128, 3, 4]) for _ in range(L)]
        gcvw = [P.sb([128, 12, 4]) for _ in range(L)]; gdns = [P.sb([4, 2]) for _ in range(L)]
        nexpA = [P.sb([4, 1]) for _ in range(L)]; gdnng = [P.sb([128, 1]) for _ in range(L)]
        qng = [P.sb([128, 3]) for _ in range(L)]; kvng = [P.sb([128, 2]) for _ in range(L)]
        lnvs = [P.sb([128, 4, 8]) for _ in range(L)]; ffnw = [P.sb([128, KC_F, 3]) for _ in range(L)]
        ffnb = [P.sb([128, KC_F]) for _ in range(L)]
        wuv = [P.sb([128, 2, 512], BF16) for _ in range(L)]
        modT = [P.sb([128, 48, G]) for _ in range(L)]
        scp1 = [P.sb([128, 8, G]) for _ in range(L)]; scp2 = [P.sb([128, 8, G]) for _ in range(L)]
        g1p = [P.sb([128, 8, G]) for _ in range(L)]; g2p = [P.sb([128, 8, G]) for _ in range(L)]
        badas = P.sb([128, 48])
        csT = P.sb([128, 8, G]); csb = P.sb([128, 8, G], BF16)
        P.dma(csT[:], cT[:, :].rearrange("(kc p) g -> p kc g", p=128))
        P.act(csb[:], csT[:], AF.Silu)
        for l in range(L):
            P.dma(cvaw[l][:], cva_w[l]); P.dma(cvav[l][:], cva_v[l]); P.dma(gcvw[l][:], gcv_w[l])
            P.dma(gdns[l][:], gdn_s[l]); P.dma(gdnng[l][:], gdn_ng[l]); P.dma(qng[l][:], qn_g[l]); P.dma(kvng[l][:], kvn_g[l])
            P.dma(lnvs[l][:], lnv[l]); P.dma(ffnw[l][:], ffn_w[l]); P.dma(ffnb[l][:], ffn_b[l])
            P.act(nexpA[l][:], gdns[l][:, 0:1], AF.Exp)
            P.ts(nexpA[l][:], nexpA[l][:], -1.0, None, ALU.mult)
            sf = stg_f[0]
            P.dma(sf[:, 0:1024].rearrange("p (kc m) -> p kc m", kc=2), w_uv[l].rearrange("(kc p) m -> p kc m", p=128))
            P.copy(wuv[l][:], sf[:, 0:1024].rearrange("p (kc m) -> p kc m", kc=2))
            P.dma(badas[:], b_ada_p[l])
            for j in range(48):
                ps = lin("ada%d" % l, j, csb, G)
                P.ts(modT[l][:, j, :], ps, badas[:, j:j + 1], None, ALU.add)
            P.ts(scp1[l][:], modT[l][:, 8:16, :], 1.0, None, ALU.add)
            P.ts(g1p[l][:], modT[l][:, 16:24, :], 1.0, None, ALU.add)
            P.ts(scp2[l][:], modT[l][:, 32:40, :], 1.0, None, ALU.add)
            P.ts(g2p[l][:], modT[l][:, 40:48, :], 1.0, None, ALU.add)

        class St:
            pass

        def mkstate(tk, name):
            s = St()
            s.Uh = P.sb([128, 4, 30]); s.Qh = P.sb([128, 12, 3]); s.S = P.sb([128, 4, 128]); s.Ah = P.sb([128, KC_F, 2])
            s.Kb2 = P.sb([1, 4])
            s.KN = P.dram("KN_" + name, [4, 128, tk], BF16); s.KR = P.dram("KR_" + name, [65, tk], BF16)
            s.VV = P.dram("VV_" + name, [4, tk, 128], BF16)
            return s

        stP = [mkstate(SEQ, "p%d" % l) for l in range(L)]
        stS = mkstate(PAST + TS, "s")

        def colstats(Xt, nch, nt, need_mean):
            npart = Xt.shape[0]
            P.act(SQ[0:npart, 0:nch, 0:nt], Xt[:, 0:nch, 0:nt], AF.Square)
            p2 = pst()
            for ch in range(nch):
                P.mm(p2[:, 0:nt], ones_f[0:npart, :], SQ[0:npart, ch, 0:nt], start=(ch == 0), stop=(ch == nch - 1))
            F = float(nch * npart)
            mean = None
            var = tmp()
            if need_mean:
                p1 = pst()
                for ch in range(nch):
                    P.mm(p1[:, 0:nt], ones_f[0:npart, :], Xt[:, ch, 0:nt], start=(ch == 0), stop=(ch == nch - 1))
                mean = tmp()
                P.act(mean[:, 0:nt], p1[:, 0:nt], AF.Copy, scale=1.0 / F)
                msq = tmp()
                P.tt(msq[:, 0:nt], mean[:, 0:nt], mean[:, 0:nt], ALU.mult)
                P.stt(var[:, 0:nt], p2[:, 0:nt], 1.0 / F, msq[:, 0:nt], ALU.mult, ALU.subtract)
                eps = LN_EPS
            else:
                P.act(var[:, 0:nt], p2[:, 0:nt], AF.Copy, scale=1.0 / F)
                eps = RMS_EPS
            P.ts(var[:, 0:nt], var[:, 0:nt], eps, None, ALU.add)
            P.act(var[:, 0:nt], var[:, 0:nt], AF.Sqrt)
            rstd = tmp()
            P.recip(rstd[:, 0:nt], var[:, 0:nt])
            return mean, rstd

        def ln_fm(dst, Xt, nch, nt, g_v, b_v):
            mean, rstd = colstats(Xt, nch, nt, True)
            mb = mean[:, 0:nt].unsqueeze(1).to_broadcast([128, nch, nt])
            rb = rstd[:, 0:nt].unsqueeze(1).to_broadcast([128, nch, nt])
            P.tt(SQ[:, 0:nch, 0:nt], Xt[:, 0:nch, 0:nt], mb, ALU.subtract)
            P.tt(SQ[:, 0:nch, 0:nt], SQ[:, 0:nch, 0:nt], rb, ALU.mult)
            P.tt(SQ[:, 0:nch, 0:nt], SQ[:, 0:nch, 0:nt], g_v.bc(nt), ALU.mult)
            P.tt(dst[:, 0:nch, 0:nt], SQ[:, 0:nch, 0:nt], b_v.bc(nt), ALU.add)

        milestone("params")
        C4 = lambda: P.sb([128, 4, 128])
        g_ktm = C4(); g_vtm = C4(); g_e1 = C4(); g_Dm = C4(); g_DmT = C4(); g_N = C4(); g_At = C4()
        g_A2 = C4(); g_At2 = C4(); g_Rt = C4(); g_qkT = C4(); g_expGb = C4(); g_gB = C4()
        g_bv = g_e1; g_bk = g_Dm; g_kd = g_DmT; g_u = g_gB; g_wT = g_N; g_vnew = g_At; g_o = g_A2; g_t = g_At2
        g_btm = P.sb([128, 4]); g_gtm = P.sb([128, 4]); g_nb = P.sb([128, 4]); g_Gcol = P.sb([128, 4]); g_eG = P.sb([128, 4])
        g_beG = P.sb([128, 4]); g_kdc = P.sb([128, 4]); g_gam = P.sb([128, 4])

        def gdn_chunk(l, st, c0, C):
            pk = pst(); pv = pst()
            for h in range(4):
                P.tr(pk[0:C, h * 128:(h + 1) * 128], QKV[:, 4 + h, c0:c0 + C], ident[:])
                P.tr(pv[0:C, h * 128:(h + 1) * 128], QKV[:, 8 + h, c0:c0 + C], ident[:])
            P.copy(g_ktm[0:C].rearrange("p h d -> p (h d)"), pk[0:C, :], eng="act")
            P.copy(g_vtm[0:C].rearrange("p h d -> p (h d)"), pv[0:C, :])
            pb = pst()
            P.tr(pb[0:C, 0:4], BE[0:4, c0:c0 + C], ident[0:4, 0:4])
            P.tr(pb[0:C, 4:8], GG[0:4, c0:c0 + C], ident[0:4, 0:4])
            P.copy(g_btm[0:C, :], pb[0:C, 0:4]); P.copy(g_gtm[0:C, :], pb[0:C, 4:8], eng="act")
            P.ts(g_nb[0:C, :], g_btm[0:C, :], -1.0, None, ALU.mult)
            milestone("g1")
            P.tt(g_gB[0:C], ones_f[0:C, :].unsqueeze(1).to_broadcast([C, 4, 128]), g_gtm[0:C, :].bc(128), ALU.mult)
            pG = pst(); pc = pst()
            for h in range(4):
                P.mm(pG[:, h * 128:h * 128 + C], g_gB[0:C, h, :], triu[0:C, 0:C], True, True)
            P.mm(pc[0:C, 0:4], triu[0:C, 0:C], g_gtm[0:C, 0:4], True, True)
            milestone("g2")
            P.copy(g_Gcol[0:C, :], pc[0:C, 0:4])
            pGv = pG[:, :].rearrange("p (h j) -> p h j", h=4)
            P.act(g_expGb[:, :, 0:C], pGv[:, :, 0:C], AF.Exp)
            P.act(g_eG[0:C, :], g_Gcol[0:C, :], AF.Exp)
            P.tt(g_beG[0:C, :], g_btm[0:C, :], g_eG[0:C, :], ALU.mult)
            P.copy(g_gam[:, :], g_expGb[:, :, C - 1])
            P.copy(g_kdc[0:C, :], pGv[0:C, :, C - 1], eng="act")
            P.tt(g_kdc[0:C, :], g_kdc[0:C, :], g_Gcol[0:C, :], ALU.subtract)
            P.act(g_kdc[0:C, :], g_kdc[0:C, :], AF.Exp)
            P.tt(g_e1[0:C, :, 0:C], pGv[0:C, :, 0:C], g_Gcol[0:C, :].bc(C), ALU.subtract)
            P.ts(g_Dm[0:C, :, 0:C], g_e1[0:C, :, 0:C], 0.0, None, ALU.max)
            P.act(g_Dm[0:C, :, 0:C], g_Dm[0:C, :, 0:C], AF.Exp, scale=-1.0)
            P.ts(g_DmT[0:C, :, 0:C], g_e1[0:C, :, 0:C], 0.0, None, ALU.min)
            P.act(g_DmT[0:C, :, 0:C], g_DmT[0:C, :, 0:C], AF.Exp)
            P.tt(g_DmT[0:C, :, 0:C], g_DmT[0:C, :, 0:C], b4(mtriT, C), ALU.mult)
            P.tt(g_Dm[0:C, :, 0:C], g_Dm[0:C, :, 0:C], b4(mstr, C), ALU.mult)
            milestone("g3")
            pkk = pst(); pqk = pst()
            for h in range(4):
                P.mm(pkk[0:C, h * 128:h * 128 + C], QKV[:, 4 + h, c0:c0 + C], QKV[:, 4 + h, c0:c0 + C], True, True)
                P.mm(pqk[0:C, h * 128:h * 128 + C], QKV[:, 4 + h, c0:c0 + C], QKV[:, h, c0:c0 + C], True, True)
            pkkv = pkk[:, :].rearrange("p (h j) -> p h j", h=4); pqkv = pqk[:, :].rearrange("p (h j) -> p h j", h=4)
            P.tt(g_N[0:C, :, 0:C], pkkv[0:C, :, 0:C], g_Dm[0:C, :, 0:C], ALU.mult)
            P.tt(g_N[0:C, :, 0:C], g_N[0:C, :, 0:C], g_nb[0:C, :].bc(C), ALU.mult)
            P.tt(g_qkT[0:C, :, 0:C], pqkv[0:C, :, 0:C], g_DmT[0:C, :, 0:C], ALU.mult)
            milestone("g4")
            pt_ = pst()
            for h in range(4):
                P.tr(pt_[0:C, h * 128:h * 128 + C], g_N[0:C, h, 0:C], ident[0:C, 0:C])
            ptv = pt_[:, :].rearrange("p (h j) -> p h j", h=4)
            milestone("g4t")
            P.copy(g_At[0:C, :, 0:C], ptv[0:C, :, 0:C], eng="act")
            milestone("g4c")
            P.tt(g_Rt[0:C, :, 0:C], g_At[0:C, :, 0:C], b4(ident, C), ALU.add)
            milestone("g4a")
            A, At, A2, At2 = g_N, g_At, g_A2, g_At2
            nsq = int(round(math.log2(C))) - 1
            for m in range(nsq):
                pa = pst(); pat = pst()
                for h in range(4):
                    P.mm(pa[0:C, h * 128:h * 128 + C], At[0:C, h, 0:C], A[0:C, h, 0:C], True, True)
                    P.mm(pat[0:C, h * 128:h * 128 + C], A[0:C, h, 0:C], At[0:C, h, 0:C], True, True)
                P.copy(A2[0:C, :, 0:C], pa[:, :].rearrange("p (h j) -> p h j", h=4)[0:C, :, 0:C], eng="act")
                P.copy(At2[0:C, :, 0:C], pat[:, :].rearrange("p (h j) -> p h j", h=4)[0:C, :, 0:C])
                A, At, A2, At2 = A2, At2, A, At
                pr = pst()
                for h in range(4):
                    P.mm(pr[0:C, h * 128:h * 128 + C], A[0:C, h, 0:C], g_Rt[0:C, h, 0:C], True, True)
                P.tt(g_Rt[0:C, :, 0:C], g_Rt[0:C, :, 0:C], pr[:, :].rearrange("p (h j) -> p h j", h=4)[0:C, :, 0:C], ALU.add)
                milestone("g4b")
            milestone("g5")
            P.tt(g_bv[0:C], g_vtm[0:C], g_btm[0:C, :].bc(128), ALU.mult)
            P.tt(g_bk[0:C], g_ktm[0:C], g_beG[0:C, :].bc(128), ALU.mult)
            P.tt(g_kd[0:C], g_ktm[0:C], g_kdc[0:C, :].bc(128), ALU.mult)
            pu = pst(); pw = pst()
            for h in range(4):
                P.mm(pu[0:C, h * 128:(h + 1) * 128], g_Rt[0:C, h, 0:C], g_bv[0:C, h, :], True, True)
                P.mm(pw[:, h * 128:h * 128 + C], g_bk[0:C, h, :], g_Rt[0:C, h, 0:C], True, True)
            P.copy(g_u[0:C].rearrange("p h d -> p (h d)"), pu[0:C, :], eng="act")
            P.copy(g_wT[:, :, 0:C], pw[:, :].rearrange("p (h j) -> p h j", h=4)[:, :, 0:C])
            milestone("g6")
            pws = pst()
            for h in range(4):
                P.mm(pws[0:C, h * 128:(h + 1) * 128], g_wT[:, h, 0:C], st.S[:, h, :], True, True)
            P.tt(g_vnew[0:C].rearrange("p h d -> p (h d)"), g_u[0:C].rearrange("p h d -> p (h d)"), pws[0:C, :], ALU.subtract)
            po1 = pst(); po2 = pst(); psn = pst()
            for h in range(4):
                P.mm(po1[:, h * 128:h * 128 + C], st.S[:, h, :], QKV[:, h, c0:c0 + C], True, True)
                P.mm(po2[:, h * 128:h * 128 + C], g_vnew[0:C, h, :], g_qkT[0:C, h, 0:C], True, True)
                P.mm(psn[:, h * 128:(h + 1) * 128], g_kd[0:C, h, :], g_vnew[0:C, h, :], True, True)
            P.tt(g_t[:, :, 0:C], po1[:, :].rearrange("p (h j) -> p h j", h=4)[:, :, 0:C], g_expGb[:, :, 0:C], ALU.mult)
            P.tt(g_o[:, :, 0:C], g_t[:, :, 0:C], po2[:, :].rearrange("p (h j) -> p h j", h=4)[:, :, 0:C], ALU.add)
            for h in range(4):
                P.stt(st.S[:, h, :], st.S[:, h, :], g_gam[:, h:h + 1], psn[:, h * 128:(h + 1) * 128], ALU.mult, ALU.add)
            milestone("g7")
            P.act(g_t[:, :, 0:C], g_o[:, :, 0:C], AF.Square)
            pss = pst()
            for h in range(4):
                P.mm(pss[:, h * 128:h * 128 + C], ones_f[:, :], g_t[:, h, 0:C], True, True)
            pssv = pss[:, :].rearrange("p (h j) -> p h j", h=4)
            P.act(g_t[:, :, 0:C], pssv[:, :, 0:C], AF.Copy, scale=1.0 / 128.0)
            P.ts(g_t[:, :, 0:C], g_t[:, :, 0:C], RMS_EPS, None, ALU.add)
            P.act(g_t[:, :, 0:C], g_t[:, :, 0:C], AF.Sqrt)
            P.recip(g_t[:, :, 0:C], g_t[:, :, 0:C])
            P.tt(g_o[:, :, 0:C], g_o[:, :, 0:C], g_t[:, :, 0:C], ALU.mult)
            P.ts(g_o[:, :, 0:C], g_o[:, :, 0:C], gdnng[l][:, 0:1], None, ALU.mult)
            P.tt(YB[:, :, c0:c0 + C], g_o[:, :, 0:C], Zs[:, :, c0:c0 + C], ALU.mult)

        def attention(st, nt, groups):
            for h in range(4):
                first = True
                nvis = sum((nk + 127) // 128 for (_, nk, _) in groups)
                vi = 0
                for (k0, nk, diag) in groups:
                    i = avi[0]; avi[0] += 1
                    kn = knt[i % 2]; kr = krt[i % 2]; vt = vts[i % 2]
                    P.dma(kn[:, 0:nk], st.KN[h, :, k0:k0 + nk], q="sp")
                    P.dma(kr[:, 0:nk], st.KR[:, k0:k0 + nk], q="pool")
                    nb_ = (nk + 127) // 128
                    if nk >= 128:
                        P.dma(vt[:, 0:nb_, :], st.VV[h, k0:k0 + nk, :].rearrange("(b p) d -> p b d", p=128), q="sp")
                    else:
                        P.dma(vt[0:nk, 0, :], st.VV[h, k0:k0 + nk, :], q="sp")
                    for r in range(nb_):
                        kk = min(128, nk - 128 * r)
                        c0 = 128 * r if diag else 0
                        if c0 >= nt:
                            vi += 1
                            continue
                        sc = pst()
                        P.mm(sc[0:kk, c0:nt], kn[:, 128 * r:128 * r + kk], Qn[:, h, c0:nt], True, False)
                        P.mm(sc[0:kk, c0:nt], kr[0:65, 128 * r:128 * r + kk], QR[0:65, h, c0:nt], False, True)
                        pt = pts[vi % 3]
                        P.act(pt[0:kk, c0:nt], sc[0:kk, c0:nt], AF.Exp)
                        if diag and kk > 64:
                            P.memset(pt[64:kk, c0:min(nt, c0 + 64)], 0.0, eng="pool")
                        last = (vi == nvis - 1)
                        P.mm(ps_o[:, c0:nt], vt[0:kk, r, :], pt[0:kk, c0:nt], first, last)
                        P.mm(ps_s[:, c0:nt], ones_b[0:kk, :], pt[0:kk, c0:nt], first, last)
                        first = False
                        vi += 1
                rs = tmp()
                P.recip(rs[:, 0:nt], ps_s[:, 0:nt])
                P.tt(YC[:, h, 0:nt], ps_o[:, 0:nt], rs[:, 0:nt], ALU.mult)

        def knorm(l, st, nt, k0):
            for h in range(4):
                pkh = lin("uk%d" % l, h, LATb, nt)
                P.copy(KNb[:, h, 0:nt], pkh, eng="act")
                P.act(KNf[:, 0:nt], pkh, AF.Square)
                pk2 = pst()
                P.mm(pk2[0:1, 0:nt], ones_f[:, 0:1], KNf[:, 0:nt], True, False)
                P.mm(pk2[0:1, 0:nt], ones_f[0:64, 0:1], SQ[0:64, 0, 0:nt], False, True)
                P.dma(st.KN[h, :, k0:k0 + nt], KNb[:, h, 0:nt], q="pool")
                P.emit("dve", lambda e, pk2=pk2, h=h, nt=nt: e.tensor_reduce(
                    out=_ap(rowk[0:1, h:h + 1]), in_=_ap(pk2[0:1, 0:nt]), op=ALU.max, axis=mybir.AxisListType.X),
                    [pk2], [rowk])
            P.tt(st.Kb2[0:1, :], st.Kb2[0:1, :], rowk[0:1, :], ALU.max)

        def block_layer(l, st, nt, g, pos_rope, t0, cachek0, groups, outs, Cg):
            for kc in range(8):
                P.act(Hb[:, kc, 0:nt], X[:, kc, 0:nt], AF.Identity, bias=modT[l][:, kc, g:g + 1], scale=scp1[l][:, kc, g:g + 1])
            P.copy(U[:, :, 0:30], st.Uh[:, :, :])
            for j in range(4):
                pv = lin("inA%d" % l, j, Hb, nt)
                pg = lin("inA%d" % l, 4 + j, Hb, nt)
                sg = tmp()
                P.act(sg[:, 0:nt], pg, AF.Sigmoid)
                P.tt(U[:, j, 30:30 + nt], pv, sg[:, 0:nt], ALU.mult)
            P.copy(st.Uh[:, :, :], U[:, :, nt:nt + 30])
            milestone("A")
            P.copy(QKVP[:, :, 0:3], st.Qh[:, :, :], eng="pool")
            for j in range(12):
                pq = lin("inA%d" % l, 8 + j, Hb, nt)
                P.copy(QKVP[:, j, 3:3 + nt], pq, eng=("act" if j % 2 else "dve"))
            P.copy(st.Qh[:, :, :], QKVP[:, :, nt:nt + 3], eng="pool")
            for j in range(4):
                pz = lin("inA%d" % l, 20 + j, Hb, nt)
                P.act(Zs[:, j, 0:nt], pz, AF.Silu)
            pb = lin("inbeta%d" % l, 0, Hb, nt)
            P.act(BE[0:4, 0:nt], pb, AF.Sigmoid)
            pd = lin("indec%d" % l, 0, Hb, nt)
            P.act(GG[0:4, 0:nt], pd, AF.Exp, bias=gdns[l][:, 1:2])
            P.act(GG[0:4, 0:nt], GG[0:4, 0:nt], AF.Ln, bias=1.0)
            P.ts(GG[0:4, 0:nt], GG[0:4, 0:nt], nexpA[l][:, 0:1], None, ALU.mult)
            milestone("B")
            for j in range(3):
                pq = lin("inql%d" % l, j, Hb, nt)
                P.copy(QLf[:, j, 0:nt], pq, eng=("act" if j % 2 else "dve"))
            _, rstd = colstats(QLf, 3, nt, False)
            P.tt(QLf[:, :, 0:nt], QLf[:, :, 0:nt], rstd[:, 0:nt].unsqueeze(1).to_broadcast([128, 3, nt]), ALU.mult)
            P.tt(QLn[:, :, 0:nt], QLf[:, :, 0:nt], qng[l][:, :].bc(nt), ALU.mult)
            for j in range(2):
                pq = lin("inkvl%d" % l, j, Hb, nt)
                P.copy(LAT[:, j, 0:nt], pq, eng=("act" if j % 2 else "dve"))
            _, rstd = colstats(LAT, 2, nt, False)
            P.tt(LAT[:, :, 0:nt], LAT[:, :, 0:nt], rstd[:, 0:nt].unsqueeze(1).to_broadcast([128, 2, nt]), ALU.mult)
            P.tt(LAT[:, :, 0:nt], LAT[:, :, 0:nt], kvng[l][:, :].bc(nt), ALU.mult)
            P.copy(LATb[:, :, 0:nt], LAT[:, :, 0:nt], eng="pool")
            P.dma(outs["lat"][:, t0:t0 + nt].rearrange("(kc p) t -> p kc t", p=128), LAT[:, :, 0:nt])
            P.dma(ropeT[:, :, 0:nt], pos_rope.rearrange("a d t -> d a t"))
            pk1 = lin("inkpe%d" % l, 0, Hb, nt)
            pk2 = lin("inkpp%d" % l, 0, Hb, nt)
            t1 = tmp(); t2 = tmp()
            P.tt(t1[0:64, 0:nt], pk1, ropeT[:, 0, 0:nt], ALU.mult)
            P.tt(t2[0:64, 0:nt], pk2, ropeT[:, 1, 0:nt], ALU.mult)
            P.tt(KPE[:, 0:nt], t1[0:64, 0:nt], t2[0:64, 0:nt], ALU.add)
            P.dma(outs["kpe"][:, t0:t0 + nt], KPE[:, 0:nt])
            P.copy(KPEb[0:64, 0:nt], KPE[:, 0:nt], eng="pool")
            P.memset(KPEb[64:65, 0:nt], 1.0, eng="pool")
            P.dma(st.KR[:, cachek0:cachek0 + nt], KPEb[:, 0:nt])
            milestone("C")
            for j in range(24):
                pgt = lin("ingate%d" % l, j, Hb, nt)
                P.act(G3[:, j, 0:nt], pgt, AF.Sigmoid)
            milestone("gates")
            acc = SQ
            for ch in range(4):
                P.ts(acc[:, 4 + ch, 0:nt], U[:, ch, 0:nt], cvaw[l][:, ch, 0:1], None, ALU.mult)
                for k in range(1, 31):
                    P.stt(acc[:, 4 + ch, 0:nt], U[:, ch, k:k + nt], cvaw[l][:, ch, k:k + 1], acc[:, 4 + ch, 0:nt], ALU.mult, ALU.add)
                P.ts(acc[:, 4 + ch, 0:nt], acc[:, 4 + ch, 0:nt], cvav[l][:, 0, ch:ch + 1], None, ALU.add)
            P.copy(X1[:, 0:4, 0:nt], acc[:, 4:8, 0:nt], eng="pool")
            ln_fm(X1[:, 4:8, :], X1[:, 0:4, :], 4, nt, cvav[l][:, 1, :], cvav[l][:, 2, :])
            P.act(YA[:, :, 0:nt], X1[:, 4:8, 0:nt], AF.Silu)
            milestone("convA")
            for j in range(12):
                P.ts(QKV[:, j, 0:nt], QKVP[:, j, 0:nt], gcvw[l][:, j, 0:1], None, ALU.mult)
                for k in range(1, 4):
                    P.stt(QKV[:, j, 0:nt], QKVP[:, j, k:k + nt], gcvw[l][:, j, k:k + 1], QKV[:, j, 0:nt], ALU.mult, ALU.add)
            P.act(QKV[:, :, 0:nt], QKV[:, :, 0:nt], AF.Silu)
            P.act(SQ[:, 0:8, 0:nt], QKV[:, 0:8, 0:nt], AF.Square)
            for j in range(8):
                pn = pst()
                P.mm(pn[:, 0:nt], ones_f[:, :], SQ[:, j, 0:nt], True, True)
                rn = tmp()
                P.ts(rn[:, 0:nt], pn[:, 0:nt], RMS_EPS, None, ALU.add)
                P.act(rn[:, 0:nt], rn[:, 0:nt], AF.Sqrt)
                P.recip(rn[:, 0:nt], rn[:, 0:nt])
                if j < 4:
                    P.stt(QKV[:, j, 0:nt], QKV[:, j, 0:nt], 128 ** -0.5, rn[:, 0:nt], ALU.mult, ALU.mult)
                else:
                    P.tt(QKV[:, j, 0:nt], QKV[:, j, 0:nt], rn[:, 0:nt], ALU.mult)
            milestone("gdnprep")
            for c0 in range(0, nt, Cg):
                gdn_chunk(l, st, c0, min(Cg, nt - c0))
            milestone("gdn")
            for h in range(4):
                pqn = lin("uqn%d" % l, h, QLn, nt)
                P.act(QNf[:, h, 0:nt], pqn, AF.Copy, scale=SCALE_Q)
                pqr = lin("uqr%d" % l, h, QLn, nt)
                pqp = lin("uqp%d" % l, h, QLn, nt)
                t1 = tmp(); t2 = tmp()
                P.tt(t1[0:64, 0:nt], pqr, ropeT[:, 0, 0:nt], ALU.mult)
                P.tt(t2[0:64, 0:nt], pqp, ropeT[:, 1, 0:nt], ALU.mult)
                P.stt(QRf[:, h, 0:nt], t1[0:64, 0:nt], 1.0, t2[0:64, 0:nt], ALU.mult, ALU.add)
            P.ts(QRf[:, :, 0:nt], QRf[:, :, 0:nt], SCALE_Q, None, ALU.mult)
            P.copy(Qn[:, :, 0:nt], QNf[:, :, 0:nt], eng="pool")
            P.copy(QR[0:64, :, 0:nt], QRf[:, :, 0:nt], eng="pool")
            P.act(SQ[0:64, 0, 0:nt], KPE[:, 0:nt], AF.Square)
            knorm(l, st, nt, cachek0)
            nsb = (nt + 127) // 128
            for sbk in range(nsb):
                kk = min(128, nt - 128 * sbk)
                pvv = pst()
                for kc in range(2):
                    P.mm(pvv[0:kk, :], LATb[:, kc, 128 * sbk:128 * sbk + kk], wuv[l][:, kc, :], kc == 0, kc == 1)
                P.copy(VVb[0:kk, 4 * sbk:4 * sbk + 4, :].rearrange("p h d -> p (h d)"), pvv[0:kk, :], eng="act")
                for h in range(4):
                    P.dma(st.VV[h, cachek0 + 128 * sbk:cachek0 + 128 * sbk + kk, :], VVb[0:kk, 4 * sbk + h, :],
                          q=("sp", "pool")[h % 2])
            P.act(SQ[:, 0:4, 0:nt], QNf[:, :, 0:nt], AF.Square)
            P.act(SQ[0:64, 4:8, 0:nt], QRf[:, :, 0:nt], AF.Square)
            for h in range(4):
                pq2 = pst()
                P.mm(pq2[0:1, 0:nt], ones_f[:, 0:1], SQ[:, h, 0:nt], True, False)
                P.mm(pq2[0:1, 0:nt], ones_f[0:64, 0:1], SQ[0:64, 4 + h, 0:nt], False, True)
                tq = tmp()
                P.ts(tq[0:1, 0:nt], pq2[0:1, 0:nt], st.Kb2[0:1, h:h + 1], None, ALU.mult)
                P.act(tq[0:1, 0:nt], tq[0:1, 0:nt], AF.Sqrt)
                rb_ = rowbf[h % 2]
                P.ts(rb_[0:1, 0:nt], tq[0:1, 0:nt], -1.0, None, ALU.mult)
                P.dma(QR[64:65, h, 0:nt], rb_[0:1, 0:nt])
            milestone("mla")
            attention(st, nt, groups)
            milestone("attn")
            for m in range(8):
                ta = tmp(); tb_ = tmp()
                for n, Y in enumerate((YA, YB, YC)):
                    pm = lin("br%d_%d" % (l, n), m, Y, nt)
                    if n == 0:
                        P.tt(ta[:, 0:nt], pm, G3[:, 0 * 8 + m, 0:nt], ALU.mult)
                    elif n == 1:
                        P.tt(tb_[:, 0:nt], pm, G3[:, 1 * 8 + m, 0:nt], ALU.mult)
                        P.tt(ta[:, 0:nt], ta[:, 0:nt], tb_[:, 0:nt], ALU.add, eng="pool")
                    else:
                        P.tt(tb_[:, 0:nt], pm, G3[:, 2 * 8 + m, 0:nt], ALU.mult)
                        P.tt(Mg[:, m, 0:nt], ta[:, 0:nt], tb_[:, 0:nt], ALU.add, eng="pool")
            milestone("merge")
            for m in range(8):
                po = lin("out%d" % l, m, Mg, nt)
                tr_ = tmp()
                P.ts(tr_[:, 0:nt], po, g1p[l][:, m, g:g + 1], None, ALU.mult)
                P.stt(X[:, m, 0:nt], X[:, m, 0:nt], ALU_ALPHA, tr_[:, 0:nt], ALU.mult, ALU.add)
            ln_fm(X1, X, 8, nt, lnvs[l][:, 0, :], lnvs[l][:, 1, :])
            milestone("ln1")
            for kc in range(8):
                P.act(Hb[:, kc, 0:nt], X1[:, kc, 0:nt], AF.Identity, bias=modT[l][:, 24 + kc, g:g + 1], scale=scp2[l][:, kc, g:g + 1])
            for j in range(KC_F):
                aj = Aj[j % 2]
                P.copy(aj[:, 0:2], st.Ah[:, j, :], eng="pool")
                pa = lin("up%d" % l, j, Hb, nt)
                P.copy(aj[:, 2:2 + nt], pa, eng="act")
                P.copy(st.Ah[:, j, :], aj[:, nt:nt + 2], eng="pool")
                pvv = lin("up%d" % l, KC_F + j, Hb, nt)
                ca = tmp()
                P.ts(ca[:, 0:nt], aj[:, 0:nt], ffnw[l][:, j, 0:1], None, ALU.mult)
                P.stt(ca[:, 0:nt], aj[:, 1:1 + nt], ffnw[l][:, j, 1:2], ca[:, 0:nt], ALU.mult, ALU.add)
                P.stt(ca[:, 0:nt], aj[:, 2:2 + nt], ffnw[l][:, j, 2:3], ca[:, 0:nt], ALU.mult, ALU.add)
                P.act(ca[:, 0:nt], ca[:, 0:nt], AF.Silu, bias=ffnb[l][:, j:j + 1])
                P.tt(Gf[:, j, 0:nt], ca[:, 0:nt], pvv, ALU.mult)
            for m in range(8):
                py = lin([("dna%d" % l, 0), ("dnb%d" % l, 8), ("dnc%d" % l, 16)], m, Gf, nt)
                tr_ = tmp()
                P.ts(tr_[:, 0:nt], py, g2p[l][:, m, g:g + 1], None, ALU.mult)
                P.stt(X1[:, m, 0:nt], X1[:, m, 0:nt], ALU_ALPHA, tr_[:, 0:nt], ALU.mult, ALU.add)
            ln_fm(X, X1, 8, nt, lnvs[l][:, 2, :], lnvs[l][:, 3, :])

        ALU_ALPHA = float(ALPHA)
        rowbf = [P.sb([1, TB], BF16) for _ in range(2)]

        def init_state_zero(st):
            P.memset(st.Uh[:], 0.0); P.memset(st.Qh[:], 0.0, eng="pool"); P.memset(st.S[:], 0.0)
            P.memset(st.Ah[:], 0.0, eng="pool"); P.memset(st.Kb2[:], 0.0)

        def write_states(l, st, nt, o_ca, o_gc, o_g, o_f):
            P.dma(o_ca.rearrange("(kc p) t -> p kc t", p=128), st.Uh[:, :, :])
            P.dma(o_gc.rearrange("(kc p) t -> p kc t", p=128), st.Qh[:, :, :])
            P.dma(o_g.rearrange("h k v -> k h v"), st.S[:, :, :])
            P.dma(o_f.rearrange("(kc p) t -> p kc t", p=128), st.Ah[:, :, :])

        for l in range(L):
            init_state_zero(stP[l])
        for b in range(NB):
            t0 = b * TB
            P.dma(X[:, :, 0:TB], xpT[:, t0:t0 + TB].rearrange("(kc p) t -> p kc t", p=128))
            ln_fm(X1, X, 8, TB, ln0s[:, 0, :], ln0s[:, 1, :])
            P.copy(X[:, :, 0:TB], X1[:, :, 0:TB], eng="pool")
            for l in range(L):
                groups = [(gb * TB, TB, gb == b) for gb in range(b + 1)]
                block_layer(l, stP[l], TB, 0, ropeP[:, :, t0:t0 + TB], t0, t0, groups,
                            {"lat": o_platT[l], "kpe": o_pkpeT[l]}, min(128, TB))
                if b == 0 and l == 0:
                    dbg("x_l0b0", X[:, :, 0:TB], [128, 8, TB])
            P.dma(ypT[:, t0:t0 + TB].rearrange("(kc p) t -> p kc t", p=128), X[:, :, 0:TB])
        for l in range(L):
            write_states(l, stP[l], TB, o_pcaT[l], o_pgcT[l], o_pg[l], o_pfT[l])

        for s in range(NS):
            P.dma(X[:, :, 0:TS], xsT[:, s * TS:(s + 1) * TS].rearrange("(kc p) t -> p kc t", p=128))
            ln_fm(X1, X, 8, TS, ln0s[:, 0, :], ln0s[:, 1, :])
            P.copy(X[:, :, 0:TS], X1[:, :, 0:TS], eng="pool")
            for l in range(L):
                st = stS
                P.dma(st.Uh[:, :, :], hA[l, s].rearrange("(kc p) t -> p kc t", p=128))
                P.dma(st.Qh[:, :, :], hB[l, s].rearrange("(kc p) t -> p kc t", p=128))
                P.dma(st.S[:, :, :], sG[l, s].rearrange("h k v -> k h v"))
                P.dma(st.Ah[:, :, :], hF[l, s].rearrange("(kc p) t -> p kc t", p=128))
                P.memset(st.Kb2[:], 0.0)
                for k0 in range(0, PAST, TB):
                    nk = min(TB, PAST - k0)
                    P.dma(LAT[:, :, 0:nk], clatT[l, s][:, k0:k0 + nk].rearrange("(kc p) t -> p kc t", p=128))
                    P.copy(LATb[:, :, 0:nk], LAT[:, :, 0:nk])
                    P.dma(KPE[:, 0:nk], ckpeT[l, s][:, k0:k0 + nk])
                    P.copy(KPEb[0:64, 0:nk], KPE[:, 0:nk], eng="pool")
                    P.memset(KPEb[64:65, 0:nk], 1.0, eng="pool")
                    P.dma(st.KR[:, k0:k0 + nk], KPEb[:, 0:nk])
                    P.act(SQ[0:64, 0, 0:nk], KPE[:, 0:nk], AF.Square)
                    knorm(l, st, nk, k0)
                    for sbk in range((nk + 127) // 128):
                        kk = min(128, nk - 128 * sbk)
                        pvv = pst()
                        for kc in range(2):
                            P.mm(pvv[0:kk, :], LATb[:, kc, 128 * sbk:128 * sbk + kk], wuv[l][:, kc, :], kc == 0, kc == 1)
                        P.copy(VVb[0:kk, 4 * sbk:4 * sbk + 4, :].rearrange("p h d -> p (h d)"), pvv[0:kk, :], eng="act")
                        for h in range(4):
                            P.dma(st.VV[h, k0 + 128 * sbk:k0 + 128 * sbk + kk, :], VVb[0:kk, 4 * sbk + h, :],
                                  q=("sp", "pool")[h % 2])
                groups = [(k0, min(512, PAST - k0), False) for k0 in range(0, PAST, 512)] + [(PAST, TS, False)]
                block_layer(l, st, TS, 1 + s, ropeS[:, :, :], 0, PAST, groups,
                            {"lat": o_slatT[l, s], "kpe": o_skpeT[l, s]}, TS)
                write_states(l, st, TS, o_scaT[l, s], o_sgcT[l, s], o_sg[l, s], o_sfT[l, s])
            P.dma(ysT[:, s * TS:(s + 1) * TS].rearrange("(kc p) t -> p kc t", p=128), X[:, :, 0:TS])


    try:
        main_body()
    except StopBuild:
        pass
    P.wait_all("sp")
    P.finalize()
    return nc, sorted(dbg_out.keys())


import numpy as np
from concourse.bass_utils import run_bass_kernel_spmd

NCORES = 8
FULL_CFG = dict(SEQ=16384, NS=4, PAST=2048, TB=256, L=2)


def _pp(vec, n):
    return np.ascontiguousarray(vec.reshape(n, 128).T)


def _rope_tables(pos):
    half = 32
    inv_freq = np.power(np.float32(10000.0), -(np.arange(half, dtype=np.float32) / np.float32(half))).astype(np.float32)
    ang = pos.astype(np.float32)[:, None] * inv_freq[None, :]
    cos = np.cos(ang).astype(np.float32); sin = np.sin(ang).astype(np.float32)
    cosT = np.concatenate([cos, cos], axis=1).T
    sinT = np.concatenate([-sin, sin], axis=1).T
    return np.ascontiguousarray(np.stack([cosT, sinT], axis=0))


def prep_inputs(inp, cfg):
    SEQ, NS, PAST, L = cfg["SEQ"], cfg["NS"], cfg["PAST"], cfg["L"]
    f = lambda a: np.ascontiguousarray(np.asarray(a, dtype=np.float32))
    perm = np.concatenate([np.arange(32, 64), np.arange(0, 32)])
    sh = {}
    sh["xpT"] = f(np.asarray(inp["x_prompt"])[0].T)
    sh["ln0p"] = f(np.stack([_pp(np.asarray(inp["ln0_g"]), 8), _pp(np.asarray(inp["ln0_b"]), 8)], axis=1))
    sh["w_ada"] = f(inp["w_ada"])
    sh["b_ada_p"] = f(np.stack([_pp(np.asarray(inp["b_ada"])[l], 48) for l in range(L)]))
    w_in = np.asarray(inp["w_in"])
    sh["w_in"] = f(w_in)
    sh["w_kpp"] = f(w_in[:, :, O_KPE + perm])
    caw = np.asarray(inp["conv_a_w"])
    sh["cva_w"] = f(np.stack([caw[l].T.reshape(4, 128, 31).transpose(1, 0, 2) for l in range(L)]))
    sh["cva_v"] = f(np.stack([np.stack([_pp(np.asarray(inp[k])[l], 4) for k in ("conv_a_b", "ln_a_g", "ln_a_b")], axis=1)
                              for l in range(L)]))
    gcw = np.asarray(inp["gdn_conv_w"])
    sh["gcv_w"] = f(np.stack([gcw[l].T.reshape(12, 128, 4).transpose(1, 0, 2) for l in range(L)]))
    sh["gdn_s"] = f(np.stack([np.asarray(inp["gdn_a_log"]), np.asarray(inp["gdn_dt_bias"])], axis=-1))
    sh["gdn_ng"] = f(np.asarray(inp["gdn_norm_g"])[:, :, None])
    sh["qn_g"] = f(np.stack([_pp(np.asarray(inp["mla_q_norm_g"])[l], 3) for l in range(L)]))
    sh["kvn_g"] = f(np.stack([_pp(np.asarray(inp["mla_kv_norm_g"])[l], 2) for l in range(L)]))
    wuq = np.asarray(inp["mla_w_uq"]).reshape(L, 384, 4, 192)
    sh["w_uqn"] = f(wuq[:, :, :, :128].reshape(L, 384, 512))
    sh["w_uqr"] = f(wuq[:, :, :, 128:].reshape(L, 384, 256))
    sh["w_uqp"] = f(wuq[:, :, :, 128 + perm].reshape(L, 384, 256))
    wukv = np.asarray(inp["mla_w_ukv"]).reshape(L, 256, 4, 256)
    sh["w_uk"] = f(wukv[:, :, :, :128].reshape(L, 256, 512))
    sh["w_uv"] = f(wukv[:, :, :, 128:].reshape(L, 256, 512))
    sh["w_br"] = f(inp["w_branch"]); sh["w_out"] = f(inp["w_out"])
    sh["lnv"] = f(np.stack([np.stack([_pp(np.asarray(inp[k])[l], 8) for k in ("ln1_g", "ln1_b", "ln2_g", "ln2_b")], axis=1)
                            for l in range(L)]))
    sh["w_up"] = f(inp["w_up"])
    fcw = np.asarray(inp["ffn_conv_w"])
    sh["ffn_w"] = f(np.stack([fcw[l].T.reshape(KC_F, 128, 3).transpose(1, 0, 2) for l in range(L)]))
    sh["ffn_b"] = f(np.stack([_pp(np.asarray(inp["ffn_conv_b"])[l], KC_F) for l in range(L)]))
    sh["w_dn"] = f(inp["w_down"])
    sh["c_ident"] = np.eye(128, dtype=np.float32)
    pi = np.arange(128)[:, None]; fi = np.arange(128)[None, :]
    sh["c_triu"] = (pi <= fi).astype(np.float32)
    sh["c_mtri"] = (pi >= fi).astype(np.float32)
    sh["c_mtriT"] = (pi <= fi).astype(np.float32)
    sh["c_mstr"] = (pi > fi).astype(np.float32)
    sh["ropeP"] = _rope_tables(np.arange(SEQ))
    sh["ropeS"] = _rope_tables(PAST + np.arange(16))
    xs = np.asarray(inp["x_sample"]); cs = np.asarray(inp["c_sample"]); cp = np.asarray(inp["c_prompt"])
    in_maps = []
    for c in range(NCORES):
        sl = slice(c * NS, (c + 1) * NS)
        m = dict(sh)
        m["xsT"] = f(xs[sl].reshape(NS * 16, D).T)
        m["cT"] = f(np.concatenate([cp[0:1], cs[sl]], axis=0).T)
        m["clatT"] = f(np.asarray(inp["cache_mla_latent"])[:, sl].transpose(0, 1, 3, 2))
        m["ckpeT"] = f(np.asarray(inp["cache_mla_kpe"])[:, sl].transpose(0, 1, 3, 2))
        m["hA"] = f(np.asarray(inp["state_conv_a"])[:, sl].transpose(0, 1, 3, 2))
        m["hB"] = f(np.asarray(inp["state_gdn_conv"])[:, sl].transpose(0, 1, 3, 2))
        m["sG"] = f(np.asarray(inp["state_gdn"])[:, sl])
        m["hF"] = f(np.asarray(inp["state_ffn_conv"])[:, sl].transpose(0, 1, 3, 2))
        in_maps.append(m)
    return in_maps


def assemble(res, cfg):
    NS = cfg["NS"]
    r0 = res[0]
    T = lambda a: np.ascontiguousarray(np.swapaxes(a, -1, -2))
    y_p = T(r0["ypT"])[None]
    y_s = np.concatenate([T(r["ysT"]).reshape(NS, 16, D) for r in res], axis=0)
    p_lat = T(r0["o_platT"])[:, None]; p_kpe = T(r0["o_pkpeT"])[:, None]
    p_ca = T(r0["o_pcaT"])[:, None]; p_gc = T(r0["o_pgcT"])[:, None]; p_g = r0["o_pg"][:, None]; p_f = T(r0["o_pfT"])[:, None]
    cat = lambda k, tr: np.concatenate([(T(r[k]) if tr else r[k]) for r in res], axis=1)
    outs = (y_p, y_s, p_lat, p_kpe, p_ca, p_gc, p_g, p_f,
            cat("o_slatT", True), cat("o_skpeT", True), cat("o_scaT", True), cat("o_sgcT", True), cat("o_sg", False),
            cat("o_sfT", True))
    return tuple(np.ascontiguousarray(o, dtype=np.float32) for o in outs)


def run(inputs, cfg, debug=()):
    nc, dbg_names = build_program(cfg, debug)
    in_maps = prep_inputs(inputs, cfg)
    res = run_bass_kernel_spmd(nc, in_maps, core_ids=list(range(NCORES)))
    return assemble(res.results, cfg), res.results


def kernel(**inputs):
    outs, _ = run(inputs, FULL_CFG)
    return outs
```

```python
from contextlib import ExitStack
import numpy as np
import concourse.bass as bass
import concourse.mybir as mybir

F32 = mybir.dt.float32
BF16 = mybir.dt.bfloat16
I32 = mybir.dt.int32
AF = mybir.ActivationFunctionType
ALU = mybir.AluOpType

ENGS = ("pe", "act", "dve", "pool", "sp")
NDMA = 12


class Trk:
    __slots__ = ("w", "r", "name", "multi", "serial")

    def __init__(self, name="", multi=False):
        self.w = {}
        self.r = {}
        self.name = name
        self.multi = multi
        self.serial = False


class T:
    def __init__(self, handle, name):
        self.h = handle
        self.k = Trk(name)
        self.shape = tuple(handle.shape)

    def __getitem__(self, key):
        return V(self.h[key] if not isinstance(key, tuple) or True else None, [self.k])

    def ap(self):
        return V(self.h.ap() if hasattr(self.h, "ap") else self.h[:], [self.k])


class V:
    def __init__(self, ap, ks):
        self.ap = ap
        self.ks = ks

    def __getitem__(self, key):
        return V(self.ap[key], self.ks)

    def rearrange(self, s, **kw):
        return V(self.ap.rearrange(s, **kw), self.ks)

    def bitcast(self, dt):
        return V(self.ap.bitcast(dt), self.ks)

    def broadcast_to(self, shape):
        return V(self.ap.broadcast_to(shape), self.ks)

    def to_broadcast(self, shape):
        return V(self.ap.to_broadcast(shape), self.ks)

    def partition_broadcast(self, n):
        return V(self.ap.partition_broadcast(n), self.ks)

    def unsqueeze(self, a):
        return V(self.ap.unsqueeze(a), self.ks)

    def bc(self, n):
        sh = list(self.ap.shape)
        return V(self.ap.unsqueeze(len(sh)).to_broadcast(sh + [n]), self.ks)

    @property
    def shape(self):
        return tuple(self.ap.shape)


def _ap(x):
    return x.ap if isinstance(x, V) else x


class Prog:
    def __init__(self, nc):
        self.nc = nc
        self.es = ExitStack()
        self.q = {e: [] for e in ENGS}
        self.cnt = {e: 0 for e in ENGS}
        self.seen = {e: {} for e in ENGS}
        self.sem = {}
        for e in ENGS:
            self.sem[("c", e)] = self.es.enter_context(nc.semaphore("s_" + e))
        self.dtot = {}
        for qn in ("sp", "pool"):
            for i in range(NDMA):
                k = ("d", qn, i)
                self.sem[k] = self.es.enter_context(nc.semaphore("d_%s%d" % (qn, i)))
                self.dtot[k] = 0
        self.drr = {"sp": 0, "pool": 0}
        self.ntile = 0
        self.psum_free = None

    def sb(self, shape, dt=F32, name=None, multi=False):
        self.ntile += 1
        name = name or "t%d" % self.ntile
        h = self.es.enter_context(self.nc.sbuf_tensor(name, list(shape), dt))
        t = T(h, name)
        t.k.multi = multi
        return t

    def ps(self, shape, dt=F32, name=None):
        self.ntile += 1
        name = name or "p%d" % self.ntile
        h = self.es.enter_context(self.nc.psum_tensor(name, list(shape), dt))
        t = T(h, name)
        t.k.serial = True
        return t

    def dram(self, name, shape, dt=F32, kind=None, multi=False):
        if kind is None:
            h = self.nc.dram_tensor(name, list(shape), dt)
        else:
            h = self.nc.dram_tensor(name, list(shape), dt, kind=kind)
        t = T(h, name)
        t.k.multi = multi
        return t

    def emit(self, eng, fn, reads, writes, dma=False, accum_pe=False):
        rk, wk = [], []
        for x in reads:
            if x is None:
                continue
            rk.extend(x.ks if isinstance(x, V) else [x.k])
        for x in writes:
            rk_ = x.ks if isinstance(x, V) else [x.k]
            wk.extend(rk_)
        deps = {}

        def need(ev):
            if ev is None:
                return
            k, v = ev
            if deps.get(k, 0) < v:
                deps[k] = v

        myk = ("c", eng)
        for k in rk:
            for sk, v in k.w.items():
                need((sk, v))
            if k.serial:
                for sk, v in k.r.items():
                    if sk != myk:
                        need((sk, v))
        for k in wk:
            if not k.multi:
                for sk, v in k.w.items():
                    if accum_pe and sk == myk:
                        continue
                    need((sk, v))
            for sk, v in k.r.items():
                if sk == myk and not dma:
                    continue
                need((sk, v))
        if dma:
            i = self.drr[eng]
            self.drr[eng] = (i + 1) % NDMA
            dk = ("d", eng, i)
            if self.dtot[dk] > 0:
                need((dk, self.dtot[dk]))
            self.dtot[dk] += 16
            ev = (dk, self.dtot[dk])
            inc = (self.sem[dk], 16)
        else:
            self.cnt[eng] += 1
            ev = (myk, self.cnt[eng])
            inc = (self.sem[myk], 1)
        waits = []
        seen = self.seen[eng]
        for k, v in deps.items():
            if seen.get(k, 0) >= v:
                continue
            seen[k] = v
            waits.append((self.sem[k], v))
        self.q[eng].append((waits, fn, inc))
        for k in wk:
            if k.multi:
                if k.w.get(ev[0], 0) < ev[1]:
                    k.w[ev[0]] = ev[1]
            else:
                k.w = {ev[0]: ev[1]}
                k.r = {}
        for k in rk:
            if k in wk:
                continue
            if k.r.get(ev[0], 0) < ev[1]:
                k.r[ev[0]] = ev[1]
        return ev

    def wait_all(self, eng="sp"):
        waits = []
        for e in ENGS:
            if self.cnt[e] > 0:
                waits.append((self.sem[("c", e)], self.cnt[e]))
        for k, v in self.dtot.items():
            if v > 0:
                waits.append((self.sem[k], v))
        self.q[eng].append((waits, None, None))

    def finalize(self):
        nc = self.nc
        q = self.q

        def run(engname):
            def body(eng):
                for waits, fn, inc in q[engname]:
                    for s, v in waits:
                        eng.wait_ge(s, v)
                    if fn is not None:
                        ins = fn(eng)
                        ins.then_inc(inc[0], inc[1])
            return body

        with nc.Block() as block:
            block.tensor(run("pe"))
            block.scalar(run("act"))
            block.vector(run("dve"))
            block.gpsimd(run("pool"))
            block.sync(run("sp"))
        self.es.close()

    def dma(self, out, in_, q="sp", **kw):
        return self.emit(q, lambda e: e.dma_start(out=_ap(out), in_=_ap(in_), **kw), [in_], [out], dma=True)

    def mm(self, out, lhsT, rhs, start=True, stop=True):
        return self.emit("pe", lambda e: e.matmul(_ap(out), _ap(lhsT), _ap(rhs), start=start, stop=stop),
                         [lhsT, rhs], [out], accum_pe=True)

    def tr(self, out, in_, ident):
        return self.emit("pe", lambda e: e.transpose(_ap(out), _ap(in_), _ap(ident)), [in_, ident], [out],
                         accum_pe=True)

    def act(self, out, in_, func, bias=None, scale=None, eng="act", accum_out=None):
        kw = {}
        rd = [in_]
        if bias is not None:
            kw["bias"] = _ap(bias)
            if isinstance(bias, (V, T)):
                rd.append(bias)
        if scale is not None:
            kw["scale"] = _ap(scale)
            if isinstance(scale, (V, T)):
                rd.append(scale)
        wr = [out]
        if accum_out is not None:
            kw["accum_out"] = _ap(accum_out)
            wr.append(accum_out)
        return self.emit("act", lambda e: e.activation(_ap(out), _ap(in_), func, **kw), rd, wr)

    def tt(self, out, a, b, op, eng="dve"):
        return self.emit(eng, lambda e: e.tensor_tensor(_ap(out), _ap(a), _ap(b), op), [a, b], [out])

    def ts(self, out, a, s1, s2=None, op0=ALU.mult, op1=None, eng="dve"):
        rd = [a] + [s for s in (s1, s2) if isinstance(s, (V, T))]
        if op1 is None:
            return self.emit(eng, lambda e: e.tensor_scalar(_ap(out), _ap(a), _ap(s1), None, op0), rd, [out])
        return self.emit(eng, lambda e: e.tensor_scalar(_ap(out), _ap(a), _ap(s1), _ap(s2), op0, op1), rd, [out])

    def stt(self, out, a, s, b, op0, op1):
        rd = [a, b] + ([s] if isinstance(s, (V, T)) else [])
        return self.emit("dve", lambda e: e.scalar_tensor_tensor(_ap(out), _ap(a), _ap(s), _ap(b), op0, op1),
                         rd, [out])

    def copy(self, out, in_, eng="dve"):
        if eng == "act":
            return self.act(out, in_, AF.Copy)
        return self.emit(eng, lambda e: e.tensor_copy(_ap(out), _ap(in_)), [in_], [out])

    def memset(self, out, val, eng="dve"):
        return self.emit(eng, lambda e: e.memset(_ap(out), val), [], [out])

    def recip(self, out, in_):
        return self.emit("dve", lambda e: e.reciprocal(_ap(out), _ap(in_)), [in_], [out])


import math
import numpy as np

D = 1024
KC_D = 8
IN_W = 6856
NH = 4
DFF = 2816
KC_F = 22
O_AV, O_AG, O_Q, O_Z, O_BETA, O_DEC, O_QL, O_KVL, O_KPE, O_GATE = 0, 512, 1024, 2560, 3072, 3076, 3080, 3464, 3720, 3784
ALPHA = 4 ** 0.25
LN_EPS = 1e-5
RMS_EPS = 1e-6
SCALE_Q = 192 ** -0.5


def build_program(cfg, debug=()):
    SEQ, NS, PAST, TB, L = cfg["SEQ"], cfg["NS"], cfg["PAST"], cfg["TB"], cfg["L"]
    TS = 16
    NB = SEQ // TB
    G = 1 + NS
    nc = bass.Bass("TRN2", target_bir_lowering=False)
    P = Prog(nc)
    dbg_out = {}

    def din(name, shape, dt=F32):
        return P.dram(name, shape, dt, kind="ExternalInput")

    def dout(name, shape, dt=F32):
        return P.dram(name, shape, dt, kind="ExternalOutput")

    xpT = din("xpT", [D, SEQ]); xsT = din("xsT", [D, NS * TS]); cT = din("cT", [D, G])
    ln0p = din("ln0p", [128, 2, 8])
    w_ada = din("w_ada", [L, D, 6 * D]); b_ada_p = din("b_ada_p", [L, 128, 48])
    w_in = din("w_in", [L, D, IN_W]); w_kpp = din("w_kpp", [L, D, 64])
    cva_w = din("cva_w", [L, 128, 4, 31]); cva_v = din("cva_v", [L, 128, 3, 4])
    gcv_w = din("gcv_w", [L, 128, 12, 4]); gdn_s = din("gdn_s", [L, 4, 2]); gdn_ng = din("gdn_ng", [L, 128, 1])
    qn_g = din("qn_g", [L, 128, 3]); kvn_g = din("kvn_g", [L, 128, 2])
    w_uqn = din("w_uqn", [L, 384, 512]); w_uqr = din("w_uqr", [L, 384, 256]); w_uqp = din("w_uqp", [L, 384, 256])
    w_uk = din("w_uk", [L, 256, 512]); w_uv = din("w_uv", [L, 256, 512])
    w_br = din("w_br", [L, 3, 512, D]); w_out = din("w_out", [L, D, D])
    lnv = din("lnv", [L, 128, 4, 8])
    w_up = din("w_up", [L, D, 2 * DFF]); ffn_w = din("ffn_w", [L, 128, KC_F, 3]); ffn_b = din("ffn_b", [L, 128, KC_F])
    w_dn = din("w_dn", [L, DFF, D])
    clatT = din("clatT", [L, NS, 256, PAST]); ckpeT = din("ckpeT", [L, NS, 64, PAST])
    hA = din("hA", [L, NS, 512, 30]); hB = din("hB", [L, NS, 1536, 3]); sG = din("sG", [L, NS, 4, 128, 128])
    hF = din("hF", [L, NS, DFF, 2])
    c_ident = din("c_ident", [128, 128]); c_triu = din("c_triu", [128, 128])
    c_mtri = din("c_mtri", [128, 128]); c_mtriT = din("c_mtriT", [128, 128]); c_mstr = din("c_mstr", [128, 128])
    ropeP = din("ropeP", [2, 64, SEQ]); ropeS = din("ropeS", [2, 64, TS])
    ypT = dout("ypT", [D, SEQ]); ysT = dout("ysT", [D, NS * TS])
    o_platT = dout("o_platT", [L, 256, SEQ]); o_pkpeT = dout("o_pkpeT", [L, 64, SEQ])
    o_pcaT = dout("o_pcaT", [L, 512, 30]); o_pgcT = dout("o_pgcT", [L, 1536, 3]); o_pg = dout("o_pg", [L, 4, 128, 128])
    o_pfT = dout("o_pfT", [L, DFF, 2])
    o_slatT = dout("o_slatT", [L, NS, 256, TS]); o_skpeT = dout("o_skpeT", [L, NS, 64, TS])
    o_scaT = dout("o_scaT", [L, NS, 512, 30]); o_sgcT = dout("o_sgcT", [L, NS, 1536, 3]); o_sg = dout("o_sg", [L, NS, 4, 128, 128])
    o_sfT = dout("o_sfT", [L, NS, DFF, 2])

    class StopBuild(Exception):
        pass

    def milestone(name):
        if cfg.get("stop") == name:
            raise StopBuild()

    def dbg(name, v, shape, dt=F32):
        if name in debug and name not in dbg_out:
            t = dout("dbg_" + name, list(shape), dt)
            dbg_out[name] = t
            P.dma(t[:], v)

    def main_body():
        ident = P.sb([128, 128]); triu = P.sb([128, 128]); mtri = P.sb([128, 128]); mtriT = P.sb([128, 128]); mstr = P.sb([128, 128])
        ones_f = P.sb([128, 128]); ones_b = P.sb([128, 128], BF16)
        for t, s in ((ident, c_ident), (triu, c_triu), (mtri, c_mtri), (mtriT, c_mtriT), (mstr, c_mstr)):
            P.dma(t[:], s[:])
        P.memset(ones_f[:], 1.0); P.memset(ones_b[:], 1.0)
        def b4(t, C):
            return t[0:C, 0:C].unsqueeze(1).to_broadcast([C, 4, C])
        ln0s = P.sb([128, 2, 8]); P.dma(ln0s[:], ln0p[:])

        PSN = 6
        pspool = [P.ps([128, 512], name="psg%d" % i) for i in range(PSN)]
        ps_o = P.ps([128, 512], name="ps_o"); ps_s = P.ps([128, 512], name="ps_s")
        psi = [0]

        def pst():
            t = pspool[psi[0] % PSN]
            psi[0] += 1
            return t
        psa = [0]; psb = [0]

        def pstA():
            t = pspool[psa[0] % 3]
            psa[0] += 1
            return t

        def pstB():
            t = pspool[3 + psb[0] % 3]
            psb[0] += 1
            return t

        W = {}
        stg_f = []
        stg_b = []
        prep_i = [0]

        def prep_w(name, src, K, N, Mc=128):
            KC = K // 128
            NM = (N + Mc - 1) // Mc
            scr = P.dram("wc_" + name, [NM, 128, KC * Mc], BF16)
            for mi in range(NM):
                m0 = mi * Mc
                mc = min(Mc, N - m0)
                i = prep_i[0]; prep_i[0] += 1
                sf = stg_f[i % 2]; sbf = stg_b[i % 2]
                q = "sp" if i % 2 == 0 else "pool"
                P.dma(sf[:, 0:KC * mc].rearrange("p (kc m) -> p kc m", kc=KC),
                      src[:, m0:m0 + mc].rearrange("(kc p) m -> p kc m", p=128), q=q)
                ce = ("dve", "act")[i % 2]
                P.copy(sbf[:, 0:KC * mc], sf[:, 0:KC * mc], eng=ce)
                P.dma(scr[mi, :, 0:KC * mc], sbf[:, 0:KC * mc], q=q)
            W[name] = (scr, KC, Mc, N)

        wbufs = [P.sb([128, 8 * 128], BF16, name="wbuf%d" % i) for i in range(8)]
        wbi = [0]

        def load_w(name, mi):
            scr, KC, Mc, N = W[name]
            mc = min(Mc, N - mi * Mc)
            i = wbi[0]; wbi[0] += 1
            wb = wbufs[i % 8]
            P.dma(wb[:, 0:KC * mc], scr[mi, :, 0:KC * mc], q="sp")
            return wb[:, 0:KC * mc].rearrange("p (kc m) -> p kc m", kc=KC), KC, mc

        def lin(name, mi, rhs, nt):
            parts = name if isinstance(name, list) else [(name, 0)]
            ps = pst()
            np_ = len(parts)
            for pi, (nm, ko) in enumerate(parts):
                wv, KC, mc = load_w(nm, mi)
                for kc in range(KC):
                    P.mm(ps[0:mc, 0:nt], wv[:, kc, :], rhs[:, ko + kc, 0:nt], start=(pi == 0 and kc == 0),
                         stop=(pi == np_ - 1 and kc == KC - 1))
            return ps[0:mc, 0:nt]

        X = P.sb([128, 8, TB]); X1 = P.sb([128, 8, TB]); Hb = P.sb([128, 8, TB], BF16)
        SQ = P.sb([128, 8, TB])
        U = P.sb([128, 4, 30 + TB]); QKVP = P.sb([128, 12, 3 + TB]); QKV = P.sb([128, 12, TB])
        Zs = P.sb([128, 4, TB]); G3 = P.sb([128, 24, TB], BF16)
        BE = P.sb([4, TB]); GG = P.sb([4, TB])
        YA = P.sb([128, 4, TB], BF16); YB = P.sb([128, 4, TB], BF16); YC = P.sb([128, 4, TB], BF16)
        QLn = P.sb([128, 3, TB], BF16); QLf = P.sb([128, 3, TB])
        LAT = P.sb([128, 2, TB]); LATb = P.sb([128, 2, TB], BF16)
        KPE = P.sb([64, TB]); KPEb = P.sb([65, TB], BF16)
        Qn = P.sb([128, 4, TB], BF16); QR = P.sb([65, 4, TB], BF16); QNf = P.sb([128, 4, TB]); QRf = P.sb([64, 4, TB])
        KNf = P.sb([128, TB]); KNb = P.sb([128, 4, TB], BF16); VVb = P.sb([128, 4 * max(1, TB // 128), 128], BF16)
        Mg = P.sb([128, 8, TB], BF16); Gf = G3
        Aj = [P.sb([128, 2 + TB]) for _ in range(2)]
        tA = [P.sb([128, TB]) for _ in range(4)]
        tAi = [0]

        def tmp():
            t = tA[tAi[0] % 4]
            tAi[0] += 1
            return t
        rowk = P.sb([1, 4])
        ropeT = P.sb([64, 2, TB])
        knt = [P.sb([128, 512], BF16) for _ in range(3)]; krt = [P.sb([65, 512], BF16) for _ in range(3)]
        vts = [P.sb([128, 4, 128], BF16) for _ in range(3)]; pts = [P.sb([128, 512], BF16) for _ in range(3)]
        avi = [0]
        accS = P.sb([128, TB])

        stg_f.extend([QKVP[:, :, :].rearrange("p a b -> p (a b)"), QKV[:, :, :].rearrange("p a b -> p (a b)")])
        stg_b.extend([Mg[:, :, :].rearrange("p a b -> p (a b)"), G3[:, :, :].rearrange("p a b -> p (a b)")])
        for l in range(L):
            prep_w("ada%d" % l, w_ada[l], D, 6 * D)
            prep_w("inA%d" % l, w_in[l][:, 0:3072], D, 3072)
            prep_w("inbeta%d" % l, w_in[l][:, O_BETA:O_BETA + 4], D, 4, Mc=4)
            prep_w("indec%d" % l, w_in[l][:, O_DEC:O_DEC + 4], D, 4, Mc=4)
            prep_w("inql%d" % l, w_in[l][:, O_QL:O_QL + 384], D, 384)
            prep_w("inkvl%d" % l, w_in[l][:, O_KVL:O_KVL + 256], D, 256)
            prep_w("inkpe%d" % l, w_in[l][:, O_KPE:O_KPE + 64], D, 64, Mc=64)
            prep_w("inkpp%d" % l, w_kpp[l], D, 64, Mc=64)
            prep_w("ingate%d" % l, w_in[l][:, O_GATE:O_GATE + 3072], D, 3072)
            prep_w("uqn%d" % l, w_uqn[l], 384, 512)
            prep_w("uqr%d" % l, w_uqr[l], 384, 256, Mc=64)
            prep_w("uqp%d" % l, w_uqp[l], 384, 256, Mc=64)
            prep_w("uk%d" % l, w_uk[l], 256, 512)
            for n in range(3):
                prep_w("br%d_%d" % (l, n), w_br[l, n], 512, D)
            prep_w("out%d" % l, w_out[l], D, D)
            prep_w("up%d" % l, w_up[l], D, 2 * DFF)
            prep_w("dna%d" % l, w_dn[l][0:1024, :], 1024, D)
            prep_w("dnb%d" % l, w_dn[l][1024:2048, :], 1024, D)
            prep_w("dnc%d" % l, w_dn[l][2048:2816, :], 768, D)

        milestone("prep")
        cvaw = [P.sb([128, 4, 31]) for _ in range(L)]; cvav = [P.sb([128, 3, 4]) for _ in range(L)]
        gcvw = [P.sb([128, 12, 4]) for _ in range(L)]; gdns = [P.sb([4, 2]) for _ in range(L)]
        nexpA = [P.sb([4, 1]) for _ in range(L)]; gdnng = [P.sb([128, 1]) for _ in range(L)]
        qng = [P.sb([128, 3]) for _ in range(L)]; kvng = [P.sb([128, 2]) for _ in range(L)]
        lnvs = [P.sb([128, 4, 8]) for _ in range(L)]; ffnw = [P.sb([128, KC_F, 3]) for _ in range(L)]
        ffnb = [P.sb([128, KC_F]) for _ in range(L)]
        wuv = [P.sb([128, 2, 512], BF16) for _ in range(L)]
        modT = [P.sb([128, 48, G]) for _ in range(L)]
        scp1 = [P.sb([128, 8, G]) for _ in range(L)]; scp2 = [P.sb([128, 8, G]) for _ in range(L)]
        g1p = [P.sb([128, 8, G]) for _ in range(L)]; g2p = [P.sb([128, 8, G]) for _ in range(L)]
        badas = P.sb([128, 48])
        csT = P.sb([128, 8, G]); csb = P.sb([128, 8, G], BF16)
        P.dma(csT[:], cT[:, :].rearrange("(kc p) g -> p kc g", p=128))
        P.act(csb[:], csT[:], AF.Silu)
        for l in range(L):
            P.dma(cvaw[l][:], cva_w[l]); P.dma(cvav[l][:], cva_v[l]); P.dma(gcvw[l][:], gcv_w[l])
            P.dma(gdns[l][:], gdn_s[l]); P.dma(gdnng[l][:], gdn_ng[l]); P.dma(qng[l][:], qn_g[l]); P.dma(kvng[l][:], kvn_g[l])
            P.dma(lnvs[l][:], lnv[l]); P.dma(ffnw[l][:], ffn_w[l]); P.dma(ffnb[l][:], ffn_b[l])
            P.act(nexpA[l][:], gdns[l][:, 0:1], AF.Exp)
            P.ts(nexpA[l][:], nexpA[l][:], -1.0, None, ALU.mult)
            sf = stg_f[0]
            P.dma(sf[:, 0:1024].rearrange("p (kc m) -> p kc m", kc=2), w_uv[l].rearrange("(kc p) m -> p kc m", p=128))
            P.copy(wuv[l][:], sf[:, 0:1024].rearrange("p (kc m) -> p kc m", kc=2))
            P.dma(badas[:], b_ada_p[l])
            for j in range(48):
                ps = lin("ada%d" % l, j, csb, G)
                P.ts(modT[l][:, j, :], ps, badas[:, j:j + 1], None, ALU.add)
            P.ts(scp1[l][:], modT[l][:, 8:16, :], 1.0, None, ALU.add)
            P.ts(g1p[l][:], modT[l][:, 16:24, :], 1.0, None, ALU.add)
            P.ts(scp2[l][:], modT[l][:, 32:40, :], 1.0, None, ALU.add)
            P.ts(g2p[l][:], modT[l][:, 40:48, :], 1.0, None, ALU.add)

        class St:
            pass

        def mkstate(tk, name):
            s = St()
            s.Uh = P.sb([128, 4, 30]); s.Qh = P.sb([128, 12, 3]); s.S = P.sb([128, 4, 128]); s.Ah = P.sb([128, KC_F, 2])
            s.Kb2 = P.sb([1, 4])
            s.KN = P.dram("KN_" + name, [4, 128, tk], BF16); s.KR = P.dram("KR_" + name, [65, tk], BF16)
            s.VV = P.dram("VV_" + name, [4, tk, 128], BF16)
            return s

        stP = [mkstate(SEQ, "p%d" % l) for l in range(L)]
        stS = mkstate(PAST + TS, "s")

        def colstats(Xt, nch, nt, need_mean):
            npart = Xt.shape[0]
            F = float(nch * npart)
            p1 = None
            if need_mean:
                p1 = pst()
                for ch in range(nch):
                    P.mm(p1[:, 0:nt], ones_f[0:npart, :], Xt[:, ch, 0:nt], start=(ch == 0), stop=(ch == nch - 1))
            p2 = pst()
            for ch in range(nch):
                P.act(SQ[0:npart, ch, 0:nt], Xt[:, ch, 0:nt], AF.Square)
            for ch in range(nch):
                P.mm(p2[:, 0:nt], ones_f[0:npart, :], SQ[0:npart, ch, 0:nt], start=(ch == 0), stop=(ch == nch - 1))
            mean = None
            var = tmp()
            if need_mean:
                mean = tmp()
                P.act(mean[:, 0:nt], p1[:, 0:nt], AF.Copy, scale=1.0 / F)
                msq = tmp()
                P.tt(msq[:, 0:nt], mean[:, 0:nt], mean[:, 0:nt], ALU.mult)
                P.stt(var[:, 0:nt], p2[:, 0:nt], 1.0 / F, msq[:, 0:nt], ALU.mult, ALU.subtract)
                eps = LN_EPS
            else:
                P.act(var[:, 0:nt], p2[:, 0:nt], AF.Copy, scale=1.0 / F)
                eps = RMS_EPS
            P.ts(var[:, 0:nt], var[:, 0:nt], eps, None, ALU.add)
            P.act(var[:, 0:nt], var[:, 0:nt], AF.Sqrt)
            rstd = tmp()
            P.recip(rstd[:, 0:nt], var[:, 0:nt])
            return mean, rstd

        def ln_fm(dst, Xt, nch, nt, g_v, b_v):
            mean, rstd = colstats(Xt, nch, nt, True)
            mb = mean[:, 0:nt].unsqueeze(1).to_broadcast([128, nch, nt])
            rb = rstd[:, 0:nt].unsqueeze(1).to_broadcast([128, nch, nt])
            P.tt(SQ[:, 0:nch, 0:nt], Xt[:, 0:nch, 0:nt], mb, ALU.subtract)
            P.tt(SQ[:, 0:nch, 0:nt], SQ[:, 0:nch, 0:nt], rb, ALU.mult)
            P.tt(SQ[:, 0:nch, 0:nt], SQ[:, 0:nch, 0:nt], g_v.bc(nt), ALU.mult)
            P.tt(dst[:, 0:nch, 0:nt], SQ[:, 0:nch, 0:nt], b_v.bc(nt), ALU.add)

        milestone("params")
        C4 = lambda: P.sb([128, 4, 128])
        g_ktm = C4(); g_vtm = C4(); g_e1 = C4(); g_Dm = C4(); g_DmT = C4(); g_N = C4(); g_At = C4()
        g_A2 = C4(); g_At2 = C4(); g_Rt = C4(); g_qkT = C4(); g_expGb = C4(); g_gB = C4()
        g_bv = g_e1; g_bk = g_Dm; g_kd = g_DmT; g_u = g_gB; g_wT = g_N; g_vnew = g_At; g_o = g_A2; g_t = g_At2
        g_btm = P.sb([128, 4]); g_gtm = P.sb([128, 4]); g_nb = P.sb([128, 4]); g_Gcol = P.sb([128, 4]); g_eG = P.sb([128, 4])
        g_beG = P.sb([128, 4]); g_kdc = P.sb([128, 4]); g_gam = P.sb([128, 4])

        def gdn_chunk(l, st, c0, C):
            pk = pstA(); pv = pstA()
            for h in range(4):
                P.tr(pk[0:C, h * 128:(h + 1) * 128], QKV[:, 4 + h, c0:c0 + C], ident[:])
                P.tr(pv[0:C, h * 128:(h + 1) * 128], QKV[:, 8 + h, c0:c0 + C], ident[:])
            P.copy(g_ktm[0:C].rearrange("p h d -> p (h d)"), pk[0:C, :], eng="act")
            P.copy(g_vtm[0:C].rearrange("p h d -> p (h d)"), pv[0:C, :])
            pb = pstA()
            P.tr(pb[0:C, 0:4], BE[0:4, c0:c0 + C], ident[0:4, 0:4])
            P.tr(pb[0:C, 4:8], GG[0:4, c0:c0 + C], ident[0:4, 0:4])
            P.copy(g_btm[0:C, :], pb[0:C, 0:4]); P.copy(g_gtm[0:C, :], pb[0:C, 4:8], eng="act")
            P.ts(g_nb[0:C, :], g_btm[0:C, :], -1.0, None, ALU.mult)
            milestone("g1")
            yield
            P.tt(g_gB[0:C], ones_f[0:C, :].unsqueeze(1).to_broadcast([C, 4, 128]), g_gtm[0:C, :].bc(128), ALU.mult)
            pG = pstA(); pc = pstA()
            for h in range(4):
                P.mm(pG[:, h * 128:h * 128 + C], g_gB[0:C, h, :], triu[0:C, 0:C], True, True)
            P.mm(pc[0:C, 0:4], triu[0:C, 0:C], g_gtm[0:C, 0:4], True, True)
            milestone("g2")
            yield
            P.copy(g_Gcol[0:C, :], pc[0:C, 0:4])
            pGv = pG[:, :].rearrange("p (h j) -> p h j", h=4)
            P.act(g_expGb[:, :, 0:C], pGv[:, :, 0:C], AF.Exp)
            P.act(g_eG[0:C, :], g_Gcol[0:C, :], AF.Exp)
            P.tt(g_beG[0:C, :], g_btm[0:C, :], g_eG[0:C, :], ALU.mult)
            P.copy(g_gam[:, :], g_expGb[:, :, C - 1])
            P.copy(g_kdc[0:C, :], pGv[0:C, :, C - 1], eng="act")
            P.tt(g_kdc[0:C, :], g_kdc[0:C, :], g_Gcol[0:C, :], ALU.subtract)
            P.act(g_kdc[0:C, :], g_kdc[0:C, :], AF.Exp)
            P.tt(g_e1[0:C, :, 0:C], pGv[0:C, :, 0:C], g_Gcol[0:C, :].bc(C), ALU.subtract)
            P.ts(g_Dm[0:C, :, 0:C], g_e1[0:C, :, 0:C], 0.0, None, ALU.max)
            P.act(g_Dm[0:C, :, 0:C], g_Dm[0:C, :, 0:C], AF.Exp, scale=-1.0)
            P.ts(g_DmT[0:C, :, 0:C], g_e1[0:C, :, 0:C], 0.0, None, ALU.min)
            P.act(g_DmT[0:C, :, 0:C], g_DmT[0:C, :, 0:C], AF.Exp)
            P.tt(g_DmT[0:C, :, 0:C], g_DmT[0:C, :, 0:C], b4(mtriT, C), ALU.mult)
            P.tt(g_Dm[0:C, :, 0:C], g_Dm[0:C, :, 0:C], b4(mstr, C), ALU.mult)
            milestone("g3")
            yield
            pkk = pstA(); pqk = pstA()
            for h in range(4):
                P.mm(pkk[0:C, h * 128:h * 128 + C], QKV[:, 4 + h, c0:c0 + C], QKV[:, 4 + h, c0:c0 + C], True, True)
                P.mm(pqk[0:C, h * 128:h * 128 + C], QKV[:, 4 + h, c0:c0 + C], QKV[:, h, c0:c0 + C], True, True)
            pkkv = pkk[:, :].rearrange("p (h j) -> p h j", h=4); pqkv = pqk[:, :].rearrange("p (h j) -> p h j", h=4)
            P.tt(g_N[0:C, :, 0:C], pkkv[0:C, :, 0:C], g_Dm[0:C, :, 0:C], ALU.mult)
            P.tt(g_N[0:C, :, 0:C], g_N[0:C, :, 0:C], g_nb[0:C, :].bc(C), ALU.mult)
            P.tt(g_qkT[0:C, :, 0:C], pqkv[0:C, :, 0:C], g_DmT[0:C, :, 0:C], ALU.mult)
            milestone("g4")
            yield
            pt_ = pstA()
            for h in range(4):
                P.tr(pt_[0:C, h * 128:h * 128 + C], g_N[0:C, h, 0:C], ident[0:C, 0:C])
            ptv = pt_[:, :].rearrange("p (h j) -> p h j", h=4)
            milestone("g4t")
            P.copy(g_At[0:C, :, 0:C], ptv[0:C, :, 0:C], eng="act")
            milestone("g4c")
            P.tt(g_Rt[0:C, :, 0:C], g_At[0:C, :, 0:C], b4(ident, C), ALU.add)
            milestone("g4a")
            yield
            A, At, A2, At2 = g_N, g_At, g_A2, g_At2
            nsq = int(round(math.log2(C))) - 1
            for m in range(nsq):
                pa = pstA(); pat = pstA()
                for h in range(4):
                    P.mm(pa[0:C, h * 128:h * 128 + C], At[0:C, h, 0:C], A[0:C, h, 0:C], True, True)
                    P.mm(pat[0:C, h * 128:h * 128 + C], A[0:C, h, 0:C], At[0:C, h, 0:C], True, True)
                P.copy(A2[0:C, :, 0:C], pa[:, :].rearrange("p (h j) -> p h j", h=4)[0:C, :, 0:C], eng="act")
                P.copy(At2[0:C, :, 0:C], pat[:, :].rearrange("p (h j) -> p h j", h=4)[0:C, :, 0:C])
                A, At, A2, At2 = A2, At2, A, At
                yield
                pr = pstA()
                for h in range(4):
                    P.mm(pr[0:C, h * 128:h * 128 + C], A[0:C, h, 0:C], g_Rt[0:C, h, 0:C], True, True)
                P.tt(g_Rt[0:C, :, 0:C], g_Rt[0:C, :, 0:C], pr[:, :].rearrange("p (h j) -> p h j", h=4)[0:C, :, 0:C], ALU.add)
                milestone("g4b")
                yield
            milestone("g5")
            yield
            P.tt(g_bv[0:C], g_vtm[0:C], g_btm[0:C, :].bc(128), ALU.mult)
            P.tt(g_bk[0:C], g_ktm[0:C], g_beG[0:C, :].bc(128), ALU.mult)
            P.tt(g_kd[0:C], g_ktm[0:C], g_kdc[0:C, :].bc(128), ALU.mult)
            pu = pstA(); pw = pstA()
            for h in range(4):
                P.mm(pu[0:C, h * 128:(h + 1) * 128], g_Rt[0:C, h, 0:C], g_bv[0:C, h, :], True, True)
                P.mm(pw[:, h * 128:h * 128 + C], g_bk[0:C, h, :], g_Rt[0:C, h, 0:C], True, True)
            P.copy(g_u[0:C].rearrange("p h d -> p (h d)"), pu[0:C, :], eng="act")
            P.copy(g_wT[:, :, 0:C], pw[:, :].rearrange("p (h j) -> p h j", h=4)[:, :, 0:C])
            milestone("g6")
            yield
            pws = pstA()
            for h in range(4):
                P.mm(pws[0:C, h * 128:(h + 1) * 128], g_wT[:, h, 0:C], st.S[:, h, :], True, True)
            P.tt(g_vnew[0:C].rearrange("p h d -> p (h d)"), g_u[0:C].rearrange("p h d -> p (h d)"), pws[0:C, :], ALU.subtract)
            yield
            po1 = pstA(); po2 = pstA(); psn = pstA()
            for h in range(4):
                P.mm(po1[:, h * 128:h * 128 + C], st.S[:, h, :], QKV[:, h, c0:c0 + C], True, True)
                P.mm(po2[:, h * 128:h * 128 + C], g_vnew[0:C, h, :], g_qkT[0:C, h, 0:C], True, True)
                P.mm(psn[:, h * 128:(h + 1) * 128], g_kd[0:C, h, :], g_vnew[0:C, h, :], True, True)
            P.tt(g_t[:, :, 0:C], po1[:, :].rearrange("p (h j) -> p h j", h=4)[:, :, 0:C], g_expGb[:, :, 0:C], ALU.mult)
            P.tt(g_o[:, :, 0:C], g_t[:, :, 0:C], po2[:, :].rearrange("p (h j) -> p h j", h=4)[:, :, 0:C], ALU.add)
            for h in range(4):
                P.stt(st.S[:, h, :], st.S[:, h, :], g_gam[:, h:h + 1], psn[:, h * 128:(h + 1) * 128], ALU.mult, ALU.add)
            milestone("g7")
            yield
            P.act(g_t[:, :, 0:C], g_o[:, :, 0:C], AF.Square)
            pss = pstA()
            for h in range(4):
                P.mm(pss[:, h * 128:h * 128 + C], ones_f[:, :], g_t[:, h, 0:C], True, True)
            pssv = pss[:, :].rearrange("p (h j) -> p h j", h=4)
            P.act(g_t[:, :, 0:C], pssv[:, :, 0:C], AF.Copy, scale=1.0 / 128.0)
            P.ts(g_t[:, :, 0:C], g_t[:, :, 0:C], RMS_EPS, None, ALU.add)
            P.act(g_t[:, :, 0:C], g_t[:, :, 0:C], AF.Sqrt)
            P.recip(g_t[:, :, 0:C], g_t[:, :, 0:C])
            P.tt(g_o[:, :, 0:C], g_o[:, :, 0:C], g_t[:, :, 0:C], ALU.mult)
            P.ts(g_o[:, :, 0:C], g_o[:, :, 0:C], gdnng[l][:, 0:1], None, ALU.mult)
            P.tt(YB[:, :, c0:c0 + C], g_o[:, :, 0:C], Zs[:, :, c0:c0 + C], ALU.mult)

        def attention(st, nt, groups):
            for h in range(4):
                first = True
                nvis = sum((nk + 127) // 128 for (_, nk, _) in groups)
                vi = 0
                pend = None
                P.memset(accS[:, 0:nt], 0.0, eng="pool")

                def flush(pd):
                    vt_, r_, kk_, c0_, pt_, fi_, la_ = pd
                    P.mm(ps_o[:, c0_:nt], vt_[0:kk_, r_, :], pt_[0:kk_, c0_:nt], fi_, la_)
                    P.tt(accS[0:kk_, c0_:nt], accS[0:kk_, c0_:nt], pt_[0:kk_, c0_:nt], ALU.add, eng="pool")
                for (k0, nk, diag) in groups:
                    i = avi[0]; avi[0] += 1
                    kn = knt[i % 3]; kr = krt[i % 3]; vt = vts[i % 3]
                    P.dma(kn[:, 0:nk], st.KN[h, :, k0:k0 + nk], q="sp")
                    P.dma(kr[:, 0:nk], st.KR[:, k0:k0 + nk], q="sp")
                    nb_ = (nk + 127) // 128
                    if nk >= 128:
                        P.dma(vt[:, 0:nb_, :], st.VV[h, k0:k0 + nk, :].rearrange("(b p) d -> p b d", p=128), q="sp")
                    else:
                        P.dma(vt[0:nk, 0, :], st.VV[h, k0:k0 + nk, :], q="sp")
                    for r in range(nb_):
                        kk = min(128, nk - 128 * r)
                        c0 = 128 * r if diag else 0
                        if c0 >= nt:
                            vi += 1
                            continue
                        sc = pstB()
                        P.mm(sc[0:kk, c0:nt], kn[:, 128 * r:128 * r + kk], Qn[:, h, c0:nt], True, False)
                        P.mm(sc[0:kk, c0:nt], kr[0:65, 128 * r:128 * r + kk], QR[0:65, h, c0:nt], False, True)
                        pt = pts[vi % 3]
                        P.act(pt[0:kk, c0:nt], sc[0:kk, c0:nt], AF.Exp)
                        if diag and kk > 64:
                            P.memset(pt[64:kk, c0:min(nt, c0 + 64)], 0.0, eng="pool")
                        last = (vi == nvis - 1)
                        if pend is not None:
                            flush(pend)
                        pend = (vt, r, kk, c0, pt, first, last)
                        first = False
                        vi += 1
                        yield
                if pend is not None:
                    flush(pend)
                P.mm(ps_s[:, 0:nt], ones_f[:, :], accS[:, 0:nt], True, True)
                rs = tmp()
                P.recip(rs[:, 0:nt], ps_s[:, 0:nt])
                P.tt(YC[:, h, 0:nt], ps_o[:, 0:nt], rs[:, 0:nt], ALU.mult)

        def interleave(gens):
            gens = list(gens)
            while gens:
                for gi in list(gens):
                    try:
                        next(gi)
                    except StopIteration:
                        gens.remove(gi)

        def knorm(l, st, nt, k0):
            for h in range(4):
                pkh = lin("uk%d" % l, h, LATb, nt)
                P.copy(KNb[:, h, 0:nt], pkh, eng="act")
                P.act(KNf[:, 0:nt], pkh, AF.Square)
                pk2 = pst()
                P.mm(pk2[0:1, 0:nt], ones_f[:, 0:1], KNf[:, 0:nt], True, False)
                P.mm(pk2[0:1, 0:nt], ones_f[0:64, 0:1], SQ[0:64, 0, 0:nt], False, True)
                P.dma(st.KN[h, :, k0:k0 + nt], KNb[:, h, 0:nt], q="pool")
                P.emit("dve", lambda e, pk2=pk2, h=h, nt=nt: e.tensor_reduce(
                    out=_ap(rowk[0:1, h:h + 1]), in_=_ap(pk2[0:1, 0:nt]), op=ALU.max, axis=mybir.AxisListType.X),
                    [pk2], [rowk])
            P.tt(st.Kb2[0:1, :], st.Kb2[0:1, :], rowk[0:1, :], ALU.max)

        def block_layer(l, st, nt, g, pos_rope, t0, cachek0, groups, outs, Cg):
            for kc in range(8):
                P.act(Hb[:, kc, 0:nt], X[:, kc, 0:nt], AF.Identity, bias=modT[l][:, kc, g:g + 1], scale=scp1[l][:, kc, g:g + 1])
            P.copy(U[:, :, 0:30], st.Uh[:, :, :])
            for j in range(4):
                pv = lin("inA%d" % l, j, Hb, nt)
                pg = lin("inA%d" % l, 4 + j, Hb, nt)
                sg = tmp()
                P.act(sg[:, 0:nt], pg, AF.Sigmoid)
                P.tt(U[:, j, 30:30 + nt], pv, sg[:, 0:nt], ALU.mult)
            P.copy(st.Uh[:, :, :], U[:, :, nt:nt + 30])
            milestone("A")
            P.copy(QKVP[:, :, 0:3], st.Qh[:, :, :], eng="pool")
            for j in range(12):
                pq = lin("inA%d" % l, 8 + j, Hb, nt)
                P.copy(QKVP[:, j, 3:3 + nt], pq, eng=("act" if j % 2 else "dve"))
            P.copy(st.Qh[:, :, :], QKVP[:, :, nt:nt + 3], eng="pool")
            for j in range(4):
                pz = lin("inA%d" % l, 20 + j, Hb, nt)
                P.act(Zs[:, j, 0:nt], pz, AF.Silu)
            pb = lin("inbeta%d" % l, 0, Hb, nt)
            P.act(BE[0:4, 0:nt], pb, AF.Sigmoid)
            pd = lin("indec%d" % l, 0, Hb, nt)
            P.act(GG[0:4, 0:nt], pd, AF.Exp, bias=gdns[l][:, 1:2])
            P.act(GG[0:4, 0:nt], GG[0:4, 0:nt], AF.Ln, bias=1.0)
            P.ts(GG[0:4, 0:nt], GG[0:4, 0:nt], nexpA[l][:, 0:1], None, ALU.mult)
            milestone("B")
            for j in range(3):
                pq = lin("inql%d" % l, j, Hb, nt)
                P.copy(QLf[:, j, 0:nt], pq, eng=("act" if j % 2 else "dve"))
            _, rstd = colstats(QLf, 3, nt, False)
            P.tt(QLf[:, :, 0:nt], QLf[:, :, 0:nt], rstd[:, 0:nt].unsqueeze(1).to_broadcast([128, 3, nt]), ALU.mult)
            P.tt(QLn[:, :, 0:nt], QLf[:, :, 0:nt], qng[l][:, :].bc(nt), ALU.mult)
            for j in range(2):
                pq = lin("inkvl%d" % l, j, Hb, nt)
                P.copy(LAT[:, j, 0:nt], pq, eng=("act" if j % 2 else "dve"))
            _, rstd = colstats(LAT, 2, nt, False)
            P.tt(LAT[:, :, 0:nt], LAT[:, :, 0:nt], rstd[:, 0:nt].unsqueeze(1).to_broadcast([128, 2, nt]), ALU.mult)
            P.tt(LAT[:, :, 0:nt], LAT[:, :, 0:nt], kvng[l][:, :].bc(nt), ALU.mult)
            P.copy(LATb[:, :, 0:nt], LAT[:, :, 0:nt], eng="pool")
            P.dma(outs["lat"][:, t0:t0 + nt].rearrange("(kc p) t -> p kc t", p=128), LAT[:, :, 0:nt], q="pool")
            P.dma(ropeT[:, :, 0:nt], pos_rope.rearrange("a d t -> d a t"))
            pk1 = lin("inkpe%d" % l, 0, Hb, nt)
            pk2 = lin("inkpp%d" % l, 0, Hb, nt)
            t1 = tmp(); t2 = tmp()
            P.tt(t1[0:64, 0:nt], pk1, ropeT[:, 0, 0:nt], ALU.mult)
            P.tt(t2[0:64, 0:nt], pk2, ropeT[:, 1, 0:nt], ALU.mult)
            P.tt(KPE[:, 0:nt], t1[0:64, 0:nt], t2[0:64, 0:nt], ALU.add)
            P.dma(outs["kpe"][:, t0:t0 + nt], KPE[:, 0:nt], q="pool")
            P.copy(KPEb[0:64, 0:nt], KPE[:, 0:nt], eng="pool")
            P.memset(KPEb[64:65, 0:nt], 1.0, eng="pool")
            P.dma(st.KR[:, cachek0:cachek0 + nt], KPEb[:, 0:nt], q="pool")
            milestone("C")
            for j in range(24):
                pgt = lin("ingate%d" % l, j, Hb, nt)
                P.act(G3[:, j, 0:nt], pgt, AF.Sigmoid)
            milestone("gates")
            acc = SQ
            for ch in range(4):
                P.ts(acc[:, 4 + ch, 0:nt], U[:, ch, 0:nt], cvaw[l][:, ch, 0:1], None, ALU.mult)
                for k in range(1, 31):
                    P.stt(acc[:, 4 + ch, 0:nt], U[:, ch, k:k + nt], cvaw[l][:, ch, k:k + 1], acc[:, 4 + ch, 0:nt], ALU.mult, ALU.add)
                P.ts(acc[:, 4 + ch, 0:nt], acc[:, 4 + ch, 0:nt], cvav[l][:, 0, ch:ch + 1], None, ALU.add)
            P.copy(X1[:, 0:4, 0:nt], acc[:, 4:8, 0:nt], eng="pool")
            ln_fm(X1[:, 4:8, :], X1[:, 0:4, :], 4, nt, cvav[l][:, 1, :], cvav[l][:, 2, :])
            P.act(YA[:, :, 0:nt], X1[:, 4:8, 0:nt], AF.Silu)
            milestone("convA")
            for j in range(12):
                P.ts(QKV[:, j, 0:nt], QKVP[:, j, 0:nt], gcvw[l][:, j, 0:1], None, ALU.mult)
                for k in range(1, 4):
                    P.stt(QKV[:, j, 0:nt], QKVP[:, j, k:k + nt], gcvw[l][:, j, k:k + 1], QKV[:, j, 0:nt], ALU.mult, ALU.add)
            P.act(QKV[:, :, 0:nt], QKV[:, :, 0:nt], AF.Silu)
            P.act(SQ[:, 0:8, 0:nt], QKV[:, 0:8, 0:nt], AF.Square)
            for j in range(8):
                pn = pst()
                P.mm(pn[:, 0:nt], ones_f[:, :], SQ[:, j, 0:nt], True, True)
                rn = tmp()
                P.ts(rn[:, 0:nt], pn[:, 0:nt], RMS_EPS, None, ALU.add)
                P.act(rn[:, 0:nt], rn[:, 0:nt], AF.Sqrt)
                P.recip(rn[:, 0:nt], rn[:, 0:nt])
                if j < 4:
                    P.stt(QKV[:, j, 0:nt], QKV[:, j, 0:nt], 128 ** -0.5, rn[:, 0:nt], ALU.mult, ALU.mult)
                else:
                    P.tt(QKV[:, j, 0:nt], QKV[:, j, 0:nt], rn[:, 0:nt], ALU.mult)
            milestone("gdnprep")
            milestone("gdn")
            for h in range(4):
                pqn = lin("uqn%d" % l, h, QLn, nt)
                P.act(QNf[:, h, 0:nt], pqn, AF.Copy, scale=SCALE_Q)
                pqr = lin("uqr%d" % l, h, QLn, nt)
                pqp = lin("uqp%d" % l, h, QLn, nt)
                t1 = tmp(); t2 = tmp()
                P.tt(t1[0:64, 0:nt], pqr, ropeT[:, 0, 0:nt], ALU.mult)
                P.tt(t2[0:64, 0:nt], pqp, ropeT[:, 1, 0:nt], ALU.mult)
                P.stt(QRf[:, h, 0:nt], t1[0:64, 0:nt], 1.0, t2[0:64, 0:nt], ALU.mult, ALU.add)
            P.ts(QRf[:, :, 0:nt], QRf[:, :, 0:nt], SCALE_Q, None, ALU.mult)
            P.copy(Qn[:, :, 0:nt], QNf[:, :, 0:nt], eng="pool")
            P.copy(QR[0:64, :, 0:nt], QRf[:, :, 0:nt], eng="pool")
            P.act(SQ[0:64, 0, 0:nt], KPE[:, 0:nt], AF.Square)
            knorm(l, st, nt, cachek0)
            nsb = (nt + 127) // 128
            for sbk in range(nsb):
                kk = min(128, nt - 128 * sbk)
                pvv = pst()
                for kc in range(2):
                    P.mm(pvv[0:kk, :], LATb[:, kc, 128 * sbk:128 * sbk + kk], wuv[l][:, kc, :], kc == 0, kc == 1)
                P.copy(VVb[0:kk, 4 * sbk:4 * sbk + 4, :].rearrange("p h d -> p (h d)"), pvv[0:kk, :], eng="act")
                for h in range(4):
                    P.dma(st.VV[h, cachek0 + 128 * sbk:cachek0 + 128 * sbk + kk, :], VVb[0:kk, 4 * sbk + h, :],
                          q=("sp", "pool")[h % 2])
            P.act(SQ[:, 0:4, 0:nt], QNf[:, :, 0:nt], AF.Square)
            P.act(SQ[0:64, 4:8, 0:nt], QRf[:, :, 0:nt], AF.Square)
            for h in range(4):
                pq2 = pst()
                P.mm(pq2[0:1, 0:nt], ones_f[:, 0:1], SQ[:, h, 0:nt], True, False)
                P.mm(pq2[0:1, 0:nt], ones_f[0:64, 0:1], SQ[0:64, 4 + h, 0:nt], False, True)
                tq = tmp()
                P.ts(tq[0:1, 0:nt], pq2[0:1, 0:nt], st.Kb2[0:1, h:h + 1], None, ALU.mult)
                P.act(tq[0:1, 0:nt], tq[0:1, 0:nt], AF.Sqrt)
                rb_ = rowbf[h % 2]
                P.ts(rb_[0:1, 0:nt], tq[0:1, 0:nt], -1.0, None, ALU.mult)
                P.dma(QR[64:65, h, 0:nt], rb_[0:1, 0:nt])
            milestone("mla")

            def gdn_all():
                for c0 in range(0, nt, Cg):
                    yield from gdn_chunk(l, st, c0, min(Cg, nt - c0))
            interleave([gdn_all(), attention(st, nt, groups)])
            milestone("attn")
            for m in range(8):
                ta = tmp(); tb_ = tmp()
                for n, Y in enumerate((YA, YB, YC)):
                    pm = lin("br%d_%d" % (l, n), m, Y, nt)
                    if n == 0:
                        P.tt(ta[:, 0:nt], pm, G3[:, 0 * 8 + m, 0:nt], ALU.mult)
                    elif n == 1:
                        P.tt(tb_[:, 0:nt], pm, G3[:, 1 * 8 + m, 0:nt], ALU.mult)
                        P.tt(ta[:, 0:nt], ta[:, 0:nt], tb_[:, 0:nt], ALU.add, eng="pool")
                    else:
                        P.tt(tb_[:, 0:nt], pm, G3[:, 2 * 8 + m, 0:nt], ALU.mult)
                        P.tt(Mg[:, m, 0:nt], ta[:, 0:nt], tb_[:, 0:nt], ALU.add, eng="pool")
            milestone("merge")
            for m in range(8):
                po = lin("out%d" % l, m, Mg, nt)
                tr_ = tmp()
                P.ts(tr_[:, 0:nt], po, g1p[l][:, m, g:g + 1], None, ALU.mult)
                P.stt(X[:, m, 0:nt], X[:, m, 0:nt], ALU_ALPHA, tr_[:, 0:nt], ALU.mult, ALU.add)
            ln_fm(X1, X, 8, nt, lnvs[l][:, 0, :], lnvs[l][:, 1, :])
            milestone("ln1")
            for kc in range(8):
                P.act(Hb[:, kc, 0:nt], X1[:, kc, 0:nt], AF.Identity, bias=modT[l][:, 24 + kc, g:g + 1], scale=scp2[l][:, kc, g:g + 1])
            for j in range(KC_F):
                aj = Aj[j % 2]
                P.copy(aj[:, 0:2], st.Ah[:, j, :], eng="pool")
                pa = lin("up%d" % l, j, Hb, nt)
                P.copy(aj[:, 2:2 + nt], pa, eng="act")
                P.copy(st.Ah[:, j, :], aj[:, nt:nt + 2], eng="pool")
                pvv = lin("up%d" % l, KC_F + j, Hb, nt)
                ca = tmp()
                P.ts(ca[:, 0:nt], aj[:, 0:nt], ffnw[l][:, j, 0:1], None, ALU.mult)
                P.stt(ca[:, 0:nt], aj[:, 1:1 + nt], ffnw[l][:, j, 1:2], ca[:, 0:nt], ALU.mult, ALU.add)
                P.stt(ca[:, 0:nt], aj[:, 2:2 + nt], ffnw[l][:, j, 2:3], ca[:, 0:nt], ALU.mult, ALU.add)
                P.act(ca[:, 0:nt], ca[:, 0:nt], AF.Silu, bias=ffnb[l][:, j:j + 1])
                P.tt(Gf[:, j, 0:nt], ca[:, 0:nt], pvv, ALU.mult)
            for m in range(8):
                py = lin([("dna%d" % l, 0), ("dnb%d" % l, 8), ("dnc%d" % l, 16)], m, Gf, nt)
                tr_ = tmp()
                P.ts(tr_[:, 0:nt], py, g2p[l][:, m, g:g + 1], None, ALU.mult)
                P.stt(X1[:, m, 0:nt], X1[:, m, 0:nt], ALU_ALPHA, tr_[:, 0:nt], ALU.mult, ALU.add)
            ln_fm(X, X1, 8, nt, lnvs[l][:, 2, :], lnvs[l][:, 3, :])

        ALU_ALPHA = float(ALPHA)
        rowbf = [P.sb([1, TB], BF16) for _ in range(2)]

        def init_state_zero(st):
            P.memset(st.Uh[:], 0.0); P.memset(st.Qh[:], 0.0, eng="pool"); P.memset(st.S[:], 0.0)
            P.memset(st.Ah[:], 0.0, eng="pool"); P.memset(st.Kb2[:], 0.0)

        def write_states(l, st, nt, o_ca, o_gc, o_g, o_f):
            P.dma(o_ca.rearrange("(kc p) t -> p kc t", p=128), st.Uh[:, :, :])
            P.dma(o_gc.rearrange("(kc p) t -> p kc t", p=128), st.Qh[:, :, :])
            P.dma(o_g.rearrange("h k v -> k h v"), st.S[:, :, :])
            P.dma(o_f.rearrange("(kc p) t -> p kc t", p=128), st.Ah[:, :, :])

        for l in range(L):
            init_state_zero(stP[l])
        for b in range(NB):
            t0 = b * TB
            P.dma(X[:, :, 0:TB], xpT[:, t0:t0 + TB].rearrange("(kc p) t -> p kc t", p=128))
            ln_fm(X1, X, 8, TB, ln0s[:, 0, :], ln0s[:, 1, :])
            P.copy(X[:, :, 0:TB], X1[:, :, 0:TB], eng="pool")
            for l in range(L):
                groups = [(gb * TB, TB, gb == b) for gb in range(b + 1)]
                block_layer(l, stP[l], TB, 0, ropeP[:, :, t0:t0 + TB], t0, t0, groups,
                            {"lat": o_platT[l], "kpe": o_pkpeT[l]}, min(128, TB))
                if b == 0 and l == 0:
                    dbg("x_l0b0", X[:, :, 0:TB], [128, 8, TB])
            P.dma(ypT[:, t0:t0 + TB].rearrange("(kc p) t -> p kc t", p=128), X[:, :, 0:TB])
        for l in range(L):
            write_states(l, stP[l], TB, o_pcaT[l], o_pgcT[l], o_pg[l], o_pfT[l])

        for s in range(NS):
            P.dma(X[:, :, 0:TS], xsT[:, s * TS:(s + 1) * TS].rearrange("(kc p) t -> p kc t", p=128))
            ln_fm(X1, X, 8, TS, ln0s[:, 0, :], ln0s[:, 1, :])
            P.copy(X[:, :, 0:TS], X1[:, :, 0:TS], eng="pool")
            for l in range(L):
                st = stS
                P.dma(st.Uh[:, :, :], hA[l, s].rearrange("(kc p) t -> p kc t", p=128))
                P.dma(st.Qh[:, :, :], hB[l, s].rearrange("(kc p) t -> p kc t", p=128))
                P.dma(st.S[:, :, :], sG[l, s].rearrange("h k v -> k h v"))
                P.dma(st.Ah[:, :, :], hF[l, s].rearrange("(kc p) t -> p kc t", p=128))
                P.memset(st.Kb2[:], 0.0)
                for k0 in range(0, PAST, TB):
                    nk = min(TB, PAST - k0)
                    P.dma(LAT[:, :, 0:nk], clatT[l, s][:, k0:k0 + nk].rearrange("(kc p) t -> p kc t", p=128))
                    P.copy(LATb[:, :, 0:nk], LAT[:, :, 0:nk])
                    P.dma(KPE[:, 0:nk], ckpeT[l, s][:, k0:k0 + nk])
                    P.copy(KPEb[0:64, 0:nk], KPE[:, 0:nk], eng="pool")
                    P.memset(KPEb[64:65, 0:nk], 1.0, eng="pool")
                    P.dma(st.KR[:, k0:k0 + nk], KPEb[:, 0:nk])
                    P.act(SQ[0:64, 0, 0:nk], KPE[:, 0:nk], AF.Square)
                    knorm(l, st, nk, k0)
                    for sbk in range((nk + 127) // 128):
                        kk = min(128, nk - 128 * sbk)
                        pvv = pst()
                        for kc in range(2):
                            P.mm(pvv[0:kk, :], LATb[:, kc, 128 * sbk:128 * sbk + kk], wuv[l][:, kc, :], kc == 0, kc == 1)
                        P.copy(VVb[0:kk, 4 * sbk:4 * sbk + 4, :].rearrange("p h d -> p (h d)"), pvv[0:kk, :], eng="act")
                        for h in range(4):
                            P.dma(st.VV[h, k0 + 128 * sbk:k0 + 128 * sbk + kk, :], VVb[0:kk, 4 * sbk + h, :],
                                  q="pool")
                groups = [(k0, min(512, PAST - k0), False) for k0 in range(0, PAST, 512)] + [(PAST, TS, False)]
                block_layer(l, st, TS, 1 + s, ropeS[:, :, :], 0, PAST, groups,
                            {"lat": o_slatT[l, s], "kpe": o_skpeT[l, s]}, TS)
                write_states(l, st, TS, o_scaT[l, s], o_sgcT[l, s], o_sg[l, s], o_sfT[l, s])
            P.dma(ysT[:, s * TS:(s + 1) * TS].rearrange("(kc p) t -> p kc t", p=128), X[:, :, 0:TS])


    try:
        main_body()
    except StopBuild:
        pass
    P.wait_all("sp")
    P.finalize()
    return nc, sorted(dbg_out.keys())


import numpy as np
from concourse.bass_utils import run_bass_kernel_spmd

NCORES = 8
FULL_CFG = dict(SEQ=16384, NS=4, PAST=2048, TB=256, L=2)


def _pp(vec, n):
    return np.ascontiguousarray(vec.reshape(n, 128).T)


def _rope_tables(pos):
    half = 32
    inv_freq = np.power(np.float32(10000.0), -(np.arange(half, dtype=np.float32) / np.float32(half))).astype(np.float32)
    ang = pos.astype(np.float32)[:, None] * inv_freq[None, :]
    cos = np.cos(ang).astype(np.float32); sin = np.sin(ang).astype(np.float32)
    cosT = np.concatenate([cos, cos], axis=1).T
    sinT = np.concatenate([-sin, sin], axis=1).T
    return np.ascontiguousarray(np.stack([cosT, sinT], axis=0))


def prep_inputs(inp, cfg):
    SEQ, NS, PAST, L = cfg["SEQ"], cfg["NS"], cfg["PAST"], cfg["L"]
    f = lambda a: np.ascontiguousarray(np.asarray(a, dtype=np.float32))
    perm = np.concatenate([np.arange(32, 64), np.arange(0, 32)])
    sh = {}
    sh["xpT"] = f(np.asarray(inp["x_prompt"])[0].T)
    sh["ln0p"] = f(np.stack([_pp(np.asarray(inp["ln0_g"]), 8), _pp(np.asarray(inp["ln0_b"]), 8)], axis=1))
    sh["w_ada"] = f(inp["w_ada"])
    sh["b_ada_p"] = f(np.stack([_pp(np.asarray(inp["b_ada"])[l], 48) for l in range(L)]))
    w_in = np.asarray(inp["w_in"])
    sh["w_in"] = f(w_in)
    sh["w_kpp"] = f(w_in[:, :, O_KPE + perm])
    caw = np.asarray(inp["conv_a_w"])
    sh["cva_w"] = f(np.stack([caw[l].T.reshape(4, 128, 31).transpose(1, 0, 2) for l in range(L)]))
    sh["cva_v"] = f(np.stack([np.stack([_pp(np.asarray(inp[k])[l], 4) for k in ("conv_a_b", "ln_a_g", "ln_a_b")], axis=1)
                              for l in range(L)]))
    gcw = np.asarray(inp["gdn_conv_w"])
    sh["gcv_w"] = f(np.stack([gcw[l].T.reshape(12, 128, 4).transpose(1, 0, 2) for l in range(L)]))
    sh["gdn_s"] = f(np.stack([np.asarray(inp["gdn_a_log"]), np.asarray(inp["gdn_dt_bias"])], axis=-1))
    sh["gdn_ng"] = f(np.asarray(inp["gdn_norm_g"])[:, :, None])
    sh["qn_g"] = f(np.stack([_pp(np.asarray(inp["mla_q_norm_g"])[l], 3) for l in range(L)]))
    sh["kvn_g"] = f(np.stack([_pp(np.asarray(inp["mla_kv_norm_g"])[l], 2) for l in range(L)]))
    wuq = np.asarray(inp["mla_w_uq"]).reshape(L, 384, 4, 192)
    sh["w_uqn"] = f(wuq[:, :, :, :128].reshape(L, 384, 512))
    sh["w_uqr"] = f(wuq[:, :, :, 128:].reshape(L, 384, 256))
    sh["w_uqp"] = f(wuq[:, :, :, 128 + perm].reshape(L, 384, 256))
    wukv = np.asarray(inp["mla_w_ukv"]).reshape(L, 256, 4, 256)
    sh["w_uk"] = f(wukv[:, :, :, :128].reshape(L, 256, 512))
    sh["w_uv"] = f(wukv[:, :, :, 128:].reshape(L, 256, 512))
    sh["w_br"] = f(inp["w_branch"]); sh["w_out"] = f(inp["w_out"])
    sh["lnv"] = f(np.stack([np.stack([_pp(np.asarray(inp[k])[l], 8) for k in ("ln1_g", "ln1_b", "ln2_g", "ln2_b")], axis=1)
                            for l in range(L)]))
    sh["w_up"] = f(inp["w_up"])
    fcw = np.asarray(inp["ffn_conv_w"])
    sh["ffn_w"] = f(np.stack([fcw[l].T.reshape(KC_F, 128, 3).transpose(1, 0, 2) for l in range(L)]))
    sh["ffn_b"] = f(np.stack([_pp(np.asarray(inp["ffn_conv_b"])[l], KC_F) for l in range(L)]))
    sh["w_dn"] = f(inp["w_down"])
    sh["c_ident"] = np.eye(128, dtype=np.float32)
    pi = np.arange(128)[:, None]; fi = np.arange(128)[None, :]
    sh["c_triu"] = (pi <= fi).astype(np.float32)
    sh["c_mtri"] = (pi >= fi).astype(np.float32)
    sh["c_mtriT"] = (pi <= fi).astype(np.float32)
    sh["c_mstr"] = (pi > fi).astype(np.float32)
    sh["ropeP"] = _rope_tables(np.arange(SEQ))
    sh["ropeS"] = _rope_tables(PAST + np.arange(16))
    xs = np.asarray(inp["x_sample"]); cs = np.asarray(inp["c_sample"]); cp = np.asarray(inp["c_prompt"])
    in_maps = []
    for c in range(NCORES):
        sl = slice(c * NS, (c + 1) * NS)
        m = dict(sh)
        m["xsT"] = f(xs[sl].reshape(NS * 16, D).T)
        m["cT"] = f(np.concatenate([cp[0:1], cs[sl]], axis=0).T)
        m["clatT"] = f(np.asarray(inp["cache_mla_latent"])[:, sl].transpose(0, 1, 3, 2))
        m["ckpeT"] = f(np.asarray(inp["cache_mla_kpe"])[:, sl].transpose(0, 1, 3, 2))
        m["hA"] = f(np.asarray(inp["state_conv_a"])[:, sl].transpose(0, 1, 3, 2))
        m["hB"] = f(np.asarray(inp["state_gdn_conv"])[:, sl].transpose(0, 1, 3, 2))
        m["sG"] = f(np.asarray(inp["state_gdn"])[:, sl])
        m["hF"] = f(np.asarray(inp["state_ffn_conv"])[:, sl].transpose(0, 1, 3, 2))
        in_maps.append(m)
    return in_maps


def assemble(res, cfg):
    NS = cfg["NS"]
    r0 = res[0]
    T = lambda a: np.ascontiguousarray(np.swapaxes(a, -1, -2))
    y_p = T(r0["ypT"])[None]
    y_s = np.concatenate([T(r["ysT"]).reshape(NS, 16, D) for r in res], axis=0)
    p_lat = T(r0["o_platT"])[:, None]; p_kpe = T(r0["o_pkpeT"])[:, None]
    p_ca = T(r0["o_pcaT"])[:, None]; p_gc = T(r0["o_pgcT"])[:, None]; p_g = r0["o_pg"][:, None]; p_f = T(r0["o_pfT"])[:, None]
    cat = lambda k, tr: np.concatenate([(T(r[k]) if tr else r[k]) for r in res], axis=1)
    outs = (y_p, y_s, p_lat, p_kpe, p_ca, p_gc, p_g, p_f,
            cat("o_slatT", True), cat("o_skpeT", True), cat("o_scaT", True), cat("o_sgcT", True), cat("o_sg", False),
            cat("o_sfT", True))
    return tuple(np.ascontiguousarray(o, dtype=np.float32) for o in outs)


def run(inputs, cfg, debug=()):
    nc, dbg_names = build_program(cfg, debug)
    in_maps = prep_inputs(inputs, cfg)
    res = run_bass_kernel_spmd(nc, in_maps, core_ids=list(range(NCORES)))
    return assemble(res.results, cfg), res.results


def kernel(**inputs):
    outs, _ = run(inputs, FULL_CFG)
    return outs
```

```python
from contextlib import ExitStack
import numpy as np
import concourse.bass as bass
import concourse.mybir as mybir

F32 = mybir.dt.float32
BF16 = mybir.dt.bfloat16
I32 = mybir.dt.int32
AF = mybir.ActivationFunctionType
ALU = mybir.AluOpType

ENGS = ("pe", "act", "dve", "pool", "sp")
NDMA = 12


class Trk:
    __slots__ = ("w", "r", "name", "multi", "serial")

    def __init__(self, name="", multi=False):
        self.w = {}
        self.r = {}
        self.name = name
        self.multi = multi
        self.serial = False


class T:
    def __init__(self, handle, name):
        self.h = handle
        self.k = Trk(name)
        self.shape = tuple(handle.shape)

    def __getitem__(self, key):
        return V(self.h[key] if not isinstance(key, tuple) or True else None, [self.k])

    def ap(self):
        return V(self.h.ap() if hasattr(self.h, "ap") else self.h[:], [self.k])


class V:
    def __init__(self, ap, ks):
        self.ap = ap
        self.ks = ks

    def __getitem__(self, key):
        return V(self.ap[key], self.ks)

    def rearrange(self, s, **kw):
        return V(self.ap.rearrange(s, **kw), self.ks)

    def bitcast(self, dt):
        return V(self.ap.bitcast(dt), self.ks)

    def broadcast_to(self, shape):
        return V(self.ap.broadcast_to(shape), self.ks)

    def to_broadcast(self, shape):
        return V(self.ap.to_broadcast(shape), self.ks)

    def partition_broadcast(self, n):
        return V(self.ap.partition_broadcast(n), self.ks)

    def unsqueeze(self, a):
        return V(self.ap.unsqueeze(a), self.ks)

    def bc(self, n):
        sh = list(self.ap.shape)
        return V(self.ap.unsqueeze(len(sh)).to_broadcast(sh + [n]), self.ks)

    @property
    def shape(self):
        return tuple(self.ap.shape)


def _ap(x):
    return x.ap if isinstance(x, V) else x


class Prog:
    def __init__(self, nc):
        self.nc = nc
        self.es = ExitStack()
        self.q = {e: [] for e in ENGS}
        self.cnt = {e: 0 for e in ENGS}
        self.seen = {e: {} for e in ENGS}
        self.sem = {}
        for e in ENGS:
            self.sem[("c", e)] = self.es.enter_context(nc.semaphore("s_" + e))
        self.dtot = {}
        for qn in ("sp", "pool"):
            for i in range(NDMA):
                k = ("d", qn, i)
                self.sem[k] = self.es.enter_context(nc.semaphore("d_%s%d" % (qn, i)))
                self.dtot[k] = 0
        self.drr = {"sp": 0, "pool": 0}
        self.ntile = 0
        self.psum_free = None

    def sb(self, shape, dt=F32, name=None, multi=False):
        self.ntile += 1
        name = name or "t%d" % self.ntile
        h = self.es.enter_context(self.nc.sbuf_tensor(name, list(shape), dt))
        t = T(h, name)
        t.k.multi = multi
        return t

    def ps(self, shape, dt=F32, name=None):
        self.ntile += 1
        name = name or "p%d" % self.ntile
        h = self.es.enter_context(self.nc.psum_tensor(name, list(shape), dt))
        t = T(h, name)
        t.k.serial = True
        return t

    def dram(self, name, shape, dt=F32, kind=None, multi=False):
        if kind is None:
            h = self.nc.dram_tensor(name, list(shape), dt)
        else:
            h = self.nc.dram_tensor(name, list(shape), dt, kind=kind)
        t = T(h, name)
        t.k.multi = multi
        return t

    def emit(self, eng, fn, reads, writes, dma=False, accum_pe=False):
        rk, wk = [], []
        for x in reads:
            if x is None:
                continue
            rk.extend(x.ks if isinstance(x, V) else [x.k])
        for x in writes:
            rk_ = x.ks if isinstance(x, V) else [x.k]
            wk.extend(rk_)
        deps = {}

        def need(ev):
            if ev is None:
                return
            k, v = ev
            if deps.get(k, 0) < v:
                deps[k] = v

        myk = ("c", eng)
        for k in rk:
            for sk, v in k.w.items():
                need((sk, v))
            if k.serial:
                for sk, v in k.r.items():
                    if sk != myk:
                        need((sk, v))
        for k in wk:
            if not k.multi:
                for sk, v in k.w.items():
                    if accum_pe and sk == myk:
                        continue
                    need((sk, v))
            for sk, v in k.r.items():
                if sk == myk and not dma:
                    continue
                need((sk, v))
        if dma:
            i = self.drr[eng]
            self.drr[eng] = (i + 1) % NDMA
            dk = ("d", eng, i)
            if self.dtot[dk] > 0:
                need((dk, self.dtot[dk]))
            self.dtot[dk] += 16
            ev = (dk, self.dtot[dk])
            inc = (self.sem[dk], 16)
        else:
            self.cnt[eng] += 1
            ev = (myk, self.cnt[eng])
            inc = (self.sem[myk], 1)
        waits = []
        seen = self.seen[eng]
        for k, v in deps.items():
            if seen.get(k, 0) >= v:
                continue
            seen[k] = v
            waits.append((self.sem[k], v))
        self.q[eng].append((waits, fn, inc))
        for k in wk:
            if k.multi:
                if k.w.get(ev[0], 0) < ev[1]:
                    k.w[ev[0]] = ev[1]
            else:
                k.w = {ev[0]: ev[1]}
                k.r = {}
        for k in rk:
            if k in wk:
                continue
            if k.r.get(ev[0], 0) < ev[1]:
                k.r[ev[0]] = ev[1]
        return ev

    def wait_all(self, eng="sp"):
        waits = []
        for e in ENGS:
            if self.cnt[e] > 0:
                waits.append((self.sem[("c", e)], self.cnt[e]))
        for k, v in self.dtot.items():
            if v > 0:
                waits.append((self.sem[k], v))
        self.q[eng].append((waits, None, None))

    def finalize(self):
        nc = self.nc
        q = self.q

        def run(engname):
            def body(eng):
                for waits, fn, inc in q[engname]:
                    for s, v in waits:
                        eng.wait_ge(s, v)
                    if fn is not None:
                        ins = fn(eng)
                        ins.then_inc(inc[0], inc[1])
            return body

        with nc.Block() as block:
            block.tensor(run("pe"))
            block.scalar(run("act"))
            block.vector(run("dve"))
            block.gpsimd(run("pool"))
            block.sync(run("sp"))
        self.es.close()

    def dma(self, out, in_, q="sp", **kw):
        return self.emit(q, lambda e: e.dma_start(out=_ap(out), in_=_ap(in_), **kw), [in_], [out], dma=True)

    def mm(self, out, lhsT, rhs, start=True, stop=True):
        return self.emit("pe", lambda e: e.matmul(_ap(out), _ap(lhsT), _ap(rhs), start=start, stop=stop),
                         [lhsT, rhs], [out], accum_pe=True)

    def tr(self, out, in_, ident):
        return self.emit("pe", lambda e: e.transpose(_ap(out), _ap(in_), _ap(ident)), [in_, ident], [out],
                         accum_pe=True)

    def act(self, out, in_, func, bias=None, scale=None, eng="act", accum_out=None):
        kw = {}
        rd = [in_]
        if bias is not None:
            kw["bias"] = _ap(bias)
            if isinstance(bias, (V, T)):
                rd.append(bias)
        if scale is not None:
            kw["scale"] = _ap(scale)
            if isinstance(scale, (V, T)):
                rd.append(scale)
        wr = [out]
        if accum_out is not None:
            kw["accum_out"] = _ap(accum_out)
            wr.append(accum_out)
        return self.emit("act", lambda e: e.activation(_ap(out), _ap(in_), func, **kw), rd, wr)

    def tt(self, out, a, b, op, eng="dve"):
        return self.emit(eng, lambda e: e.tensor_tensor(_ap(out), _ap(a), _ap(b), op), [a, b], [out])

    def ts(self, out, a, s1, s2=None, op0=ALU.mult, op1=None, eng="dve"):
        rd = [a] + [s for s in (s1, s2) if isinstance(s, (V, T))]
        if op1 is None:
            return self.emit(eng, lambda e: e.tensor_scalar(_ap(out), _ap(a), _ap(s1), None, op0), rd, [out])
        return self.emit(eng, lambda e: e.tensor_scalar(_ap(out), _ap(a), _ap(s1), _ap(s2), op0, op1), rd, [out])

    def stt(self, out, a, s, b, op0, op1):
        rd = [a, b] + ([s] if isinstance(s, (V, T)) else [])
        return self.emit("dve", lambda e: e.scalar_tensor_tensor(_ap(out), _ap(a), _ap(s), _ap(b), op0, op1),
                         rd, [out])

    def copy(self, out, in_, eng="dve"):
        if eng == "act":
            return self.act(out, in_, AF.Copy)
        return self.emit(eng, lambda e: e.tensor_copy(_ap(out), _ap(in_)), [in_], [out])

    def memset(self, out, val, eng="dve"):
        return self.emit(eng, lambda e: e.memset(_ap(out), val), [], [out])

    def recip(self, out, in_):
        return self.emit("dve", lambda e: e.reciprocal(_ap(out), _ap(in_)), [in_], [out])


import math
import numpy as np

D = 1024
KC_D = 8
IN_W = 6856
NH = 4
DFF = 2816
KC_F = 22
O_AV, O_AG, O_Q, O_Z, O_BETA, O_DEC, O_QL, O_KVL, O_KPE, O_GATE = 0, 512, 1024, 2560, 3072, 3076, 3080, 3464, 3720, 3784
ALPHA = 4 ** 0.25
LN_EPS = 1e-5
RMS_EPS = 1e-6
SCALE_Q = 192 ** -0.5


def build_program(cfg, debug=()):
    SEQ, NS, PAST, TB, L = cfg["SEQ"], cfg["NS"], cfg["PAST"], cfg["TB"], cfg["L"]
    TS = 16
    NB = SEQ // TB
    G = 1 + NS
    nc = bass.Bass("TRN2", target_bir_lowering=False)
    P = Prog(nc)
    dbg_out = {}

    def din(name, shape, dt=F32):
        return P.dram(name, shape, dt, kind="ExternalInput")

    def dout(name, shape, dt=F32):
        return P.dram(name, shape, dt, kind="ExternalOutput")

    xpT = din("xpT", [D, SEQ]); xsT = din("xsT", [D, NS * TS]); cT = din("cT", [D, G])
    ln0p = din("ln0p", [128, 2, 8])
    w_ada = din("w_ada", [L, D, 6 * D]); b_ada_p = din("b_ada_p", [L, 128, 48])
    w_in = din("w_in", [L, D, IN_W]); w_kpp = din("w_kpp", [L, D, 64])
    cva_w = din("cva_w", [L, 128, 4, 31]); cva_v = din("cva_v", [L, 128, 3, 4])
    gcv_w = din("gcv_w", [L, 128, 12, 4]); gdn_s = din("gdn_s", [L, 4, 2]); gdn_ng = din("gdn_ng", [L, 128, 1])
    qn_g = din("qn_g", [L, 128, 3]); kvn_g = din("kvn_g", [L, 128, 2])
    w_uqn = din("w_uqn", [L, 384, 512]); w_uqr = din("w_uqr", [L, 384, 256]); w_uqp = din("w_uqp", [L, 384, 256])
    w_uk = din("w_uk", [L, 256, 512]); w_uv = din("w_uv", [L, 256, 512])
    w_br = din("w_br", [L, 3, 512, D]); w_out = din("w_out", [L, D, D])
    lnv = din("lnv", [L, 128, 4, 8])
    w_up = din("w_up", [L, D, 2 * DFF]); ffn_w = din("ffn_w", [L, 128, KC_F, 3]); ffn_b = din("ffn_b", [L, 128, KC_F])
    w_dn = din("w_dn", [L, DFF, D])
    clatT = din("clatT", [L, NS, 256, PAST]); ckpeT = din("ckpeT", [L, NS, 64, PAST])
    hA = din("hA", [L, NS, 512, 30]); hB = din("hB", [L, NS, 1536, 3]); sG = din("sG", [L, NS, 4, 128, 128])
    hF = din("hF", [L, NS, DFF, 2])
    c_ident = din("c_ident", [128, 128]); c_triu = din("c_triu", [128, 128])
    c_mtri = din("c_mtri", [128, 128]); c_mtriT = din("c_mtriT", [128, 128]); c_mstr = din("c_mstr", [128, 128])
    ropeP = din("ropeP", [2, 64, SEQ]); ropeS = din("ropeS", [2, 64, TS])
    ypT = dout("ypT", [D, SEQ]); ysT = dout("ysT", [D, NS * TS])
    o_platT = dout("o_platT", [L, 256, SEQ]); o_pkpeT = dout("o_pkpeT", [L, 64, SEQ])
    o_pcaT = dout("o_pcaT", [L, 512, 30]); o_pgcT = dout("o_pgcT", [L, 1536, 3]); o_pg = dout("o_pg", [L, 4, 128, 128])
    o_pfT = dout("o_pfT", [L, DFF, 2])
    o_slatT = dout("o_slatT", [L, NS, 256, TS]); o_skpeT = dout("o_skpeT", [L, NS, 64, TS])
    o_scaT = dout("o_scaT", [L, NS, 512, 30]); o_sgcT = dout("o_sgcT", [L, NS, 1536, 3]); o_sg = dout("o_sg", [L, NS, 4, 128, 128])
    o_sfT = dout("o_sfT", [L, NS, DFF, 2])

    class StopBuild(Exception):
        pass

    def milestone(name):
        if cfg.get("stop") == name:
            raise StopBuild()

    def dbg(name, v, shape, dt=F32):
        if name in debug and name not in dbg_out:
            t = dout("dbg_" + name, list(shape), dt)
            dbg_out[name] = t
            P.dma(t[:], v)

    def main_body():
        ident = P.sb([128, 128]); triu = P.sb([128, 128]); mtri = P.sb([128, 128]); mtriT = P.sb([128, 128]); mstr = P.sb([128, 128])
        ones_f = P.sb([128, 128]); ones_b = P.sb([128, 128], BF16)
        for t, s in ((ident, c_ident), (triu, c_triu), (mtri, c_mtri), (mtriT, c_mtriT), (mstr, c_mstr)):
            P.dma(t[:], s[:])
        P.memset(ones_f[:], 1.0); P.memset(ones_b[:], 1.0)
        def b4(t, C):
            return t[0:C, 0:C].unsqueeze(1).to_broadcast([C, 4, C])
        ln0s = P.sb([128, 2, 8]); P.dma(ln0s[:], ln0p[:])

        PSN = 6
        pspool = [P.ps([128, 512], name="psg%d" % i) for i in range(PSN)]
        ps_o = P.ps([128, 512], name="ps_o"); ps_s = P.ps([128, 512], name="ps_s")
        psi = [0]

        def pst():
            t = pspool[psi[0] % PSN]
            psi[0] += 1
            return t
        psa = [0]; psb = [0]

        def pstA():
            t = pspool[psa[0] % 3]
            psa[0] += 1
            return t

        def pstB():
            t = pspool[3 + psb[0] % 3]
            psb[0] += 1
            return t

        W = {}
        stg_f = []
        stg_b = []
        prep_i = [0]

        def prep_w(name, src, K, N, Mc=128):
            KC = K // 128
            NM = (N + Mc - 1) // Mc
            scr = P.dram("wc_" + name, [NM, 128, KC * Mc], BF16)
            for mi in range(NM):
                m0 = mi * Mc
                mc = min(Mc, N - m0)
                i = prep_i[0]; prep_i[0] += 1
                sf = stg_f[i % 2]; sbf = stg_b[i % 2]
                q = "sp" if i % 2 == 0 else "pool"
                P.dma(sf[:, 0:KC * mc].rearrange("p (kc m) -> p kc m", kc=KC),
                      src[:, m0:m0 + mc].rearrange("(kc p) m -> p kc m", p=128), q=q)
                ce = ("dve", "act")[i % 2]
                P.copy(sbf[:, 0:KC * mc], sf[:, 0:KC * mc], eng=ce)
                P.dma(scr[mi, :, 0:KC * mc], sbf[:, 0:KC * mc], q=q)
            W[name] = (scr, KC, Mc, N)

        wbufs = [P.sb([128, 8 * 128], BF16, name="wbuf%d" % i) for i in range(8)]
        wbi = [0]

        def load_w(name, mi):
            scr, KC, Mc, N = W[name]
            mc = min(Mc, N - mi * Mc)
            i = wbi[0]; wbi[0] += 1
            wb = wbufs[i % 8]
            P.dma(wb[:, 0:KC * mc], scr[mi, :, 0:KC * mc], q="sp")
            return wb[:, 0:KC * mc].rearrange("p (kc m) -> p kc m", kc=KC), KC, mc

        def lin(name, mi, rhs, nt):
            parts = name if isinstance(name, list) else [(name, 0)]
            ps = pst()
            np_ = len(parts)
            for pi, (nm, ko) in enumerate(parts):
                wv, KC, mc = load_w(nm, mi)
                for kc in range(KC):
                    P.mm(ps[0:mc, 0:nt], wv[:, kc, :], rhs[:, ko + kc, 0:nt], start=(pi == 0 and kc == 0),
                         stop=(pi == np_ - 1 and kc == KC - 1))
            return ps[0:mc, 0:nt]

        X = P.sb([128, 8, TB]); X1 = P.sb([128, 8, TB]); Hb = P.sb([128, 8, TB], BF16)
        SQ = P.sb([128, 8, TB])
        U = P.sb([128, 4, 30 + TB]); QKVP = P.sb([128, 12, 3 + TB]); QKV = P.sb([128, 12, TB])
        Zs = P.sb([128, 4, TB]); G3 = P.sb([128, 24, TB], BF16)
        BE = P.sb([4, TB]); GG = P.sb([4, TB])
        YA = P.sb([128, 4, TB], BF16); YB = P.sb([128, 4, TB], BF16); YC = P.sb([128, 4, TB], BF16)
        QLn = P.sb([128, 3, TB], BF16); QLf = P.sb([128, 3, TB])
        LAT = P.sb([128, 2, TB]); LATb = P.sb([128, 2, TB], BF16)
        KPE = P.sb([64, TB]); KPEb = P.sb([65, TB], BF16)
        Qn = P.sb([128, 4, TB], BF16); QR = P.sb([65, 4, TB], BF16); QNf = P.sb([128, 4, TB]); QRf = P.sb([64, 4, TB])
        KNf = P.sb([128, TB]); KNb = P.sb([128, 4, TB], BF16); VVb = P.sb([128, 4 * max(1, TB // 128), 128], BF16)
        Mg = P.sb([128, 8, TB], BF16); Gf = G3
        Aj = [P.sb([128, 2 + TB]) for _ in range(2)]
        tA = [P.sb([128, TB]) for _ in range(4)]
        tAi = [0]

        def tmp():
            t = tA[tAi[0] % 4]
            tAi[0] += 1
            return t
        rowk = P.sb([1, 4])
        ropeT = P.sb([64, 2, TB])
        knt = [P.sb([128, 512], BF16) for _ in range(3)]; krt = [P.sb([65, 512], BF16) for _ in range(3)]
        vts = [P.sb([128, 4, 128], BF16) for _ in range(3)]; pts = [P.sb([128, 512], BF16) for _ in range(3)]
        avi = [0]
        accS = P.sb([128, TB])

        stg_f.extend([QKVP[:, :, :].rearrange("p a b -> p (a b)"), QKV[:, :, :].rearrange("p a b -> p (a b)")])
        stg_b.extend([Mg[:, :, :].rearrange("p a b -> p (a b)"), G3[:, :, :].rearrange("p a b -> p (a b)")])
        for l in range(L):
            prep_w("ada%d" % l, w_ada[l], D, 6 * D)
            prep_w("inA%d" % l, w_in[l][:, 0:3072], D, 3072)
            prep_w("inbeta%d" % l, w_in[l][:, O_BETA:O_BETA + 4], D, 4, Mc=4)
            prep_w("indec%d" % l, w_in[l][:, O_DEC:O_DEC + 4], D, 4, Mc=4)
            prep_w("inql%d" % l, w_in[l][:, O_QL:O_QL + 384], D, 384)
            prep_w("inkvl%d" % l, w_in[l][:, O_KVL:O_KVL + 256], D, 256)
            prep_w("inkpe%d" % l, w_in[l][:, O_KPE:O_KPE + 64], D, 64, Mc=64)
            prep_w("inkpp%d" % l, w_kpp[l], D, 64, Mc=64)
            prep_w("ingate%d" % l, w_in[l][:, O_GATE:O_GATE + 3072], D, 3072)
            prep_w("uqn%d" % l, w_uqn[l], 384, 512)
            prep_w("uqr%d" % l, w_uqr[l], 384, 256, Mc=64)
            prep_w("uqp%d" % l, w_uqp[l], 384, 256, Mc=64)
            prep_w("uk%d" % l, w_uk[l], 256, 512)
            for n in range(3):
                prep_w("br%d_%d" % (l, n), w_br[l, n], 512, D)
            prep_w("out%d" % l, w_out[l], D, D)
            prep_w("up%d" % l, w_up[l], D, 2 * DFF)
            prep_w("dna%d" % l, w_dn[l][0:1024, :], 1024, D)
            prep_w("dnb%d" % l, w_dn[l][1024:2048, :], 1024, D)
            prep_w("dnc%d" % l, w_dn[l][2048:2816, :], 768, D)

        milestone("prep")
        cvaw = [P.sb([128, 4, 31]) for _ in range(L)]; cvav = [P.sb([128, 3, 4]) for _ in range(L)]
        gcvw = [P.sb([128, 12, 4]) for _ in range(L)]; gdns = [P.sb([4, 2]) for _ in range(L)]
        nexpA = [P.sb([4, 1]) for _ in range(L)]; gdnng = [P.sb([128, 1]) for _ in range(L)]
        qng = [P.sb([128, 3]) for _ in range(L)]; kvng = [P.sb([128, 2]) for _ in range(L)]
        lnvs = [P.sb([128, 4, 8]) for _ in range(L)]; ffnw = [P.sb([128, KC_F, 3]) for _ in range(L)]
        ffnb = [P.sb([128, KC_F]) for _ in range(L)]
        wuv = [P.sb([128, 2, 512], BF16) for _ in range(L)]
        modT = [P.sb([128, 48, G]) for _ in range(L)]
        scp1 = [P.sb([128, 8, G]) for _ in range(L)]; scp2 = [P.sb([128, 8, G]) for _ in range(L)]
        g1p = [P.sb([128, 8, G]) for _ in range(L)]; g2p = [P.sb([128, 8, G]) for _ in range(L)]
        badas = P.sb([128, 48])
        csT = P.sb([128, 8, G]); csb = P.sb([128, 8, G], BF16)
        P.dma(csT[:], cT[:, :].rearrange("(kc p) g -> p kc g", p=128))
        P.act(csb[:], csT[:], AF.Silu)
        for l in range(L):
            P.dma(cvaw[l][:], cva_w[l]); P.dma(cvav[l][:], cva_v[l]); P.dma(gcvw[l][:], gcv_w[l])
            P.dma(gdns[l][:], gdn_s[l]); P.dma(gdnng[l][:], gdn_ng[l]); P.dma(qng[l][:], qn_g[l]); P.dma(kvng[l][:], kvn_g[l])
            P.dma(lnvs[l][:], lnv[l]); P.dma(ffnw[l][:], ffn_w[l]); P.dma(ffnb[l][:], ffn_b[l])
            P.act(nexpA[l][:], gdns[l][:, 0:1], AF.Exp)
            P.ts(nexpA[l][:], nexpA[l][:], -1.0, None, ALU.mult)
            sf = stg_f[0]
            P.dma(sf[:, 0:1024].rearrange("p (kc m) -> p kc m", kc=2), w_uv[l].rearrange("(kc p) m -> p kc m", p=128))
            P.copy(wuv[l][:], sf[:, 0:1024].rearrange("p (kc m) -> p kc m", kc=2))
            P.dma(badas[:], b_ada_p[l])
            for j in range(48):
                ps = lin("ada%d" % l, j, csb, G)
                P.ts(modT[l][:, j, :], ps, badas[:, j:j + 1], None, ALU.add)
            P.ts(scp1[l][:], modT[l][:, 8:16, :], 1.0, None, ALU.add)
            P.ts(g1p[l][:], modT[l][:, 16:24, :], 1.0, None, ALU.add)
            P.ts(scp2[l][:], modT[l][:, 32:40, :], 1.0, None, ALU.add)
            P.ts(g2p[l][:], modT[l][:, 40:48, :], 1.0, None, ALU.add)

        class St:
            pass

        def mkstate(tk, name):
            s = St()
            s.Uh = P.sb([128, 4, 30]); s.Qh = P.sb([128, 12, 3]); s.S = P.sb([128, 4, 128]); s.Ah = P.sb([128, KC_F, 2])
            s.Kb2 = P.sb([1, 4])
            s.KN = P.dram("KN_" + name, [4, 128, tk], BF16); s.KR = P.dram("KR_" + name, [65, tk], BF16)
            s.VV = P.dram("VV_" + name, [4, tk, 128], BF16)
            return s

        stP = [mkstate(SEQ, "p%d" % l) for l in range(L)]
        stS = mkstate(PAST + TS, "s")

        def colstats(Xt, nch, nt, need_mean):
            npart = Xt.shape[0]
            F = float(nch * npart)
            p1 = None
            if need_mean:
                p1 = pst()
                for ch in range(nch):
                    P.mm(p1[:, 0:nt], ones_f[0:npart, :], Xt[:, ch, 0:nt], start=(ch == 0), stop=(ch == nch - 1))
            p2 = pst()
            for ch in range(nch):
                P.act(SQ[0:npart, ch, 0:nt], Xt[:, ch, 0:nt], AF.Square)
            for ch in range(nch):
                P.mm(p2[:, 0:nt], ones_f[0:npart, :], SQ[0:npart, ch, 0:nt], start=(ch == 0), stop=(ch == nch - 1))
            mean = None
            var = tmp()
            if need_mean:
                mean = tmp()
                P.act(mean[:, 0:nt], p1[:, 0:nt], AF.Copy, scale=1.0 / F)
                msq = tmp()
                P.tt(msq[:, 0:nt], mean[:, 0:nt], mean[:, 0:nt], ALU.mult)
                P.stt(var[:, 0:nt], p2[:, 0:nt], 1.0 / F, msq[:, 0:nt], ALU.mult, ALU.subtract)
                eps = LN_EPS
            else:
                P.act(var[:, 0:nt], p2[:, 0:nt], AF.Copy, scale=1.0 / F)
                eps = RMS_EPS
            P.act(var[:, 0:nt], var[:, 0:nt], AF.Ln, bias=eps)
            rstd = tmp()
            P.act(rstd[:, 0:nt], var[:, 0:nt], AF.Exp, scale=-0.5)
            return mean, rstd

        def ln_fm(dst, Xt, nch, nt, g_v, b_v):
            mean, rstd = colstats(Xt, nch, nt, True)
            mb = mean[:, 0:nt].unsqueeze(1).to_broadcast([128, nch, nt])
            rb = rstd[:, 0:nt].unsqueeze(1).to_broadcast([128, nch, nt])
            P.tt(SQ[:, 0:nch, 0:nt], Xt[:, 0:nch, 0:nt], mb, ALU.subtract)
            P.tt(SQ[:, 0:nch, 0:nt], SQ[:, 0:nch, 0:nt], rb, ALU.mult)
            P.tt(SQ[:, 0:nch, 0:nt], SQ[:, 0:nch, 0:nt], g_v.bc(nt), ALU.mult)
            P.tt(dst[:, 0:nch, 0:nt], SQ[:, 0:nch, 0:nt], b_v.bc(nt), ALU.add)

        milestone("params")
        C4 = lambda: P.sb([128, 4, 128])
        g_ktm = C4(); g_vtm = C4(); g_e1 = C4(); g_Dm = C4(); g_DmT = C4(); g_N = C4(); g_At = C4()
        g_A2 = C4(); g_At2 = C4(); g_Rt = C4(); g_qkT = C4(); g_expGb = C4(); g_gB = C4()
        g_bv = g_e1; g_bk = g_Dm; g_kd = g_DmT; g_u = g_gB; g_wT = g_N; g_vnew = g_At; g_o = g_A2; g_t = g_At2
        g_btm = P.sb([128, 4]); g_gtm = P.sb([128, 4]); g_nb = P.sb([128, 4]); g_Gcol = P.sb([128, 4]); g_eG = P.sb([128, 4])
        g_beG = P.sb([128, 4]); g_kdc = P.sb([128, 4]); g_gam = P.sb([128, 4])

        def gdn_chunk(l, st, c0, C):
            pk = pstA(); pv = pstA()
            for h in range(4):
                P.tr(pk[0:C, h * 128:(h + 1) * 128], QKV[:, 4 + h, c0:c0 + C], ident[:])
                P.tr(pv[0:C, h * 128:(h + 1) * 128], QKV[:, 8 + h, c0:c0 + C], ident[:])
            P.copy(g_ktm[0:C].rearrange("p h d -> p (h d)"), pk[0:C, :], eng="act")
            P.copy(g_vtm[0:C].rearrange("p h d -> p (h d)"), pv[0:C, :])
            pb = pstA()
            P.tr(pb[0:C, 0:4], BE[0:4, c0:c0 + C], ident[0:4, 0:4])
            P.tr(pb[0:C, 4:8], GG[0:4, c0:c0 + C], ident[0:4, 0:4])
            P.copy(g_btm[0:C, :], pb[0:C, 0:4]); P.copy(g_gtm[0:C, :], pb[0:C, 4:8], eng="act")
            P.ts(g_nb[0:C, :], g_btm[0:C, :], -1.0, None, ALU.mult)
            milestone("g1")
            yield
            P.tt(g_gB[0:C], ones_f[0:C, :].unsqueeze(1).to_broadcast([C, 4, 128]), g_gtm[0:C, :].bc(128), ALU.mult)
            pG = pstA(); pc = pstA()
            for h in range(4):
                P.mm(pG[:, h * 128:h * 128 + C], g_gB[0:C, h, :], triu[0:C, 0:C], True, True)
            P.mm(pc[0:C, 0:4], triu[0:C, 0:C], g_gtm[0:C, 0:4], True, True)
            milestone("g2")
            yield
            P.copy(g_Gcol[0:C, :], pc[0:C, 0:4])
            pGv = pG[:, :].rearrange("p (h j) -> p h j", h=4)
            P.act(g_expGb[:, :, 0:C], pGv[:, :, 0:C], AF.Exp)
            P.act(g_eG[0:C, :], g_Gcol[0:C, :], AF.Exp)
            P.tt(g_beG[0:C, :], g_btm[0:C, :], g_eG[0:C, :], ALU.mult)
            P.copy(g_gam[:, :], g_expGb[:, :, C - 1])
            P.copy(g_kdc[0:C, :], pGv[0:C, :, C - 1], eng="act")
            P.tt(g_kdc[0:C, :], g_kdc[0:C, :], g_Gcol[0:C, :], ALU.subtract)
            P.act(g_kdc[0:C, :], g_kdc[0:C, :], AF.Exp)
            P.tt(g_e1[0:C, :, 0:C], pGv[0:C, :, 0:C], g_Gcol[0:C, :].bc(C), ALU.subtract)
            P.ts(g_Dm[0:C, :, 0:C], g_e1[0:C, :, 0:C], 0.0, None, ALU.max)
            P.act(g_Dm[0:C, :, 0:C], g_Dm[0:C, :, 0:C], AF.Exp, scale=-1.0)
            P.ts(g_DmT[0:C, :, 0:C], g_e1[0:C, :, 0:C], 0.0, None, ALU.min)
            P.act(g_DmT[0:C, :, 0:C], g_DmT[0:C, :, 0:C], AF.Exp)
            P.tt(g_DmT[0:C, :, 0:C], g_DmT[0:C, :, 0:C], b4(mtriT, C), ALU.mult)
            P.tt(g_Dm[0:C, :, 0:C], g_Dm[0:C, :, 0:C], b4(mstr, C), ALU.mult)
            milestone("g3")
            yield
            pkk = pstA(); pqk = pstA()
            for h in range(4):
                P.mm(pkk[0:C, h * 128:h * 128 + C], QKV[:, 4 + h, c0:c0 + C], QKV[:, 4 + h, c0:c0 + C], True, True)
                P.mm(pqk[0:C, h * 128:h * 128 + C], QKV[:, 4 + h, c0:c0 + C], QKV[:, h, c0:c0 + C], True, True)
            pkkv = pkk[:, :].rearrange("p (h j) -> p h j", h=4); pqkv = pqk[:, :].rearrange("p (h j) -> p h j", h=4)
            P.tt(g_N[0:C, :, 0:C], pkkv[0:C, :, 0:C], g_Dm[0:C, :, 0:C], ALU.mult)
            P.tt(g_N[0:C, :, 0:C], g_N[0:C, :, 0:C], g_nb[0:C, :].bc(C), ALU.mult)
            P.tt(g_qkT[0:C, :, 0:C], pqkv[0:C, :, 0:C], g_DmT[0:C, :, 0:C], ALU.mult)
            milestone("g4")
            yield
            pt_ = pstA()
            for h in range(4):
                P.tr(pt_[0:C, h * 128:h * 128 + C], g_N[0:C, h, 0:C], ident[0:C, 0:C])
            ptv = pt_[:, :].rearrange("p (h j) -> p h j", h=4)
            milestone("g4t")
            P.copy(g_At[0:C, :, 0:C], ptv[0:C, :, 0:C], eng="act")
            milestone("g4c")
            P.tt(g_Rt[0:C, :, 0:C], g_At[0:C, :, 0:C], b4(ident, C), ALU.add)
            milestone("g4a")
            yield
            A, At, A2, At2 = g_N, g_At, g_A2, g_At2
            nsq = int(round(math.log2(C))) - 1
            for m in range(nsq):
                pa = pstA(); pat = pstA()
                for h in range(4):
                    P.mm(pa[0:C, h * 128:h * 128 + C], At[0:C, h, 0:C], A[0:C, h, 0:C], True, True)
                    P.mm(pat[0:C, h * 128:h * 128 + C], A[0:C, h, 0:C], At[0:C, h, 0:C], True, True)
                P.copy(A2[0:C, :, 0:C], pa[:, :].rearrange("p (h j) -> p h j", h=4)[0:C, :, 0:C], eng="act")
                P.copy(At2[0:C, :, 0:C], pat[:, :].rearrange("p (h j) -> p h j", h=4)[0:C, :, 0:C])
                A, At, A2, At2 = A2, At2, A, At
                yield
                pr = pstA()
                for h in range(4):
                    P.mm(pr[0:C, h * 128:h * 128 + C], A[0:C, h, 0:C], g_Rt[0:C, h, 0:C], True, True)
                P.tt(g_Rt[0:C, :, 0:C], g_Rt[0:C, :, 0:C], pr[:, :].rearrange("p (h j) -> p h j", h=4)[0:C, :, 0:C], ALU.add)
                milestone("g4b")
                yield
            milestone("g5")
            yield
            P.tt(g_bv[0:C], g_vtm[0:C], g_btm[0:C, :].bc(128), ALU.mult)
            P.tt(g_bk[0:C], g_ktm[0:C], g_beG[0:C, :].bc(128), ALU.mult)
            P.tt(g_kd[0:C], g_ktm[0:C], g_kdc[0:C, :].bc(128), ALU.mult)
            pu = pstA(); pw = pstA()
            for h in range(4):
                P.mm(pu[0:C, h * 128:(h + 1) * 128], g_Rt[0:C, h, 0:C], g_bv[0:C, h, :], True, True)
                P.mm(pw[:, h * 128:h * 128 + C], g_bk[0:C, h, :], g_Rt[0:C, h, 0:C], True, True)
            P.copy(g_u[0:C].rearrange("p h d -> p (h d)"), pu[0:C, :], eng="act")
            P.copy(g_wT[:, :, 0:C], pw[:, :].rearrange("p (h j) -> p h j", h=4)[:, :, 0:C])
            milestone("g6")
            yield
            pws = pstA()
            for h in range(4):
                P.mm(pws[0:C, h * 128:(h + 1) * 128], g_wT[:, h, 0:C], st.S[:, h, :], True, True)
            P.tt(g_vnew[0:C].rearrange("p h d -> p (h d)"), g_u[0:C].rearrange("p h d -> p (h d)"), pws[0:C, :], ALU.subtract)
            yield
            po1 = pstA(); po2 = pstA(); psn = pstA()
            for h in range(4):
                P.mm(po1[:, h * 128:h * 128 + C], st.S[:, h, :], QKV[:, h, c0:c0 + C], True, True)
                P.mm(po2[:, h * 128:h * 128 + C], g_vnew[0:C, h, :], g_qkT[0:C, h, 0:C], True, True)
                P.mm(psn[:, h * 128:(h + 1) * 128], g_kd[0:C, h, :], g_vnew[0:C, h, :], True, True)
            P.tt(g_t[:, :, 0:C], po1[:, :].rearrange("p (h j) -> p h j", h=4)[:, :, 0:C], g_expGb[:, :, 0:C], ALU.mult)
            P.tt(g_o[:, :, 0:C], g_t[:, :, 0:C], po2[:, :].rearrange("p (h j) -> p h j", h=4)[:, :, 0:C], ALU.add)
            for h in range(4):
                P.stt(st.S[:, h, :], st.S[:, h, :], g_gam[:, h:h + 1], psn[:, h * 128:(h + 1) * 128], ALU.mult, ALU.add)
            milestone("g7")
            yield
            P.act(g_t[:, :, 0:C], g_o[:, :, 0:C], AF.Square)
            pss = pstA()
            for h in range(4):
                P.mm(pss[:, h * 128:h * 128 + C], ones_f[:, :], g_t[:, h, 0:C], True, True)
            pssv = pss[:, :].rearrange("p (h j) -> p h j", h=4)
            P.act(g_t[:, :, 0:C], pssv[:, :, 0:C], AF.Ln, bias=RMS_EPS, scale=1.0 / 128.0)
            P.act(g_t[:, :, 0:C], g_t[:, :, 0:C], AF.Exp, scale=-0.5)
            P.tt(g_o[:, :, 0:C], g_o[:, :, 0:C], g_t[:, :, 0:C], ALU.mult)
            P.ts(g_o[:, :, 0:C], g_o[:, :, 0:C], gdnng[l][:, 0:1], None, ALU.mult)
            P.tt(YB[:, :, c0:c0 + C], g_o[:, :, 0:C], Zs[:, :, c0:c0 + C], ALU.mult)

        def attention(st, nt, groups):
            for h in range(4):
                first = True
                nvis = sum((nk + 127) // 128 for (_, nk, _) in groups)
                vi = 0
                pend = None
                P.memset(accS[:, 0:nt], 0.0, eng="pool")

                def flush(pd):
                    vt_, r_, kk_, c0_, pt_, fi_, la_ = pd
                    P.mm(ps_o[:, c0_:nt], vt_[0:kk_, r_, :], pt_[0:kk_, c0_:nt], fi_, la_)
                    P.tt(accS[0:kk_, c0_:nt], accS[0:kk_, c0_:nt], pt_[0:kk_, c0_:nt], ALU.add, eng="pool")
                for (k0, nk, diag) in groups:
                    i = avi[0]; avi[0] += 1
                    kn = knt[i % 3]; kr = krt[i % 3]; vt = vts[i % 3]
                    P.dma(kn[:, 0:nk], st.KN[h, :, k0:k0 + nk], q="sp")
                    P.dma(kr[:, 0:nk], st.KR[:, k0:k0 + nk], q="sp")
                    nb_ = (nk + 127) // 128
                    if nk >= 128:
                        P.dma(vt[:, 0:nb_, :], st.VV[h, k0:k0 + nk, :].rearrange("(b p) d -> p b d", p=128), q="sp")
                    else:
                        P.dma(vt[0:nk, 0, :], st.VV[h, k0:k0 + nk, :], q="sp")
                    for r in range(nb_):
                        kk = min(128, nk - 128 * r)
                        c0 = 128 * r if diag else 0
                        if c0 >= nt:
                            vi += 1
                            continue
                        sc = pstB()
                        P.mm(sc[0:kk, c0:nt], kn[:, 128 * r:128 * r + kk], Qn[:, h, c0:nt], True, False)
                        P.mm(sc[0:kk, c0:nt], kr[0:65, 128 * r:128 * r + kk], QR[0:65, h, c0:nt], False, True)
                        pt = pts[vi % 3]
                        P.act(pt[0:kk, c0:nt], sc[0:kk, c0:nt], AF.Exp)
                        if diag and kk > 64:
                            P.memset(pt[64:kk, c0:min(nt, c0 + 64)], 0.0, eng="pool")
                        last = (vi == nvis - 1)
                        if pend is not None:
                            flush(pend)
                        pend = (vt, r, kk, c0, pt, first, last)
                        first = False
                        vi += 1
                        yield
                if pend is not None:
                    flush(pend)
                P.mm(ps_s[:, 0:nt], ones_f[:, :], accS[:, 0:nt], True, True)
                rs = tmp()
                P.recip(rs[:, 0:nt], ps_s[:, 0:nt])
                P.tt(YC[:, h, 0:nt], ps_o[:, 0:nt], rs[:, 0:nt], ALU.mult)

        def interleave(gens):
            gens = list(gens)
            while gens:
                for gi in list(gens):
                    try:
                        next(gi)
                    except StopIteration:
                        gens.remove(gi)

        def knorm(l, st, nt, k0):
            for h in range(4):
                pkh = lin("uk%d" % l, h, LATb, nt)
                P.copy(KNb[:, h, 0:nt], pkh, eng="act")
                P.act(KNf[:, 0:nt], pkh, AF.Square)
                pk2 = pst()
                P.mm(pk2[0:1, 0:nt], ones_f[:, 0:1], KNf[:, 0:nt], True, False)
                P.mm(pk2[0:1, 0:nt], ones_f[0:64, 0:1], SQ[0:64, 0, 0:nt], False, True)
                P.dma(st.KN[h, :, k0:k0 + nt], KNb[:, h, 0:nt], q="pool")
                P.emit("dve", lambda e, pk2=pk2, h=h, nt=nt: e.tensor_reduce(
                    out=_ap(rowk[0:1, h:h + 1]), in_=_ap(pk2[0:1, 0:nt]), op=ALU.max, axis=mybir.AxisListType.X),
                    [pk2], [rowk])
            P.tt(st.Kb2[0:1, :], st.Kb2[0:1, :], rowk[0:1, :], ALU.max)

        def block_layer(l, st, nt, g, pos_rope, t0, cachek0, groups, outs, Cg):
            for kc in range(8):
                P.act(Hb[:, kc, 0:nt], X[:, kc, 0:nt], AF.Identity, bias=modT[l][:, kc, g:g + 1], scale=scp1[l][:, kc, g:g + 1])
            P.copy(U[:, :, 0:30], st.Uh[:, :, :])
            for j in range(4):
                pv = lin("inA%d" % l, j, Hb, nt)
                pg = lin("inA%d" % l, 4 + j, Hb, nt)
                sg = tmp()
                P.act(sg[:, 0:nt], pg, AF.Sigmoid)
                P.tt(U[:, j, 30:30 + nt], pv, sg[:, 0:nt], ALU.mult)
            P.copy(st.Uh[:, :, :], U[:, :, nt:nt + 30])
            milestone("A")
            P.copy(QKVP[:, :, 0:3], st.Qh[:, :, :], eng="pool")
            for j in range(12):
                pq = lin("inA%d" % l, 8 + j, Hb, nt)
                P.copy(QKVP[:, j, 3:3 + nt], pq, eng=("act" if j % 2 else "dve"))
            P.copy(st.Qh[:, :, :], QKVP[:, :, nt:nt + 3], eng="pool")
            for j in range(4):
                pz = lin("inA%d" % l, 20 + j, Hb, nt)
                P.act(Zs[:, j, 0:nt], pz, AF.Silu)
            pb = lin("inbeta%d" % l, 0, Hb, nt)
            P.act(BE[0:4, 0:nt], pb, AF.Sigmoid)
            pd = lin("indec%d" % l, 0, Hb, nt)
            P.act(GG[0:4, 0:nt], pd, AF.Exp, bias=gdns[l][:, 1:2])
            P.act(GG[0:4, 0:nt], GG[0:4, 0:nt], AF.Ln, bias=1.0)
            P.ts(GG[0:4, 0:nt], GG[0:4, 0:nt], nexpA[l][:, 0:1], None, ALU.mult)
            milestone("B")
            for j in range(3):
                pq = lin("inql%d" % l, j, Hb, nt)
                P.copy(QLf[:, j, 0:nt], pq, eng=("act" if j % 2 else "dve"))
            _, rstd = colstats(QLf, 3, nt, False)
            P.tt(QLf[:, :, 0:nt], QLf[:, :, 0:nt], rstd[:, 0:nt].unsqueeze(1).to_broadcast([128, 3, nt]), ALU.mult)
            P.tt(QLn[:, :, 0:nt], QLf[:, :, 0:nt], qng[l][:, :].bc(nt), ALU.mult)
            for j in range(2):
                pq = lin("inkvl%d" % l, j, Hb, nt)
                P.copy(LAT[:, j, 0:nt], pq, eng=("act" if j % 2 else "dve"))
            _, rstd = colstats(LAT, 2, nt, False)
            P.tt(LAT[:, :, 0:nt], LAT[:, :, 0:nt], rstd[:, 0:nt].unsqueeze(1).to_broadcast([128, 2, nt]), ALU.mult)
            P.tt(LAT[:, :, 0:nt], LAT[:, :, 0:nt], kvng[l][:, :].bc(nt), ALU.mult)
            P.copy(LATb[:, :, 0:nt], LAT[:, :, 0:nt], eng="pool")
            P.dma(outs["lat"][:, t0:t0 + nt].rearrange("(kc p) t -> p kc t", p=128), LAT[:, :, 0:nt], q="pool")
            P.dma(ropeT[:, :, 0:nt], pos_rope.rearrange("a d t -> d a t"))
            pk1 = lin("inkpe%d" % l, 0, Hb, nt)
            pk2 = lin("inkpp%d" % l, 0, Hb, nt)
            t1 = tmp(); t2 = tmp()
            P.tt(t1[0:64, 0:nt], pk1, ropeT[:, 0, 0:nt], ALU.mult)
            P.tt(t2[0:64, 0:nt], pk2, ropeT[:, 1, 0:nt], ALU.mult)
            P.tt(KPE[:, 0:nt], t1[0:64, 0:nt], t2[0:64, 0:nt], ALU.add)
            P.dma(outs["kpe"][:, t0:t0 + nt], KPE[:, 0:nt], q="pool")
            P.copy(KPEb[0:64, 0:nt], KPE[:, 0:nt], eng="pool")
            P.memset(KPEb[64:65, 0:nt], 1.0, eng="pool")
            P.dma(st.KR[:, cachek0:cachek0 + nt], KPEb[:, 0:nt], q="pool")
            milestone("C")
            for j in range(24):
                pgt = lin("ingate%d" % l, j, Hb, nt)
                P.act(G3[:, j, 0:nt], pgt, AF.Sigmoid)
            milestone("gates")
            acc = SQ
            for ch in range(4):
                P.ts(acc[:, 4 + ch, 0:nt], U[:, ch, 0:nt], cvaw[l][:, ch, 0:1], None, ALU.mult)
                for k in range(1, 31):
                    P.stt(acc[:, 4 + ch, 0:nt], U[:, ch, k:k + nt], cvaw[l][:, ch, k:k + 1], acc[:, 4 + ch, 0:nt], ALU.mult, ALU.add)
                P.ts(acc[:, 4 + ch, 0:nt], acc[:, 4 + ch, 0:nt], cvav[l][:, 0, ch:ch + 1], None, ALU.add)
            P.copy(X1[:, 0:4, 0:nt], acc[:, 4:8, 0:nt], eng="pool")
            ln_fm(X1[:, 4:8, :], X1[:, 0:4, :], 4, nt, cvav[l][:, 1, :], cvav[l][:, 2, :])
            P.act(YA[:, :, 0:nt], X1[:, 4:8, 0:nt], AF.Silu)
            milestone("convA")
            for j in range(12):
                P.ts(QKV[:, j, 0:nt], QKVP[:, j, 0:nt], gcvw[l][:, j, 0:1], None, ALU.mult)
                for k in range(1, 4):
                    P.stt(QKV[:, j, 0:nt], QKVP[:, j, k:k + nt], gcvw[l][:, j, k:k + 1], QKV[:, j, 0:nt], ALU.mult, ALU.add)
            P.act(QKV[:, :, 0:nt], QKV[:, :, 0:nt], AF.Silu)
            P.act(SQ[:, 0:8, 0:nt], QKV[:, 0:8, 0:nt], AF.Square)
            for j in range(8):
                pn = pst()
                P.mm(pn[:, 0:nt], ones_f[:, :], SQ[:, j, 0:nt], True, True)
                rn = tmp()
                P.act(rn[:, 0:nt], pn[:, 0:nt], AF.Ln, bias=RMS_EPS)
                P.act(rn[:, 0:nt], rn[:, 0:nt], AF.Exp, scale=-0.5)
                if j < 4:
                    P.stt(QKV[:, j, 0:nt], QKV[:, j, 0:nt], 128 ** -0.5, rn[:, 0:nt], ALU.mult, ALU.mult)
                else:
                    P.tt(QKV[:, j, 0:nt], QKV[:, j, 0:nt], rn[:, 0:nt], ALU.mult)
            milestone("gdnprep")
            milestone("gdn")
            for h in range(4):
                pqn = lin("uqn%d" % l, h, QLn, nt)
                P.act(QNf[:, h, 0:nt], pqn, AF.Copy, scale=SCALE_Q)
                pqr = lin("uqr%d" % l, h, QLn, nt)
                pqp = lin("uqp%d" % l, h, QLn, nt)
                t1 = tmp(); t2 = tmp()
                P.tt(t1[0:64, 0:nt], pqr, ropeT[:, 0, 0:nt], ALU.mult)
                P.tt(t2[0:64, 0:nt], pqp, ropeT[:, 1, 0:nt], ALU.mult)
                P.stt(QRf[:, h, 0:nt], t1[0:64, 0:nt], 1.0, t2[0:64, 0:nt], ALU.mult, ALU.add)
            P.ts(QRf[:, :, 0:nt], QRf[:, :, 0:nt], SCALE_Q, None, ALU.mult)
            P.copy(Qn[:, :, 0:nt], QNf[:, :, 0:nt], eng="pool")
            P.copy(QR[0:64, :, 0:nt], QRf[:, :, 0:nt], eng="pool")
            P.act(SQ[0:64, 0, 0:nt], KPE[:, 0:nt], AF.Square)
            knorm(l, st, nt, cachek0)
            nsb = (nt + 127) // 128
            for sbk in range(nsb):
                kk = min(128, nt - 128 * sbk)
                pvv = pst()
                for kc in range(2):
                    P.mm(pvv[0:kk, :], LATb[:, kc, 128 * sbk:128 * sbk + kk], wuv[l][:, kc, :], kc == 0, kc == 1)
                P.copy(VVb[0:kk, 4 * sbk:4 * sbk + 4, :].rearrange("p h d -> p (h d)"), pvv[0:kk, :], eng="act")
                for h in range(4):
                    P.dma(st.VV[h, cachek0 + 128 * sbk:cachek0 + 128 * sbk + kk, :], VVb[0:kk, 4 * sbk + h, :],
                          q=("sp", "pool")[h % 2])
            P.act(SQ[:, 0:4, 0:nt], QNf[:, :, 0:nt], AF.Square)
            P.act(SQ[0:64, 4:8, 0:nt], QRf[:, :, 0:nt], AF.Square)
            for h in range(4):
                pq2 = pst()
                P.mm(pq2[0:1, 0:nt], ones_f[:, 0:1], SQ[:, h, 0:nt], True, False)
                P.mm(pq2[0:1, 0:nt], ones_f[0:64, 0:1], SQ[0:64, 4 + h, 0:nt], False, True)
                tq = tmp()
                P.ts(tq[0:1, 0:nt], pq2[0:1, 0:nt], st.Kb2[0:1, h:h + 1], None, ALU.mult)
                P.act(tq[0:1, 0:nt], tq[0:1, 0:nt], AF.Sqrt)
                rb_ = rowbf[h % 2]
                P.ts(rb_[0:1, 0:nt], tq[0:1, 0:nt], -1.0, None, ALU.mult)
                P.dma(QR[64:65, h, 0:nt], rb_[0:1, 0:nt])
            milestone("mla")

            def gdn_all():
                for c0 in range(0, nt, Cg):
                    yield from gdn_chunk(l, st, c0, min(Cg, nt - c0))
            interleave([gdn_all(), attention(st, nt, groups)])
            milestone("attn")
            for m in range(8):
                ta = tmp(); tb_ = tmp()
                for n, Y in enumerate((YA, YB, YC)):
                    pm = lin("br%d_%d" % (l, n), m, Y, nt)
                    if n == 0:
                        P.tt(ta[:, 0:nt], pm, G3[:, 0 * 8 + m, 0:nt], ALU.mult)
                    elif n == 1:
                        P.tt(tb_[:, 0:nt], pm, G3[:, 1 * 8 + m, 0:nt], ALU.mult)
                        P.tt(ta[:, 0:nt], ta[:, 0:nt], tb_[:, 0:nt], ALU.add, eng="pool")
                    else:
                        P.tt(tb_[:, 0:nt], pm, G3[:, 2 * 8 + m, 0:nt], ALU.mult)
                        P.tt(Mg[:, m, 0:nt], ta[:, 0:nt], tb_[:, 0:nt], ALU.add, eng="pool")
            milestone("merge")
            for m in range(8):
                po = lin("out%d" % l, m, Mg, nt)
                tr_ = tmp()
                P.ts(tr_[:, 0:nt], po, g1p[l][:, m, g:g + 1], None, ALU.mult)
                P.stt(X[:, m, 0:nt], X[:, m, 0:nt], ALU_ALPHA, tr_[:, 0:nt], ALU.mult, ALU.add)
            ln_fm(X1, X, 8, nt, lnvs[l][:, 0, :], lnvs[l][:, 1, :])
            milestone("ln1")
            for kc in range(8):
                P.act(Hb[:, kc, 0:nt], X1[:, kc, 0:nt], AF.Identity, bias=modT[l][:, 24 + kc, g:g + 1], scale=scp2[l][:, kc, g:g + 1])
            for j in range(KC_F):
                aj = Aj[j % 2]
                P.copy(aj[:, 0:2], st.Ah[:, j, :], eng="pool")
                pa = lin("up%d" % l, j, Hb, nt)
                P.copy(aj[:, 2:2 + nt], pa, eng="act")
                P.copy(st.Ah[:, j, :], aj[:, nt:nt + 2], eng="pool")
                pvv = lin("up%d" % l, KC_F + j, Hb, nt)
                ca = tmp()
                P.ts(ca[:, 0:nt], aj[:, 0:nt], ffnw[l][:, j, 0:1], None, ALU.mult)
                P.stt(ca[:, 0:nt], aj[:, 1:1 + nt], ffnw[l][:, j, 1:2], ca[:, 0:nt], ALU.mult, ALU.add)
                P.stt(ca[:, 0:nt], aj[:, 2:2 + nt], ffnw[l][:, j, 2:3], ca[:, 0:nt], ALU.mult, ALU.add)
                P.act(ca[:, 0:nt], ca[:, 0:nt], AF.Silu, bias=ffnb[l][:, j:j + 1])
                P.tt(Gf[:, j, 0:nt], ca[:, 0:nt], pvv, ALU.mult)
            for m in range(8):
                py = lin([("dna%d" % l, 0), ("dnb%d" % l, 8), ("dnc%d" % l, 16)], m, Gf, nt)
                tr_ = tmp()
                P.ts(tr_[:, 0:nt], py, g2p[l][:, m, g:g + 1], None, ALU.mult)
                P.stt(X1[:, m, 0:nt], X1[:, m, 0:nt], ALU_ALPHA, tr_[:, 0:nt], ALU.mult, ALU.add)
            ln_fm(X, X1, 8, nt, lnvs[l][:, 2, :], lnvs[l][:, 3, :])

        ALU_ALPHA = float(ALPHA)
        rowbf = [P.sb([1, TB], BF16) for _ in range(2)]

        def init_state_zero(st):
            P.memset(st.Uh[:], 0.0); P.memset(st.Qh[:], 0.0, eng="pool"); P.memset(st.S[:], 0.0)
            P.memset(st.Ah[:], 0.0, eng="pool"); P.memset(st.Kb2[:], 0.0)

        def write_states(l, st, nt, o_ca, o_gc, o_g, o_f):
            P.dma(o_ca.rearrange("(kc p) t -> p kc t", p=128), st.Uh[:, :, :])
            P.dma(o_gc.rearrange("(kc p) t -> p kc t", p=128), st.Qh[:, :, :])
            P.dma(o_g.rearrange("h k v -> k h v"), st.S[:, :, :])
            P.dma(o_f.rearrange("(kc p) t -> p kc t", p=128), st.Ah[:, :, :])

        for l in range(L):
            init_state_zero(stP[l])
        for b in range(NB):
            t0 = b * TB
            P.dma(X[:, :, 0:TB], xpT[:, t0:t0 + TB].rearrange("(kc p) t -> p kc t", p=128))
            ln_fm(X1, X, 8, TB, ln0s[:, 0, :], ln0s[:, 1, :])
            P.copy(X[:, :, 0:TB], X1[:, :, 0:TB], eng="pool")
            for l in range(L):
                groups = [(gb * TB, TB, gb == b) for gb in range(b + 1)]
                block_layer(l, stP[l], TB, 0, ropeP[:, :, t0:t0 + TB], t0, t0, groups,
                            {"lat": o_platT[l], "kpe": o_pkpeT[l]}, min(128, TB))
                if b == 0 and l == 0:
                    dbg("x_l0b0", X[:, :, 0:TB], [128, 8, TB])
            P.dma(ypT[:, t0:t0 + TB].rearrange("(kc p) t -> p kc t", p=128), X[:, :, 0:TB])
        for l in range(L):
            write_states(l, stP[l], TB, o_pcaT[l], o_pgcT[l], o_pg[l], o_pfT[l])

        for s in range(NS):
            P.dma(X[:, :, 0:TS], xsT[:, s * TS:(s + 1) * TS].rearrange("(kc p) t -> p kc t", p=128))
            ln_fm(X1, X, 8, TS, ln0s[:, 0, :], ln0s[:, 1, :])
            P.copy(X[:, :, 0:TS], X1[:, :, 0:TS], eng="pool")
            for l in range(L):
                st = stS
                P.dma(st.Uh[:, :, :], hA[l, s].rearrange("(kc p) t -> p kc t", p=128))
                P.dma(st.Qh[:, :, :], hB[l, s].rearrange("(kc p) t -> p kc t", p=128))
                P.dma(st.S[:, :, :], sG[l, s].rearrange("h k v -> k h v"))
                P.dma(st.Ah[:, :, :], hF[l, s].rearrange("(kc p) t -> p kc t", p=128))
                P.memset(st.Kb2[:], 0.0)
                for k0 in range(0, PAST, TB):
                    nk = min(TB, PAST - k0)
                    P.dma(LAT[:, :, 0:nk], clatT[l, s][:, k0:k0 + nk].rearrange("(kc p) t -> p kc t", p=128))
                    P.copy(LATb[:, :, 0:nk], LAT[:, :, 0:nk])
                    P.dma(KPE[:, 0:nk], ckpeT[l, s][:, k0:k0 + nk])
                    P.copy(KPEb[0:64, 0:nk], KPE[:, 0:nk], eng="pool")
                    P.memset(KPEb[64:65, 0:nk], 1.0, eng="pool")
                    P.dma(st.KR[:, k0:k0 + nk], KPEb[:, 0:nk])
                    P.act(SQ[0:64, 0, 0:nk], KPE[:, 0:nk], AF.Square)
                    knorm(l, st, nk, k0)
                    for sbk in range((nk + 127) // 128):
                        kk = min(128, nk - 128 * sbk)
                        pvv = pst()
                        for kc in range(2):
                            P.mm(pvv[0:kk, :], LATb[:, kc, 128 * sbk:128 * sbk + kk], wuv[l][:, kc, :], kc == 0, kc == 1)
                        P.copy(VVb[0:kk, 4 * sbk:4 * sbk + 4, :].rearrange("p h d -> p (h d)"), pvv[0:kk, :], eng="act")
                        for h in range(4):
                            P.dma(st.VV[h, k0 + 128 * sbk:k0 + 128 * sbk + kk, :], VVb[0:kk, 4 * sbk + h, :],
                                  q="pool")
                groups = [(k0, min(512, PAST - k0), False) for k0 in range(0, PAST, 512)] + [(PAST, TS, False)]
                block_layer(l, st, TS, 1 + s, ropeS[:, :, :], 0, PAST, groups,
                            {"lat": o_slatT[l, s], "kpe": o_skpeT[l, s]}, TS)
                write_states(l, st, TS, o_scaT[l, s], o_sgcT[l, s], o_sg[l, s], o_sfT[l, s])
            P.dma(ysT[:, s * TS:(s + 1) * TS].rearrange("(kc p) t -> p kc t", p=128), X[:, :, 0:TS])


    try:
        main_body()
    except StopBuild:
        pass
    P.wait_all("sp")
    P.finalize()
    return nc, sorted(dbg_out.keys())


import numpy as np
from concourse.bass_utils import run_bass_kernel_spmd

NCORES = 8
FULL_CFG = dict(SEQ=16384, NS=4, PAST=2048, TB=256, L=2)


def _pp(vec, n):
    return np.ascontiguousarray(vec.reshape(n, 128).T)


def _rope_tables(pos):
    half = 32
    inv_freq = np.power(np.float32(10000.0), -(np.arange(half, dtype=np.float32) / np.float32(half))).astype(np.float32)
    ang = pos.astype(np.float32)[:, None] * inv_freq[None, :]
    cos = np.cos(ang).astype(np.float32); sin = np.sin(ang).astype(np.float32)
    cosT = np.concatenate([cos, cos], axis=1).T
    sinT = np.concatenate([-sin, sin], axis=1).T
    return np.ascontiguousarray(np.stack([cosT, sinT], axis=0))


def prep_inputs(inp, cfg):
    SEQ, NS, PAST, L = cfg["SEQ"], cfg["NS"], cfg["PAST"], cfg["L"]
    f = lambda a: np.ascontiguousarray(np.asarray(a, dtype=np.float32))
    perm = np.concatenate([np.arange(32, 64), np.arange(0, 32)])
    sh = {}
    sh["xpT"] = f(np.asarray(inp["x_prompt"])[0].T)
    sh["ln0p"] = f(np.stack([_pp(np.asarray(inp["ln0_g"]), 8), _pp(np.asarray(inp["ln0_b"]), 8)], axis=1))
    sh["w_ada"] = f(inp["w_ada"])
    sh["b_ada_p"] = f(np.stack([_pp(np.asarray(inp["b_ada"])[l], 48) for l in range(L)]))
    w_in = np.asarray(inp["w_in"])
    sh["w_in"] = f(w_in)
    sh["w_kpp"] = f(w_in[:, :, O_KPE + perm])
    caw = np.asarray(inp["conv_a_w"])
    sh["cva_w"] = f(np.stack([caw[l].T.reshape(4, 128, 31).transpose(1, 0, 2) for l in range(L)]))
    sh["cva_v"] = f(np.stack([np.stack([_pp(np.asarray(inp[k])[l], 4) for k in ("conv_a_b", "ln_a_g", "ln_a_b")], axis=1)
                              for l in range(L)]))
    gcw = np.asarray(inp["gdn_conv_w"])
    sh["gcv_w"] = f(np.stack([gcw[l].T.reshape(12, 128, 4).transpose(1, 0, 2) for l in range(L)]))
    sh["gdn_s"] = f(np.stack([np.asarray(inp["gdn_a_log"]), np.asarray(inp["gdn_dt_bias"])], axis=-1))
    sh["gdn_ng"] = f(np.asarray(inp["gdn_norm_g"])[:, :, None])
    sh["qn_g"] = f(np.stack([_pp(np.asarray(inp["mla_q_norm_g"])[l], 3) for l in range(L)]))
    sh["kvn_g"] = f(np.stack([_pp(np.asarray(inp["mla_kv_norm_g"])[l], 2) for l in range(L)]))
    wuq = np.asarray(inp["mla_w_uq"]).reshape(L, 384, 4, 192)
    sh["w_uqn"] = f(wuq[:, :, :, :128].reshape(L, 384, 512))
    sh["w_uqr"] = f(wuq[:, :, :, 128:].reshape(L, 384, 256))
    sh["w_uqp"] = f(wuq[:, :, :, 128 + perm].reshape(L, 384, 256))
    wukv = np.asarray(inp["mla_w_ukv"]).reshape(L, 256, 4, 256)
    sh["w_uk"] = f(wukv[:, :, :, :128].reshape(L, 256, 512))
    sh["w_uv"] = f(wukv[:, :, :, 128:].reshape(L, 256, 512))
    sh["w_br"] = f(inp["w_branch"]); sh["w_out"] = f(inp["w_out"])
    sh["lnv"] = f(np.stack([np.stack([_pp(np.asarray(inp[k])[l], 8) for k in ("ln1_g", "ln1_b", "ln2_g", "ln2_b")], axis=1)
                            for l in range(L)]))
    sh["w_up"] = f(inp["w_up"])
    fcw = np.asarray(inp["ffn_conv_w"])
    sh["ffn_w"] = f(np.stack([fcw[l].T.reshape(KC_F, 128, 3).transpose(1, 0, 2) for l in range(L)]))
    sh["ffn_b"] = f(np.stack([_pp(np.asarray(inp["ffn_conv_b"])[l], KC_F) for l in range(L)]))
    sh["w_dn"] = f(inp["w_down"])
    sh["c_ident"] = np.eye(128, dtype=np.float32)
    pi = np.arange(128)[:, None]; fi = np.arange(128)[None, :]
    sh["c_triu"] = (pi <= fi).astype(np.float32)
    sh["c_mtri"] = (pi >= fi).astype(np.float32)
    sh["c_mtriT"] = (pi <= fi).astype(np.float32)
    sh["c_mstr"] = (pi > fi).astype(np.float32)
    sh["ropeP"] = _rope_tables(np.arange(SEQ))
    sh["ropeS"] = _rope_tables(PAST + np.arange(16))
    xs = np.asarray(inp["x_sample"]); cs = np.asarray(inp["c_sample"]); cp = np.asarray(inp["c_prompt"])
    in_maps = []
    for c in range(NCORES):
        sl = slice(c * NS, (c + 1) * NS)
        m = dict(sh)
        m["xsT"] = f(xs[sl].reshape(NS * 16, D).T)
        m["cT"] = f(np.concatenate([cp[0:1], cs[sl]], axis=0).T)
        m["clatT"] = f(np.asarray(inp["cache_mla_latent"])[:, sl].transpose(0, 1, 3, 2))
        m["ckpeT"] = f(np.asarray(inp["cache_mla_kpe"])[:, sl].transpose(0, 1, 3, 2))
        m["hA"] = f(np.asarray(inp["state_conv_a"])[:, sl].transpose(0, 1, 3, 2))
        m["hB"] = f(np.asarray(inp["state_gdn_conv"])[:, sl].transpose(0, 1, 3, 2))
        m["sG"] = f(np.asarray(inp["state_gdn"])[:, sl])
        m["hF"] = f(np.asarray(inp["state_ffn_conv"])[:, sl].transpose(0, 1, 3, 2))
        in_maps.append(m)
    return in_maps


def assemble(res, cfg):
    NS = cfg["NS"]
    r0 = res[0]
    T = lambda a: np.ascontiguousarray(np.swapaxes(a, -1, -2))
    y_p = T(r0["ypT"])[None]
    y_s = np.concatenate([T(r["ysT"]).reshape(NS, 16, D) for r in res], axis=0)
    p_lat = T(r0["o_platT"])[:, None]; p_kpe = T(r0["o_pkpeT"])[:, None]
    p_ca = T(r0["o_pcaT"])[:, None]; p_gc = T(r0["o_pgcT"])[:, None]; p_g = r0["o_pg"][:, None]; p_f = T(r0["o_pfT"])[:, None]
    cat = lambda k, tr: np.concatenate([(T(r[k]) if tr else r[k]) for r in res], axis=1)
    outs = (y_p, y_s, p_lat, p_kpe, p_ca, p_gc, p_g, p_f,
            cat("o_slatT", True), cat("o_skpeT", True), cat("o_scaT", True), cat("o_sgcT", True), cat("o_sg", False),
            cat("o_sfT", True))
    return tuple(np.ascontiguousarray(o, dtype=np.float32) for o in outs)


def run(inputs, cfg, debug=()):
    nc, dbg_names = build_program(cfg, debug)
    in_maps = prep_inputs(inputs, cfg)
    res = run_bass_kernel_spmd(nc, in_maps, core_ids=list(range(NCORES)))
    return assemble(res.results, cfg), res.results


def kernel(**inputs):
    outs, _ = run(inputs, FULL_CFG)
    return outs
```

```python
from contextlib import ExitStack
import numpy as np
import concourse.bass as bass
import concourse.mybir as mybir

F32 = mybir.dt.float32
BF16 = mybir.dt.bfloat16
I32 = mybir.dt.int32
AF = mybir.ActivationFunctionType
ALU = mybir.AluOpType

ENGS = ("pe", "act", "dve", "pool", "sp")
NDMA = 12


class Trk:
    __slots__ = ("w", "r", "name", "multi", "serial")

    def __init__(self, name="", multi=False):
        self.w = {}
        self.r = {}
        self.name = name
        self.multi = multi
        self.serial = False


class T:
    def __init__(self, handle, name):
        self.h = handle
        self.k = Trk(name)
        self.shape = tuple(handle.shape)

    def __getitem__(self, key):
        return V(self.h[key] if not isinstance(key, tuple) or True else None, [self.k])

    def ap(self):
        return V(self.h.ap() if hasattr(self.h, "ap") else self.h[:], [self.k])


class V:
    def __init__(self, ap, ks):
        self.ap = ap
        self.ks = ks

    def __getitem__(self, key):
        return V(self.ap[key], self.ks)

    def rearrange(self, s, **kw):
        return V(self.ap.rearrange(s, **kw), self.ks)

    def bitcast(self, dt):
        return V(self.ap.bitcast(dt), self.ks)

    def broadcast_to(self, shape):
        return V(self.ap.broadcast_to(shape), self.ks)

    def to_broadcast(self, shape):
        return V(self.ap.to_broadcast(shape), self.ks)

    def partition_broadcast(self, n):
        return V(self.ap.partition_broadcast(n), self.ks)

    def unsqueeze(self, a):
        return V(self.ap.unsqueeze(a), self.ks)

    def bc(self, n):
        sh = list(self.ap.shape)
        return V(self.ap.unsqueeze(len(sh)).to_broadcast(sh + [n]), self.ks)

    @property
    def shape(self):
        return tuple(self.ap.shape)


def _ap(x):
    return x.ap if isinstance(x, V) else x


class Prog:
    def __init__(self, nc):
        self.nc = nc
        self.es = ExitStack()
        self.q = {e: [] for e in ENGS}
        self.cnt = {e: 0 for e in ENGS}
        self.seen = {e: {} for e in ENGS}
        self.sem = {}
        for e in ENGS:
            self.sem[("c", e)] = self.es.enter_context(nc.semaphore("s_" + e))
        self.dtot = {}
        for qn in ("sp", "pool"):
            for i in range(NDMA):
                k = ("d", qn, i)
                self.sem[k] = self.es.enter_context(nc.semaphore("d_%s%d" % (qn, i)))
                self.dtot[k] = 0
        self.drr = {"sp": 0, "pool": 0}
        self.ntile = 0
        self.psum_free = None

    def sb(self, shape, dt=F32, name=None, multi=False):
        self.ntile += 1
        name = name or "t%d" % self.ntile
        h = self.es.enter_context(self.nc.sbuf_tensor(name, list(shape), dt))
        t = T(h, name)
        t.k.multi = multi
        return t

    def ps(self, shape, dt=F32, name=None):
        self.ntile += 1
        name = name or "p%d" % self.ntile
        h = self.es.enter_context(self.nc.psum_tensor(name, list(shape), dt))
        t = T(h, name)
        t.k.serial = True
        return t

    def dram(self, name, shape, dt=F32, kind=None, multi=False):
        if kind is None:
            h = self.nc.dram_tensor(name, list(shape), dt)
        else:
            h = self.nc.dram_tensor(name, list(shape), dt, kind=kind)
        t = T(h, name)
        t.k.multi = multi
        return t

    def emit(self, eng, fn, reads, writes, dma=False, accum_pe=False):
        rk, wk = [], []
        for x in reads:
            if x is None:
                continue
            rk.extend(x.ks if isinstance(x, V) else [x.k])
        for x in writes:
            rk_ = x.ks if isinstance(x, V) else [x.k]
            wk.extend(rk_)
        deps = {}

        def need(ev):
            if ev is None:
                return
            k, v = ev
            if deps.get(k, 0) < v:
                deps[k] = v

        myk = ("c", eng)
        for k in rk:
            for sk, v in k.w.items():
                need((sk, v))
            if k.serial:
                for sk, v in k.r.items():
                    if sk != myk:
                        need((sk, v))
        for k in wk:
            if not k.multi:
                for sk, v in k.w.items():
                    if accum_pe and sk == myk:
                        continue
                    need((sk, v))
            for sk, v in k.r.items():
                if sk == myk and not dma:
                    continue
                need((sk, v))
        if dma:
            i = self.drr[eng]
            self.drr[eng] = (i + 1) % NDMA
            dk = ("d", eng, i)
            if self.dtot[dk] > 0:
                need((dk, self.dtot[dk]))
            self.dtot[dk] += 16
            ev = (dk, self.dtot[dk])
            inc = (self.sem[dk], 16)
        else:
            self.cnt[eng] += 1
            ev = (myk, self.cnt[eng])
            inc = (self.sem[myk], 1)
        waits = []
        seen = self.seen[eng]
        for k, v in deps.items():
            if seen.get(k, 0) >= v:
                continue
            seen[k] = v
            waits.append((self.sem[k], v))
        self.q[eng].append((waits, fn, inc))
        for k in wk:
            if k.multi:
                if k.w.get(ev[0], 0) < ev[1]:
                    k.w[ev[0]] = ev[1]
            else:
                k.w = {ev[0]: ev[1]}
                k.r = {}
        for k in rk:
            if k in wk:
                continue
            if k.r.get(ev[0], 0) < ev[1]:
                k.r[ev[0]] = ev[1]
        return ev

    def wait_all(self, eng="sp"):
        waits = []
        for e in ENGS:
            if self.cnt[e] > 0:
                waits.append((self.sem[("c", e)], self.cnt[e]))
        for k, v in self.dtot.items():
            if v > 0:
                waits.append((self.sem[k], v))
        self.q[eng].append((waits, None, None))

    def finalize(self):
        nc = self.nc
        q = self.q

        def run(engname):
            def body(eng):
                for waits, fn, inc in q[engname]:
                    for s, v in waits:
                        eng.wait_ge(s, v)
                    if fn is not None:
                        ins = fn(eng)
                        ins.then_inc(inc[0], inc[1])
            return body

        with nc.Block() as block:
            block.tensor(run("pe"))
            block.scalar(run("act"))
            block.vector(run("dve"))
            block.gpsimd(run("pool"))
            block.sync(run("sp"))
        self.es.close()

    def dma(self, out, in_, q="sp", **kw):
        return self.emit(q, lambda e: e.dma_start(out=_ap(out), in_=_ap(in_), **kw), [in_], [out], dma=True)

    def mm(self, out, lhsT, rhs, start=True, stop=True):
        return self.emit("pe", lambda e: e.matmul(_ap(out), _ap(lhsT), _ap(rhs), start=start, stop=stop),
                         [lhsT, rhs], [out], accum_pe=True)

    def tr(self, out, in_, ident):
        return self.emit("pe", lambda e: e.transpose(_ap(out), _ap(in_), _ap(ident)), [in_, ident], [out],
                         accum_pe=True)

    def act(self, out, in_, func, bias=None, scale=None, eng="act", accum_out=None):
        kw = {}
        rd = [in_]
        if bias is not None:
            kw["bias"] = _ap(bias)
            if isinstance(bias, (V, T)):
                rd.append(bias)
        if scale is not None:
            kw["scale"] = _ap(scale)
            if isinstance(scale, (V, T)):
                rd.append(scale)
        wr = [out]
        if accum_out is not None:
            kw["accum_out"] = _ap(accum_out)
            wr.append(accum_out)
        return self.emit("act", lambda e: e.activation(_ap(out), _ap(in_), func, **kw), rd, wr)

    def tt(self, out, a, b, op, eng="dve"):
        return self.emit(eng, lambda e: e.tensor_tensor(_ap(out), _ap(a), _ap(b), op), [a, b], [out])

    def ts(self, out, a, s1, s2=None, op0=ALU.mult, op1=None, eng="dve"):
        rd = [a] + [s for s in (s1, s2) if isinstance(s, (V, T))]
        if op1 is None:
            return self.emit(eng, lambda e: e.tensor_scalar(_ap(out), _ap(a), _ap(s1), None, op0), rd, [out])
        return self.emit(eng, lambda e: e.tensor_scalar(_ap(out), _ap(a), _ap(s1), _ap(s2), op0, op1), rd, [out])

    def stt(self, out, a, s, b, op0, op1):
        rd = [a, b] + ([s] if isinstance(s, (V, T)) else [])
        return self.emit("dve", lambda e: e.scalar_tensor_tensor(_ap(out), _ap(a), _ap(s), _ap(b), op0, op1),
                         rd, [out])

    def copy(self, out, in_, eng="dve"):
        if eng == "act":
            return self.act(out, in_, AF.Copy)
        return self.emit(eng, lambda e: e.tensor_copy(_ap(out), _ap(in_)), [in_], [out])

    def memset(self, out, val, eng="dve"):
        return self.emit(eng, lambda e: e.memset(_ap(out), val), [], [out])

    def recip(self, out, in_):
        return self.emit("dve", lambda e: e.reciprocal(_ap(out), _ap(in_)), [in_], [out])


import math
import numpy as np

D = 1024
KC_D = 8
IN_W = 6856
NH = 4
DFF = 2816
KC_F = 22
O_AV, O_AG, O_Q, O_Z, O_BETA, O_DEC, O_QL, O_KVL, O_KPE, O_GATE = 0, 512, 1024, 2560, 3072, 3076, 3080, 3464, 3720, 3784
ALPHA = 4 ** 0.25
LN_EPS = 1e-5
RMS_EPS = 1e-6
SCALE_Q = 192 ** -0.5


def build_program(cfg, debug=()):
    SEQ, NS, PAST, TB, L = cfg["SEQ"], cfg["NS"], cfg["PAST"], cfg["TB"], cfg["L"]
    TS = 16
    NB = SEQ // TB
    G = 1 + NS
    nc = bass.Bass("TRN2", target_bir_lowering=False)
    P = Prog(nc)
    dbg_out = {}

    def din(name, shape, dt=F32):
        return P.dram(name, shape, dt, kind="ExternalInput")

    def dout(name, shape, dt=F32):
        return P.dram(name, shape, dt, kind="ExternalOutput")

    xpT = din("xpT", [D, SEQ]); xsT = din("xsT", [D, NS * TS]); cT = din("cT", [D, G])
    ln0p = din("ln0p", [128, 2, 8])
    w_ada = din("w_ada", [L, D, 6 * D]); b_ada_p = din("b_ada_p", [L, 128, 48])
    w_in = din("w_in", [L, D, IN_W]); w_kpp = din("w_kpp", [L, D, 64])
    cva_w = din("cva_w", [L, 128, 4, 31]); cva_v = din("cva_v", [L, 128, 3, 4])
    gcv_w = din("gcv_w", [L, 128, 12, 4]); gdn_s = din("gdn_s", [L, 4, 2]); gdn_ng = din("gdn_ng", [L, 128, 1])
    qn_g = din("qn_g", [L, 128, 3]); kvn_g = din("kvn_g", [L, 128, 2])
    w_uqn = din("w_uqn", [L, 384, 512]); w_uqr = din("w_uqr", [L, 384, 256]); w_uqp = din("w_uqp", [L, 384, 256])
    w_uk = din("w_uk", [L, 256, 512]); w_uv = din("w_uv", [L, 256, 512])
    w_br = din("w_br", [L, 3, 512, D]); w_out = din("w_out", [L, D, D])
    lnv = din("lnv", [L, 128, 4, 8])
    w_up = din("w_up", [L, D, 2 * DFF]); ffn_w = din("ffn_w", [L, 128, KC_F, 3]); ffn_b = din("ffn_b", [L, 128, KC_F])
    w_dn = din("w_dn", [L, DFF, D])
    clatT = din("clatT", [L, NS, 256, PAST]); ckpeT = din("ckpeT", [L, NS, 64, PAST])
    hA = din("hA", [L, NS, 512, 30]); hB = din("hB", [L, NS, 1536, 3]); sG = din("sG", [L, NS, 4, 128, 128])
    hF = din("hF", [L, NS, DFF, 2])
    c_ident = din("c_ident", [128, 128]); c_triu = din("c_triu", [128, 128])
    c_mtri = din("c_mtri", [128, 128]); c_mtriT = din("c_mtriT", [128, 128]); c_mstr = din("c_mstr", [128, 128])
    ropeP = din("ropeP", [2, 64, SEQ]); ropeS = din("ropeS", [2, 64, TS])
    ypT = dout("ypT", [D, SEQ]); ysT = dout("ysT", [D, NS * TS])
    o_platT = dout("o_platT", [L, 256, SEQ]); o_pkpeT = dout("o_pkpeT", [L, 64, SEQ])
    o_pcaT = dout("o_pcaT", [L, 512, 30]); o_pgcT = dout("o_pgcT", [L, 1536, 3]); o_pg = dout("o_pg", [L, 4, 128, 128])
    o_pfT = dout("o_pfT", [L, DFF, 2])
    o_slatT = dout("o_slatT", [L, NS, 256, TS]); o_skpeT = dout("o_skpeT", [L, NS, 64, TS])
    o_scaT = dout("o_scaT", [L, NS, 512, 30]); o_sgcT = dout("o_sgcT", [L, NS, 1536, 3]); o_sg = dout("o_sg", [L, NS, 4, 128, 128])
    o_sfT = dout("o_sfT", [L, NS, DFF, 2])

    class StopBuild(Exception):
        pass

    def milestone(name):
        if cfg.get("stop") == name:
            raise StopBuild()

    def dbg(name, v, shape, dt=F32):
        if name in debug and name not in dbg_out:
            t = dout("dbg_" + name, list(shape), dt)
            dbg_out[name] = t
            P.dma(t[:], v)

    def main_body():
        ident = P.sb([128, 128]); triu = P.sb([128, 128]); mtri = P.sb([128, 128]); mtriT = P.sb([128, 128]); mstr = P.sb([128, 128])
        ones_f = P.sb([128, 128]); ones_b = P.sb([128, 128], BF16)
        for t, s in ((ident, c_ident), (triu, c_triu), (mtri, c_mtri), (mtriT, c_mtriT), (mstr, c_mstr)):
            P.dma(t[:], s[:])
        P.memset(ones_f[:], 1.0); P.memset(ones_b[:], 1.0)
        def b4(t, C):
            return t[0:C, 0:C].unsqueeze(1).to_broadcast([C, 4, C])
        ln0s = P.sb([128, 2, 8]); P.dma(ln0s[:], ln0p[:])

        PSN = 6
        pspool = [P.ps([128, 512], name="psg%d" % i) for i in range(PSN)]
        ps_o = P.ps([128, 512], name="ps_o"); ps_s = P.ps([128, 512], name="ps_s")
        psi = [0]

        def pst():
            t = pspool[psi[0] % PSN]
            psi[0] += 1
            return t
        psa = [0]; psb = [0]

        def pstA():
            t = pspool[psa[0] % 3]
            psa[0] += 1
            return t

        def pstB():
            t = pspool[3 + psb[0] % 3]
            psb[0] += 1
            return t

        W = {}
        stg_f = []
        stg_b = []
        prep_i = [0]

        def prep_w(name, src, K, N, Mc=128):
            KC = K // 128
            NM = (N + Mc - 1) // Mc
            scr = P.dram("wc_" + name, [NM, 128, KC * Mc], BF16)
            for mi in range(NM):
                m0 = mi * Mc
                mc = min(Mc, N - m0)
                i = prep_i[0]; prep_i[0] += 1
                sf = stg_f[i % 2]; sbf = stg_b[i % 2]
                q = "sp" if i % 2 == 0 else "pool"
                P.dma(sf[:, 0:KC * mc].rearrange("p (kc m) -> p kc m", kc=KC),
                      src[:, m0:m0 + mc].rearrange("(kc p) m -> p kc m", p=128), q=q)
                ce = ("dve", "act")[i % 2]
                P.copy(sbf[:, 0:KC * mc], sf[:, 0:KC * mc], eng=ce)
                P.dma(scr[mi, :, 0:KC * mc], sbf[:, 0:KC * mc], q=q)
            W[name] = (scr, KC, Mc, N)

        wbufs = [P.sb([128, 8 * 128], BF16, name="wbuf%d" % i) for i in range(8)]
        wbi = [0]

        def load_w(name, mi):
            scr, KC, Mc, N = W[name]
            mc = min(Mc, N - mi * Mc)
            i = wbi[0]; wbi[0] += 1
            wb = wbufs[i % 8]
            P.dma(wb[:, 0:KC * mc], scr[mi, :, 0:KC * mc], q="sp")
            return wb[:, 0:KC * mc].rearrange("p (kc m) -> p kc m", kc=KC), KC, mc

        def lin(name, mi, rhs, nt):
            parts = name if isinstance(name, list) else [(name, 0)]
            ps = pst()
            np_ = len(parts)
            for pi, (nm, ko) in enumerate(parts):
                wv, KC, mc = load_w(nm, mi)
                for kc in range(KC):
                    P.mm(ps[0:mc, 0:nt], wv[:, kc, :], rhs[:, ko + kc, 0:nt], start=(pi == 0 and kc == 0),
                         stop=(pi == np_ - 1 and kc == KC - 1))
            return ps[0:mc, 0:nt]

        X = P.sb([128, 8, TB]); X1 = P.sb([128, 8, TB]); Hb = P.sb([128, 8, TB], BF16)
        SQ = P.sb([128, 8, TB])
        U = P.sb([128, 4, 30 + TB]); QKVP = P.sb([128, 12, 3 + TB]); QKV = P.sb([128, 12, TB])
        Zs = P.sb([128, 4, TB]); G3 = P.sb([128, 24, TB], BF16)
        BE = P.sb([4, TB]); GG = P.sb([4, TB])
        YA = P.sb([128, 4, TB], BF16); YB = P.sb([128, 4, TB], BF16); YC = P.sb([128, 4, TB], BF16)
        QLn = P.sb([128, 3, TB], BF16); QLf = P.sb([128, 3, TB])
        LAT = P.sb([128, 2, TB]); LATb = P.sb([128, 2, TB], BF16)
        KPE = P.sb([64, TB]); KPEb = P.sb([65, TB], BF16)
        Qn = P.sb([128, 4, TB], BF16); QR = P.sb([65, 4, TB], BF16); QNf = P.sb([128, 4, TB]); QRf = P.sb([64, 4, TB])
        KNf = P.sb([128, TB]); KNb = P.sb([128, 4, TB], BF16); VVb = P.sb([128, 4 * max(1, TB // 128), 128], BF16)
        Mg = P.sb([128, 8, TB], BF16); Gf = G3
        Aj = [P.sb([128, 2 + TB]) for _ in range(2)]
        tA = [P.sb([128, TB]) for _ in range(4)]
        tAi = [0]

        def tmp():
            t = tA[tAi[0] % 4]
            tAi[0] += 1
            return t
        rowk = P.sb([1, 4])
        ropeT = P.sb([64, 2, TB])
        knt = [P.sb([128, 512], BF16) for _ in range(3)]; krt = [P.sb([65, 512], BF16) for _ in range(3)]
        vts = [P.sb([128, 4, 128], BF16) for _ in range(3)]; pts = [P.sb([128, 512], BF16) for _ in range(3)]
        avi = [0]
        accS = P.sb([128, TB])

        stg_f.extend([QKVP[:, :, :].rearrange("p a b -> p (a b)"), QKV[:, :, :].rearrange("p a b -> p (a b)")])
        stg_b.extend([Mg[:, :, :].rearrange("p a b -> p (a b)"), G3[:, :, :].rearrange("p a b -> p (a b)")])
        for l in range(L):
            prep_w("ada%d" % l, w_ada[l], D, 6 * D)
            prep_w("inA%d" % l, w_in[l][:, 0:3072], D, 3072)
            prep_w("inbeta%d" % l, w_in[l][:, O_BETA:O_BETA + 4], D, 4, Mc=4)
            prep_w("indec%d" % l, w_in[l][:, O_DEC:O_DEC + 4], D, 4, Mc=4)
            prep_w("inql%d" % l, w_in[l][:, O_QL:O_QL + 384], D, 384)
            prep_w("inkvl%d" % l, w_in[l][:, O_KVL:O_KVL + 256], D, 256)
            prep_w("inkpe%d" % l, w_in[l][:, O_KPE:O_KPE + 64], D, 64, Mc=64)
            prep_w("inkpp%d" % l, w_kpp[l], D, 64, Mc=64)
            prep_w("ingate%d" % l, w_in[l][:, O_GATE:O_GATE + 3072], D, 3072)
            prep_w("uqn%d" % l, w_uqn[l], 384, 512)
            prep_w("uqr%d" % l, w_uqr[l], 384, 256, Mc=64)
            prep_w("uqp%d" % l, w_uqp[l], 384, 256, Mc=64)
            prep_w("uk%d" % l, w_uk[l], 256, 512)
            for n in range(3):
                prep_w("br%d_%d" % (l, n), w_br[l, n], 512, D)
            prep_w("out%d" % l, w_out[l], D, D)
            prep_w("up%d" % l, w_up[l], D, 2 * DFF)
            prep_w("dna%d" % l, w_dn[l][0:1024, :], 1024, D)
            prep_w("dnb%d" % l, w_dn[l][1024:2048, :], 1024, D)
            prep_w("dnc%d" % l, w_dn[l][2048:2816, :], 768, D)

        milestone("prep")
        cvaw = [P.sb([128, 4, 31]) for _ in range(L)]; cvav = [P.sb([128, 3, 4]) for _ in range(L)]
        gcvw = [P.sb([128, 12, 4]) for _ in range(L)]; gdns = [P.sb([4, 2]) for _ in range(L)]
        nexpA = [P.sb([4, 1]) for _ in range(L)]; gdnng = [P.sb([128, 1]) for _ in range(L)]
        qng = [P.sb([128, 3]) for _ in range(L)]; kvng = [P.sb([128, 2]) for _ in range(L)]
        lnvs = [P.sb([128, 4, 8]) for _ in range(L)]; ffnw = [P.sb([128, KC_F, 3]) for _ in range(L)]
        ffnb = [P.sb([128, KC_F]) for _ in range(L)]
        wuv = [P.sb([128, 2, 512], BF16) for _ in range(L)]
        modT = [P.sb([128, 48, G]) for _ in range(L)]
        scp1 = [P.sb([128, 8, G]) for _ in range(L)]; scp2 = [P.sb([128, 8, G]) for _ in range(L)]
        g1p = [P.sb([128, 8, G]) for _ in range(L)]; g2p = [P.sb([128, 8, G]) for _ in range(L)]
        badas = P.sb([128, 48])
        csT = P.sb([128, 8, G]); csb = P.sb([128, 8, G], BF16)
        P.dma(csT[:], cT[:, :].rearrange("(kc p) g -> p kc g", p=128))
        P.act(csb[:], csT[:], AF.Silu)
        for l in range(L):
            P.dma(cvaw[l][:], cva_w[l]); P.dma(cvav[l][:], cva_v[l]); P.dma(gcvw[l][:], gcv_w[l])
            P.dma(gdns[l][:], gdn_s[l]); P.dma(gdnng[l][:], gdn_ng[l]); P.dma(qng[l][:], qn_g[l]); P.dma(kvng[l][:], kvn_g[l])
            P.dma(lnvs[l][:], lnv[l]); P.dma(ffnw[l][:], ffn_w[l]); P.dma(ffnb[l][:], ffn_b[l])
            P.act(nexpA[l][:], gdns[l][:, 0:1], AF.Exp)
            P.ts(nexpA[l][:], nexpA[l][:], -1.0, None, ALU.mult)
            sf = stg_f[0]
            P.dma(sf[:, 0:1024].rearrange("p (kc m) -> p kc m", kc=2), w_uv[l].rearrange("(kc p) m -> p kc m", p=128))
            P.copy(wuv[l][:], sf[:, 0:1024].rearrange("p (kc m) -> p kc m", kc=2))
            P.dma(badas[:], b_ada_p[l])
            for j in range(48):
                ps = lin("ada%d" % l, j, csb, G)
                P.ts(modT[l][:, j, :], ps, badas[:, j:j + 1], None, ALU.add)
            P.ts(scp1[l][:], modT[l][:, 8:16, :], 1.0, None, ALU.add)
            P.ts(g1p[l][:], modT[l][:, 16:24, :], 1.0, None, ALU.add)
            P.ts(scp2[l][:], modT[l][:, 32:40, :], 1.0, None, ALU.add)
            P.ts(g2p[l][:], modT[l][:, 40:48, :], 1.0, None, ALU.add)

        class St:
            pass

        def mkstate(tk, name):
            s = St()
            s.Uh = P.sb([128, 4, 30]); s.Qh = P.sb([128, 12, 3]); s.S = P.sb([128, 4, 128]); s.Ah = P.sb([128, KC_F, 2])
            s.Kb2 = P.sb([1, 4])
            s.KN = P.dram("KN_" + name, [4, 128, tk], BF16); s.KR = P.dram("KR_" + name, [65, tk], BF16)
            s.VV = P.dram("VV_" + name, [4, tk, 128], BF16)
            return s

        stP = [mkstate(SEQ, "p%d" % l) for l in range(L)]
        stS = mkstate(PAST + TS, "s")

        def colstats(Xt, nch, nt, need_mean):
            npart = Xt.shape[0]
            F = float(nch * npart)
            p1 = None
            if need_mean:
                p1 = pst()
                for ch in range(nch):
                    P.mm(p1[:, 0:nt], ones_f[0:npart, :], Xt[:, ch, 0:nt], start=(ch == 0), stop=(ch == nch - 1))
            p2 = pst()
            for ch in range(nch):
                P.act(SQ[0:npart, ch, 0:nt], Xt[:, ch, 0:nt], AF.Square)
            for ch in range(nch):
                P.mm(p2[:, 0:nt], ones_f[0:npart, :], SQ[0:npart, ch, 0:nt], start=(ch == 0), stop=(ch == nch - 1))
            mean = None
            var = tmp()
            if need_mean:
                mean = tmp()
                P.act(mean[:, 0:nt], p1[:, 0:nt], AF.Copy, scale=1.0 / F)
                msq = tmp()
                P.tt(msq[:, 0:nt], mean[:, 0:nt], mean[:, 0:nt], ALU.mult)
                P.stt(var[:, 0:nt], p2[:, 0:nt], 1.0 / F, msq[:, 0:nt], ALU.mult, ALU.subtract)
                eps = LN_EPS
            else:
                P.act(var[:, 0:nt], p2[:, 0:nt], AF.Copy, scale=1.0 / F)
                eps = RMS_EPS
            P.act(var[:, 0:nt], var[:, 0:nt], AF.Ln, bias=eps)
            rstd = tmp()
            P.act(rstd[:, 0:nt], var[:, 0:nt], AF.Exp, scale=-0.5)
            return mean, rstd

        def ln_fm(dst, Xt, nch, nt, g_v, b_v):
            mean, rstd = colstats(Xt, nch, nt, True)
            mb = mean[:, 0:nt].unsqueeze(1).to_broadcast([128, nch, nt])
            rb = rstd[:, 0:nt].unsqueeze(1).to_broadcast([128, nch, nt])
            P.tt(SQ[:, 0:nch, 0:nt], Xt[:, 0:nch, 0:nt], mb, ALU.subtract)
            P.tt(SQ[:, 0:nch, 0:nt], SQ[:, 0:nch, 0:nt], rb, ALU.mult)
            for ch in range(nch):
                P.ts(dst[:, ch, 0:nt], SQ[:, ch, 0:nt], g_v[:, ch:ch + 1], b_v[:, ch:ch + 1], ALU.mult, ALU.add)

        milestone("params")
        C4 = lambda: P.sb([128, 4, 128])
        g_ktm = C4(); g_vtm = C4(); g_e1 = C4(); g_Dm = C4(); g_DmT = C4(); g_N = C4(); g_At = C4()
        g_A2 = C4(); g_At2 = C4(); g_Rt = C4(); g_qkT = C4(); g_expGb = C4(); g_gB = C4()
        g_bv = g_e1; g_bk = g_Dm; g_kd = g_DmT; g_u = g_gB; g_wT = g_N; g_vnew = g_At; g_o = g_A2; g_t = g_At2
        g_btm = P.sb([128, 4]); g_gtm = P.sb([128, 4]); g_nb = P.sb([128, 4]); g_Gcol = P.sb([128, 4]); g_eG = P.sb([128, 4])
        g_beG = P.sb([128, 4]); g_kdc = P.sb([128, 4]); g_gam = P.sb([128, 4])

        def gdn_chunk(l, st, c0, C):
            pk = pstA(); pv = pstA()
            for h in range(4):
                P.tr(pk[0:C, h * 128:(h + 1) * 128], QKV[:, 4 + h, c0:c0 + C], ident[:])
                P.tr(pv[0:C, h * 128:(h + 1) * 128], QKV[:, 8 + h, c0:c0 + C], ident[:])
            P.copy(g_ktm[0:C].rearrange("p h d -> p (h d)"), pk[0:C, :], eng="act")
            P.copy(g_vtm[0:C].rearrange("p h d -> p (h d)"), pv[0:C, :])
            pb = pstA()
            P.tr(pb[0:C, 0:4], BE[0:4, c0:c0 + C], ident[0:4, 0:4])
            P.tr(pb[0:C, 4:8], GG[0:4, c0:c0 + C], ident[0:4, 0:4])
            P.copy(g_btm[0:C, :], pb[0:C, 0:4]); P.copy(g_gtm[0:C, :], pb[0:C, 4:8], eng="act")
            P.ts(g_nb[0:C, :], g_btm[0:C, :], -1.0, None, ALU.mult)
            milestone("g1")
            yield
            P.tt(g_gB[0:C], ones_f[0:C, :].unsqueeze(1).to_broadcast([C, 4, 128]), g_gtm[0:C, :].bc(128), ALU.mult)
            pG = pstA(); pc = pstA()
            for h in range(4):
                P.mm(pG[:, h * 128:h * 128 + C], g_gB[0:C, h, :], triu[0:C, 0:C], True, True)
            P.mm(pc[0:C, 0:4], triu[0:C, 0:C], g_gtm[0:C, 0:4], True, True)
            milestone("g2")
            yield
            P.copy(g_Gcol[0:C, :], pc[0:C, 0:4])
            pGv = pG[:, :].rearrange("p (h j) -> p h j", h=4)
            P.act(g_expGb[:, :, 0:C], pGv[:, :, 0:C], AF.Exp)
            P.act(g_eG[0:C, :], g_Gcol[0:C, :], AF.Exp)
            P.tt(g_beG[0:C, :], g_btm[0:C, :], g_eG[0:C, :], ALU.mult)
            P.copy(g_gam[:, :], g_expGb[:, :, C - 1])
            P.copy(g_kdc[0:C, :], pGv[0:C, :, C - 1], eng="act")
            P.tt(g_kdc[0:C, :], g_kdc[0:C, :], g_Gcol[0:C, :], ALU.subtract)
            P.act(g_kdc[0:C, :], g_kdc[0:C, :], AF.Exp)
            P.tt(g_e1[0:C, :, 0:C], pGv[0:C, :, 0:C], g_Gcol[0:C, :].bc(C), ALU.subtract)
            P.ts(g_Dm[0:C, :, 0:C], g_e1[0:C, :, 0:C], 0.0, None, ALU.max)
            P.act(g_Dm[0:C, :, 0:C], g_Dm[0:C, :, 0:C], AF.Exp, scale=-1.0)
            P.ts(g_DmT[0:C, :, 0:C], g_e1[0:C, :, 0:C], 0.0, None, ALU.min)
            P.act(g_DmT[0:C, :, 0:C], g_DmT[0:C, :, 0:C], AF.Exp)
            P.tt(g_DmT[0:C, :, 0:C], g_DmT[0:C, :, 0:C], b4(mtriT, C), ALU.mult)
            P.tt(g_Dm[0:C, :, 0:C], g_Dm[0:C, :, 0:C], b4(mstr, C), ALU.mult)
            milestone("g3")
            yield
            pkk = pstA(); pqk = pstA()
            for h in range(4):
                P.mm(pkk[0:C, h * 128:h * 128 + C], QKV[:, 4 + h, c0:c0 + C], QKV[:, 4 + h, c0:c0 + C], True, True)
                P.mm(pqk[0:C, h * 128:h * 128 + C], QKV[:, 4 + h, c0:c0 + C], QKV[:, h, c0:c0 + C], True, True)
            pkkv = pkk[:, :].rearrange("p (h j) -> p h j", h=4); pqkv = pqk[:, :].rearrange("p (h j) -> p h j", h=4)
            P.tt(g_N[0:C, :, 0:C], pkkv[0:C, :, 0:C], g_Dm[0:C, :, 0:C], ALU.mult)
            P.tt(g_N[0:C, :, 0:C], g_N[0:C, :, 0:C], g_nb[0:C, :].bc(C), ALU.mult)
            P.tt(g_qkT[0:C, :, 0:C], pqkv[0:C, :, 0:C], g_DmT[0:C, :, 0:C], ALU.mult)
            milestone("g4")
            yield
            pt_ = pstA()
            for h in range(4):
                P.tr(pt_[0:C, h * 128:h * 128 + C], g_N[0:C, h, 0:C], ident[0:C, 0:C])
            ptv = pt_[:, :].rearrange("p (h j) -> p h j", h=4)
            milestone("g4t")
            P.copy(g_At[0:C, :, 0:C], ptv[0:C, :, 0:C], eng="act")
            milestone("g4c")
            P.tt(g_Rt[0:C, :, 0:C], g_At[0:C, :, 0:C], b4(ident, C), ALU.add)
            milestone("g4a")
            yield
            A, At, A2, At2 = g_N, g_At, g_A2, g_At2
            nsq = int(round(math.log2(C))) - 1
            for m in range(nsq):
                pa = pstA(); pat = pstA()
                for h in range(4):
                    P.mm(pa[0:C, h * 128:h * 128 + C], At[0:C, h, 0:C], A[0:C, h, 0:C], True, True)
                    P.mm(pat[0:C, h * 128:h * 128 + C], A[0:C, h, 0:C], At[0:C, h, 0:C], True, True)
                P.copy(A2[0:C, :, 0:C], pa[:, :].rearrange("p (h j) -> p h j", h=4)[0:C, :, 0:C], eng="act")
                P.copy(At2[0:C, :, 0:C], pat[:, :].rearrange("p (h j) -> p h j", h=4)[0:C, :, 0:C])
                A, At, A2, At2 = A2, At2, A, At
                yield
                pr = pstA()
                for h in range(4):
                    P.mm(pr[0:C, h * 128:h * 128 + C], A[0:C, h, 0:C], g_Rt[0:C, h, 0:C], True, True)
                P.tt(g_Rt[0:C, :, 0:C], g_Rt[0:C, :, 0:C], pr[:, :].rearrange("p (h j) -> p h j", h=4)[0:C, :, 0:C], ALU.add)
                milestone("g4b")
                yield
            milestone("g5")
            yield
            P.tt(g_bv[0:C], g_vtm[0:C], g_btm[0:C, :].bc(128), ALU.mult)
            P.tt(g_bk[0:C], g_ktm[0:C], g_beG[0:C, :].bc(128), ALU.mult)
            P.tt(g_kd[0:C], g_ktm[0:C], g_kdc[0:C, :].bc(128), ALU.mult)
            pu = pstA(); pw = pstA()
            for h in range(4):
                P.mm(pu[0:C, h * 128:(h + 1) * 128], g_Rt[0:C, h, 0:C], g_bv[0:C, h, :], True, True)
                P.mm(pw[:, h * 128:h * 128 + C], g_bk[0:C, h, :], g_Rt[0:C, h, 0:C], True, True)
            P.copy(g_u[0:C].rearrange("p h d -> p (h d)"), pu[0:C, :], eng="act")
            P.copy(g_wT[:, :, 0:C], pw[:, :].rearrange("p (h j) -> p h j", h=4)[:, :, 0:C])
            milestone("g6")
            yield
            pws = pstA()
            for h in range(4):
                P.mm(pws[0:C, h * 128:(h + 1) * 128], g_wT[:, h, 0:C], st.S[:, h, :], True, True)
            P.tt(g_vnew[0:C].rearrange("p h d -> p (h d)"), g_u[0:C].rearrange("p h d -> p (h d)"), pws[0:C, :], ALU.subtract)
            yield
            po1 = pstA(); po2 = pstA(); psn = pstA()
            for h in range(4):
                P.mm(po1[:, h * 128:h * 128 + C], st.S[:, h, :], QKV[:, h, c0:c0 + C], True, True)
                P.mm(po2[:, h * 128:h * 128 + C], g_vnew[0:C, h, :], g_qkT[0:C, h, 0:C], True, True)
                P.mm(psn[:, h * 128:(h + 1) * 128], g_kd[0:C, h, :], g_vnew[0:C, h, :], True, True)
            P.tt(g_t[:, :, 0:C], po1[:, :].rearrange("p (h j) -> p h j", h=4)[:, :, 0:C], g_expGb[:, :, 0:C], ALU.mult)
            P.tt(g_o[:, :, 0:C], g_t[:, :, 0:C], po2[:, :].rearrange("p (h j) -> p h j", h=4)[:, :, 0:C], ALU.add)
            for h in range(4):
                P.stt(st.S[:, h, :], st.S[:, h, :], g_gam[:, h:h + 1], psn[:, h * 128:(h + 1) * 128], ALU.mult, ALU.add)
            milestone("g7")
            yield
            P.act(g_t[:, :, 0:C], g_o[:, :, 0:C], AF.Square)
            pss = pstA()
            for h in range(4):
                P.mm(pss[:, h * 128:h * 128 + C], ones_f[:, :], g_t[:, h, 0:C], True, True)
            pssv = pss[:, :].rearrange("p (h j) -> p h j", h=4)
            P.act(g_t[:, :, 0:C], pssv[:, :, 0:C], AF.Ln, bias=RMS_EPS, scale=1.0 / 128.0)
            P.act(g_t[:, :, 0:C], g_t[:, :, 0:C], AF.Exp, scale=-0.5)
            P.tt(g_o[:, :, 0:C], g_o[:, :, 0:C], g_t[:, :, 0:C], ALU.mult)
            P.ts(g_o[:, :, 0:C], g_o[:, :, 0:C], gdnng[l][:, 0:1], None, ALU.mult)
            P.tt(YB[:, :, c0:c0 + C], g_o[:, :, 0:C], Zs[:, :, c0:c0 + C], ALU.mult)

        def attention(st, nt, groups):
            for h in range(4):
                first = True
                nvis = sum((nk + 127) // 128 for (_, nk, _) in groups)
                vi = 0
                pend = None
                P.memset(accS[:, 0:nt], 0.0, eng="pool")

                def flush(pd):
                    vt_, r_, kk_, c0_, pt_, fi_, la_ = pd
                    P.mm(ps_o[:, c0_:nt], vt_[0:kk_, r_, :], pt_[0:kk_, c0_:nt], fi_, la_)
                    P.tt(accS[0:kk_, c0_:nt], accS[0:kk_, c0_:nt], pt_[0:kk_, c0_:nt], ALU.add, eng="pool")
                for (k0, nk, diag) in groups:
                    i = avi[0]; avi[0] += 1
                    kn = knt[i % 3]; kr = krt[i % 3]; vt = vts[i % 3]
                    P.dma(kn[:, 0:nk], st.KN[h, :, k0:k0 + nk], q="sp")
                    P.dma(kr[:, 0:nk], st.KR[:, k0:k0 + nk], q="sp")
                    nb_ = (nk + 127) // 128
                    if nk >= 128:
                        P.dma(vt[:, 0:nb_, :], st.VV[h, k0:k0 + nk, :].rearrange("(b p) d -> p b d", p=128), q="sp")
                    else:
                        P.dma(vt[0:nk, 0, :], st.VV[h, k0:k0 + nk, :], q="sp")
                    for r in range(nb_):
                        kk = min(128, nk - 128 * r)
                        c0 = 128 * r if diag else 0
                        if c0 >= nt:
                            vi += 1
                            continue
                        sc = pstB()
                        P.mm(sc[0:kk, c0:nt], kn[:, 128 * r:128 * r + kk], Qn[:, h, c0:nt], True, False)
                        P.mm(sc[0:kk, c0:nt], kr[0:65, 128 * r:128 * r + kk], QR[0:65, h, c0:nt], False, True)
                        pt = pts[vi % 3]
                        P.act(pt[0:kk, c0:nt], sc[0:kk, c0:nt], AF.Exp)
                        if diag and kk > 64:
                            P.memset(pt[64:kk, c0:min(nt, c0 + 64)], 0.0, eng="pool")
                        last = (vi == nvis - 1)
                        if pend is not None:
                            flush(pend)
                        pend = (vt, r, kk, c0, pt, first, last)
                        first = False
                        vi += 1
                        yield
                if pend is not None:
                    flush(pend)
                P.mm(ps_s[:, 0:nt], ones_f[:, :], accS[:, 0:nt], True, True)
                rs = tmp()
                P.recip(rs[:, 0:nt], ps_s[:, 0:nt])
                P.tt(YC[:, h, 0:nt], ps_o[:, 0:nt], rs[:, 0:nt], ALU.mult)

        def interleave(gens):
            gens = list(gens)
            while gens:
                for gi in list(gens):
                    try:
                        next(gi)
                    except StopIteration:
                        gens.remove(gi)

        def knorm(l, st, nt, k0):
            for h in range(4):
                pkh = lin("uk%d" % l, h, LATb, nt)
                P.copy(KNb[:, h, 0:nt], pkh, eng="act")
                P.act(KNf[:, 0:nt], pkh, AF.Square)
                pk2 = pst()
                P.mm(pk2[0:1, 0:nt], ones_f[:, 0:1], KNf[:, 0:nt], True, False)
                P.mm(pk2[0:1, 0:nt], ones_f[0:64, 0:1], SQ[0:64, 0, 0:nt], False, True)
                P.dma(st.KN[h, :, k0:k0 + nt], KNb[:, h, 0:nt], q="pool")
                P.emit("dve", lambda e, pk2=pk2, h=h, nt=nt: e.tensor_reduce(
                    out=_ap(rowk[0:1, h:h + 1]), in_=_ap(pk2[0:1, 0:nt]), op=ALU.max, axis=mybir.AxisListType.X),
                    [pk2], [rowk])
            P.tt(st.Kb2[0:1, :], st.Kb2[0:1, :], rowk[0:1, :], ALU.max)

        def block_layer(l, st, nt, g, pos_rope, t0, cachek0, groups, outs, Cg):
            for kc in range(8):
                P.act(Hb[:, kc, 0:nt], X[:, kc, 0:nt], AF.Identity, bias=modT[l][:, kc, g:g + 1], scale=scp1[l][:, kc, g:g + 1])
            P.copy(U[:, :, 0:30], st.Uh[:, :, :])
            for j in range(4):
                pv = lin("inA%d" % l, j, Hb, nt)
                pg = lin("inA%d" % l, 4 + j, Hb, nt)
                sg = tmp()
                P.act(sg[:, 0:nt], pg, AF.Sigmoid)
                P.tt(U[:, j, 30:30 + nt], pv, sg[:, 0:nt], ALU.mult)
            P.copy(st.Uh[:, :, :], U[:, :, nt:nt + 30])
            milestone("A")
            P.copy(QKVP[:, :, 0:3], st.Qh[:, :, :], eng="pool")
            for j in range(12):
                pq = lin("inA%d" % l, 8 + j, Hb, nt)
                P.copy(QKVP[:, j, 3:3 + nt], pq, eng=("act" if j % 2 else "dve"))
            P.copy(st.Qh[:, :, :], QKVP[:, :, nt:nt + 3], eng="pool")
            for j in range(4):
                pz = lin("inA%d" % l, 20 + j, Hb, nt)
                P.act(Zs[:, j, 0:nt], pz, AF.Silu)
            pb = lin("inbeta%d" % l, 0, Hb, nt)
            P.act(BE[0:4, 0:nt], pb, AF.Sigmoid)
            pd = lin("indec%d" % l, 0, Hb, nt)
            P.act(GG[0:4, 0:nt], pd, AF.Exp, bias=gdns[l][:, 1:2])
            P.act(GG[0:4, 0:nt], GG[0:4, 0:nt], AF.Ln, bias=1.0)
            P.ts(GG[0:4, 0:nt], GG[0:4, 0:nt], nexpA[l][:, 0:1], None, ALU.mult)
            milestone("B")
            for j in range(3):
                pq = lin("inql%d" % l, j, Hb, nt)
                P.copy(QLf[:, j, 0:nt], pq, eng=("act" if j % 2 else "dve"))
            _, rstd = colstats(QLf, 3, nt, False)
            P.tt(QLf[:, :, 0:nt], QLf[:, :, 0:nt], rstd[:, 0:nt].unsqueeze(1).to_broadcast([128, 3, nt]), ALU.mult)
            P.tt(QLn[:, :, 0:nt], QLf[:, :, 0:nt], qng[l][:, :].bc(nt), ALU.mult)
            for j in range(2):
                pq = lin("inkvl%d" % l, j, Hb, nt)
                P.copy(LAT[:, j, 0:nt], pq, eng=("act" if j % 2 else "dve"))
            _, rstd = colstats(LAT, 2, nt, False)
            P.tt(LAT[:, :, 0:nt], LAT[:, :, 0:nt], rstd[:, 0:nt].unsqueeze(1).to_broadcast([128, 2, nt]), ALU.mult)
            P.tt(LAT[:, :, 0:nt], LAT[:, :, 0:nt], kvng[l][:, :].bc(nt), ALU.mult)
            P.copy(LATb[:, :, 0:nt], LAT[:, :, 0:nt], eng="pool")
            P.dma(outs["lat"][:, t0:t0 + nt].rearrange("(kc p) t -> p kc t", p=128), LAT[:, :, 0:nt], q="pool")
            P.dma(ropeT[:, :, 0:nt], pos_rope.rearrange("a d t -> d a t"))
            pk1 = lin("inkpe%d" % l, 0, Hb, nt)
            pk2 = lin("inkpp%d" % l, 0, Hb, nt)
            t1 = tmp(); t2 = tmp()
            P.tt(t1[0:64, 0:nt], pk1, ropeT[:, 0, 0:nt], ALU.mult)
            P.tt(t2[0:64, 0:nt], pk2, ropeT[:, 1, 0:nt], ALU.mult)
            P.tt(KPE[:, 0:nt], t1[0:64, 0:nt], t2[0:64, 0:nt], ALU.add)
            P.dma(outs["kpe"][:, t0:t0 + nt], KPE[:, 0:nt], q="pool")
            P.copy(KPEb[0:64, 0:nt], KPE[:, 0:nt], eng="pool")
            P.memset(KPEb[64:65, 0:nt], 1.0, eng="pool")
            P.dma(st.KR[:, cachek0:cachek0 + nt], KPEb[:, 0:nt], q="pool")
            milestone("C")
            for j in range(24):
                pgt = lin("ingate%d" % l, j, Hb, nt)
                P.act(G3[:, j, 0:nt], pgt, AF.Sigmoid)
            milestone("gates")
            acc = SQ
            for ch in range(4):
                P.ts(acc[:, 4 + ch, 0:nt], U[:, ch, 0:nt], cvaw[l][:, ch, 0:1], None, ALU.mult)
                for k in range(1, 31):
                    P.stt(acc[:, 4 + ch, 0:nt], U[:, ch, k:k + nt], cvaw[l][:, ch, k:k + 1], acc[:, 4 + ch, 0:nt], ALU.mult, ALU.add)
                P.ts(acc[:, 4 + ch, 0:nt], acc[:, 4 + ch, 0:nt], cvav[l][:, 0, ch:ch + 1], None, ALU.add)
            P.copy(X1[:, 0:4, 0:nt], acc[:, 4:8, 0:nt], eng="pool")
            ln_fm(X1[:, 4:8, :], X1[:, 0:4, :], 4, nt, cvav[l][:, 1, :], cvav[l][:, 2, :])
            P.act(YA[:, :, 0:nt], X1[:, 4:8, 0:nt], AF.Silu)
            milestone("convA")
            for j in range(12):
                P.ts(QKV[:, j, 0:nt], QKVP[:, j, 0:nt], gcvw[l][:, j, 0:1], None, ALU.mult)
                for k in range(1, 4):
                    P.stt(QKV[:, j, 0:nt], QKVP[:, j, k:k + nt], gcvw[l][:, j, k:k + 1], QKV[:, j, 0:nt], ALU.mult, ALU.add)
            P.act(QKV[:, :, 0:nt], QKV[:, :, 0:nt], AF.Silu)
            P.act(SQ[:, 0:8, 0:nt], QKV[:, 0:8, 0:nt], AF.Square)
            for j in range(8):
                pn = pst()
                P.mm(pn[:, 0:nt], ones_f[:, :], SQ[:, j, 0:nt], True, True)
                rn = tmp()
                P.act(rn[:, 0:nt], pn[:, 0:nt], AF.Ln, bias=RMS_EPS)
                P.act(rn[:, 0:nt], rn[:, 0:nt], AF.Exp, scale=-0.5)
                if j < 4:
                    P.stt(QKV[:, j, 0:nt], QKV[:, j, 0:nt], 128 ** -0.5, rn[:, 0:nt], ALU.mult, ALU.mult)
                else:
                    P.tt(QKV[:, j, 0:nt], QKV[:, j, 0:nt], rn[:, 0:nt], ALU.mult)
            milestone("gdnprep")
            milestone("gdn")
            for h in range(4):
                pqn = lin("uqn%d" % l, h, QLn, nt)
                P.act(QNf[:, h, 0:nt], pqn, AF.Copy, scale=SCALE_Q)
                pqr = lin("uqr%d" % l, h, QLn, nt)
                pqp = lin("uqp%d" % l, h, QLn, nt)
                t1 = tmp(); t2 = tmp()
                P.tt(t1[0:64, 0:nt], pqr, ropeT[:, 0, 0:nt], ALU.mult)
                P.tt(t2[0:64, 0:nt], pqp, ropeT[:, 1, 0:nt], ALU.mult)
                P.stt(QRf[:, h, 0:nt], t1[0:64, 0:nt], 1.0, t2[0:64, 0:nt], ALU.mult, ALU.add)
            P.ts(QRf[:, :, 0:nt], QRf[:, :, 0:nt], SCALE_Q, None, ALU.mult)
            P.copy(Qn[:, :, 0:nt], QNf[:, :, 0:nt], eng="pool")
            P.copy(QR[0:64, :, 0:nt], QRf[:, :, 0:nt], eng="pool")
            P.act(SQ[0:64, 0, 0:nt], KPE[:, 0:nt], AF.Square)
            knorm(l, st, nt, cachek0)
            nsb = (nt + 127) // 128
            for sbk in range(nsb):
                kk = min(128, nt - 128 * sbk)
                pvv = pst()
                for kc in range(2):
                    P.mm(pvv[0:kk, :], LATb[:, kc, 128 * sbk:128 * sbk + kk], wuv[l][:, kc, :], kc == 0, kc == 1)
                P.copy(VVb[0:kk, 4 * sbk:4 * sbk + 4, :].rearrange("p h d -> p (h d)"), pvv[0:kk, :], eng="act")
                for h in range(4):
                    P.dma(st.VV[h, cachek0 + 128 * sbk:cachek0 + 128 * sbk + kk, :], VVb[0:kk, 4 * sbk + h, :],
                          q=("sp", "pool")[h % 2])
            P.act(SQ[:, 0:4, 0:nt], QNf[:, :, 0:nt], AF.Square)
            P.act(SQ[0:64, 4:8, 0:nt], QRf[:, :, 0:nt], AF.Square)
            for h in range(4):
                pq2 = pst()
                P.mm(pq2[0:1, 0:nt], ones_f[:, 0:1], SQ[:, h, 0:nt], True, False)
                P.mm(pq2[0:1, 0:nt], ones_f[0:64, 0:1], SQ[0:64, 4 + h, 0:nt], False, True)
                tq = tmp()
                P.ts(tq[0:1, 0:nt], pq2[0:1, 0:nt], st.Kb2[0:1, h:h + 1], None, ALU.mult)
                P.act(tq[0:1, 0:nt], tq[0:1, 0:nt], AF.Sqrt)
                rb_ = rowbf[h % 2]
                P.ts(rb_[0:1, 0:nt], tq[0:1, 0:nt], -1.0, None, ALU.mult)
                P.dma(QR[64:65, h, 0:nt], rb_[0:1, 0:nt])
            milestone("mla")

            def gdn_all():
                for c0 in range(0, nt, Cg):
                    yield from gdn_chunk(l, st, c0, min(Cg, nt - c0))
            interleave([gdn_all(), attention(st, nt, groups)])
            milestone("attn")
            for m in range(8):
                ta = tmp(); tb_ = tmp()
                for n, Y in enumerate((YA, YB, YC)):
                    pm = lin("br%d_%d" % (l, n), m, Y, nt)
                    if n == 0:
                        P.tt(ta[:, 0:nt], pm, G3[:, 0 * 8 + m, 0:nt], ALU.mult)
                    elif n == 1:
                        P.tt(tb_[:, 0:nt], pm, G3[:, 1 * 8 + m, 0:nt], ALU.mult)
                        P.tt(ta[:, 0:nt], ta[:, 0:nt], tb_[:, 0:nt], ALU.add, eng="pool")
                    else:
                        P.tt(tb_[:, 0:nt], pm, G3[:, 2 * 8 + m, 0:nt], ALU.mult)
                        P.tt(Mg[:, m, 0:nt], ta[:, 0:nt], tb_[:, 0:nt], ALU.add, eng="pool")
            milestone("merge")
            for m in range(8):
                po = lin("out%d" % l, m, Mg, nt)
                tr_ = tmp()
                P.ts(tr_[:, 0:nt], po, g1p[l][:, m, g:g + 1], None, ALU.mult)
                P.stt(X[:, m, 0:nt], X[:, m, 0:nt], ALU_ALPHA, tr_[:, 0:nt], ALU.mult, ALU.add)
            ln_fm(X1, X, 8, nt, lnvs[l][:, 0, :], lnvs[l][:, 1, :])
            milestone("ln1")
            for kc in range(8):
                P.act(Hb[:, kc, 0:nt], X1[:, kc, 0:nt], AF.Identity, bias=modT[l][:, 24 + kc, g:g + 1], scale=scp2[l][:, kc, g:g + 1])
            for j in range(KC_F):
                aj = Aj[j % 2]
                P.copy(aj[:, 0:2], st.Ah[:, j, :], eng="pool")
                pa = lin("up%d" % l, j, Hb, nt)
                P.copy(aj[:, 2:2 + nt], pa, eng="act")
                P.copy(st.Ah[:, j, :], aj[:, nt:nt + 2], eng="pool")
                pvv = lin("up%d" % l, KC_F + j, Hb, nt)
                ca = tmp()
                P.ts(ca[:, 0:nt], aj[:, 0:nt], ffnw[l][:, j, 0:1], None, ALU.mult)
                P.stt(ca[:, 0:nt], aj[:, 1:1 + nt], ffnw[l][:, j, 1:2], ca[:, 0:nt], ALU.mult, ALU.add)
                P.stt(ca[:, 0:nt], aj[:, 2:2 + nt], ffnw[l][:, j, 2:3], ca[:, 0:nt], ALU.mult, ALU.add)
                P.act(ca[:, 0:nt], ca[:, 0:nt], AF.Silu, bias=ffnb[l][:, j:j + 1])
                P.tt(Gf[:, j, 0:nt], ca[:, 0:nt], pvv, ALU.mult)
            for m in range(8):
                py = lin([("dna%d" % l, 0), ("dnb%d" % l, 8), ("dnc%d" % l, 16)], m, Gf, nt)
                tr_ = tmp()
                P.ts(tr_[:, 0:nt], py, g2p[l][:, m, g:g + 1], None, ALU.mult)
                P.stt(X1[:, m, 0:nt], X1[:, m, 0:nt], ALU_ALPHA, tr_[:, 0:nt], ALU.mult, ALU.add)
            ln_fm(X, X1, 8, nt, lnvs[l][:, 2, :], lnvs[l][:, 3, :])

        ALU_ALPHA = float(ALPHA)
        rowbf = [P.sb([1, TB], BF16) for _ in range(2)]

        def init_state_zero(st):
            P.memset(st.Uh[:], 0.0); P.memset(st.Qh[:], 0.0, eng="pool"); P.memset(st.S[:], 0.0)
            P.memset(st.Ah[:], 0.0, eng="pool"); P.memset(st.Kb2[:], 0.0)

        def write_states(l, st, nt, o_ca, o_gc, o_g, o_f):
            P.dma(o_ca.rearrange("(kc p) t -> p kc t", p=128), st.Uh[:, :, :])
            P.dma(o_gc.rearrange("(kc p) t -> p kc t", p=128), st.Qh[:, :, :])
            P.dma(o_g.rearrange("h k v -> k h v"), st.S[:, :, :])
            P.dma(o_f.rearrange("(kc p) t -> p kc t", p=128), st.Ah[:, :, :])

        for l in range(L):
            init_state_zero(stP[l])
        for b in range(NB):
            t0 = b * TB
            P.dma(X[:, :, 0:TB], xpT[:, t0:t0 + TB].rearrange("(kc p) t -> p kc t", p=128))
            ln_fm(X1, X, 8, TB, ln0s[:, 0, :], ln0s[:, 1, :])
            P.copy(X[:, :, 0:TB], X1[:, :, 0:TB], eng="pool")
            for l in range(L):
                groups = [(gb * TB, TB, gb == b) for gb in range(b + 1)]
                block_layer(l, stP[l], TB, 0, ropeP[:, :, t0:t0 + TB], t0, t0, groups,
                            {"lat": o_platT[l], "kpe": o_pkpeT[l]}, min(128, TB))
                if b == 0 and l == 0:
                    dbg("x_l0b0", X[:, :, 0:TB], [128, 8, TB])
            P.dma(ypT[:, t0:t0 + TB].rearrange("(kc p) t -> p kc t", p=128), X[:, :, 0:TB])
        for l in range(L):
            write_states(l, stP[l], TB, o_pcaT[l], o_pgcT[l], o_pg[l], o_pfT[l])

        for s in range(NS):
            P.dma(X[:, :, 0:TS], xsT[:, s * TS:(s + 1) * TS].rearrange("(kc p) t -> p kc t", p=128))
            ln_fm(X1, X, 8, TS, ln0s[:, 0, :], ln0s[:, 1, :])
            P.copy(X[:, :, 0:TS], X1[:, :, 0:TS], eng="pool")
            for l in range(L):
                st = stS
                P.dma(st.Uh[:, :, :], hA[l, s].rearrange("(kc p) t -> p kc t", p=128))
                P.dma(st.Qh[:, :, :], hB[l, s].rearrange("(kc p) t -> p kc t", p=128))
                P.dma(st.S[:, :, :], sG[l, s].rearrange("h k v -> k h v"))
                P.dma(st.Ah[:, :, :], hF[l, s].rearrange("(kc p) t -> p kc t", p=128))
                P.memset(st.Kb2[:], 0.0)
                for k0 in range(0, PAST, TB):
                    nk = min(TB, PAST - k0)
                    P.dma(LAT[:, :, 0:nk], clatT[l, s][:, k0:k0 + nk].rearrange("(kc p) t -> p kc t", p=128))
                    P.copy(LATb[:, :, 0:nk], LAT[:, :, 0:nk])
                    P.dma(KPE[:, 0:nk], ckpeT[l, s][:, k0:k0 + nk])
                    P.copy(KPEb[0:64, 0:nk], KPE[:, 0:nk], eng="pool")
                    P.memset(KPEb[64:65, 0:nk], 1.0, eng="pool")
                    P.dma(st.KR[:, k0:k0 + nk], KPEb[:, 0:nk])
                    P.act(SQ[0:64, 0, 0:nk], KPE[:, 0:nk], AF.Square)
                    knorm(l, st, nk, k0)
                    for sbk in range((nk + 127) // 128):
                        kk = min(128, nk - 128 * sbk)
                        pvv = pst()
                        for kc in range(2):
                            P.mm(pvv[0:kk, :], LATb[:, kc, 128 * sbk:128 * sbk + kk], wuv[l][:, kc, :], kc == 0, kc == 1)
                        P.copy(VVb[0:kk, 4 * sbk:4 * sbk + 4, :].rearrange("p h d -> p (h d)"), pvv[0:kk, :], eng="act")
                        for h in range(4):
                            P.dma(st.VV[h, k0 + 128 * sbk:k0 + 128 * sbk + kk, :], VVb[0:kk, 4 * sbk + h, :],
                                  q="pool")
                groups = [(k0, min(512, PAST - k0), False) for k0 in range(0, PAST, 512)] + [(PAST, TS, False)]
                block_layer(l, st, TS, 1 + s, ropeS[:, :, :], 0, PAST, groups,
                            {"lat": o_slatT[l, s], "kpe": o_skpeT[l, s]}, TS)
                write_states(l, st, TS, o_scaT[l, s], o_sgcT[l, s], o_sg[l, s], o_sfT[l, s])
            P.dma(ysT[:, s * TS:(s + 1) * TS].rearrange("(kc p) t -> p kc t", p=128), X[:, :, 0:TS])


    try:
        main_body()
    except StopBuild:
        pass
    P.wait_all("sp")
    P.finalize()
    return nc, sorted(dbg_out.keys())


import numpy as np
from concourse.bass_utils import run_bass_kernel_spmd

NCORES = 8
FULL_CFG = dict(SEQ=16384, NS=4, PAST=2048, TB=256, L=2)


def _pp(vec, n):
    return np.ascontiguousarray(vec.reshape(n, 128).T)


def _rope_tables(pos):
    half = 32
    inv_freq = np.power(np.float32(10000.0), -(np.arange(half, dtype=np.float32) / np.float32(half))).astype(np.float32)
    ang = pos.astype(np.float32)[:, None] * inv_freq[None, :]
    cos = np.cos(ang).astype(np.float32); sin = np.sin(ang).astype(np.float32)
    cosT = np.concatenate([cos, cos], axis=1).T
    sinT = np.concatenate([-sin, sin], axis=1).T
    return np.ascontiguousarray(np.stack([cosT, sinT], axis=0))


def prep_inputs(inp, cfg):
    SEQ, NS, PAST, L = cfg["SEQ"], cfg["NS"], cfg["PAST"], cfg["L"]
    f = lambda a: np.ascontiguousarray(np.asarray(a, dtype=np.float32))
    perm = np.concatenate([np.arange(32, 64), np.arange(0, 32)])
    sh = {}
    sh["xpT"] = f(np.asarray(inp["x_prompt"])[0].T)
    sh["ln0p"] = f(np.stack([_pp(np.asarray(inp["ln0_g"]), 8), _pp(np.asarray(inp["ln0_b"]), 8)], axis=1))
    sh["w_ada"] = f(inp["w_ada"])
    sh["b_ada_p"] = f(np.stack([_pp(np.asarray(inp["b_ada"])[l], 48) for l in range(L)]))
    w_in = np.asarray(inp["w_in"])
    sh["w_in"] = f(w_in)
    sh["w_kpp"] = f(w_in[:, :, O_KPE + perm])
    caw = np.asarray(inp["conv_a_w"])
    sh["cva_w"] = f(np.stack([caw[l].T.reshape(4, 128, 31).transpose(1, 0, 2) for l in range(L)]))
    sh["cva_v"] = f(np.stack([np.stack([_pp(np.asarray(inp[k])[l], 4) for k in ("conv_a_b", "ln_a_g", "ln_a_b")], axis=1)
                              for l in range(L)]))
    gcw = np.asarray(inp["gdn_conv_w"])
    sh["gcv_w"] = f(np.stack([gcw[l].T.reshape(12, 128, 4).transpose(1, 0, 2) for l in range(L)]))
    sh["gdn_s"] = f(np.stack([np.asarray(inp["gdn_a_log"]), np.asarray(inp["gdn_dt_bias"])], axis=-1))
    sh["gdn_ng"] = f(np.asarray(inp["gdn_norm_g"])[:, :, None])
    sh["qn_g"] = f(np.stack([_pp(np.asarray(inp["mla_q_norm_g"])[l], 3) for l in range(L)]))
    sh["kvn_g"] = f(np.stack([_pp(np.asarray(inp["mla_kv_norm_g"])[l], 2) for l in range(L)]))
    wuq = np.asarray(inp["mla_w_uq"]).reshape(L, 384, 4, 192)
    sh["w_uqn"] = f(wuq[:, :, :, :128].reshape(L, 384, 512))
    sh["w_uqr"] = f(wuq[:, :, :, 128:].reshape(L, 384, 256))
    sh["w_uqp"] = f(wuq[:, :, :, 128 + perm].reshape(L, 384, 256))
    wukv = np.asarray(inp["mla_w_ukv"]).reshape(L, 256, 4, 256)
    sh["w_uk"] = f(wukv[:, :, :, :128].reshape(L, 256, 512))
    sh["w_uv"] = f(wukv[:, :, :, 128:].reshape(L, 256, 512))
    sh["w_br"] = f(inp["w_branch"]); sh["w_out"] = f(inp["w_out"])
    sh["lnv"] = f(np.stack([np.stack([_pp(np.asarray(inp[k])[l], 8) for k in ("ln1_g", "ln1_b", "ln2_g", "ln2_b")], axis=1)
                            for l in range(L)]))
    sh["w_up"] = f(inp["w_up"])
    fcw = np.asarray(inp["ffn_conv_w"])
    sh["ffn_w"] = f(np.stack([fcw[l].T.reshape(KC_F, 128, 3).transpose(1, 0, 2) for l in range(L)]))
    sh["ffn_b"] = f(np.stack([_pp(np.asarray(inp["ffn_conv_b"])[l], KC_F) for l in range(L)]))
    sh["w_dn"] = f(inp["w_down"])
    sh["c_ident"] = np.eye(128, dtype=np.float32)
    pi = np.arange(128)[:, None]; fi = np.arange(128)[None, :]
    sh["c_triu"] = (pi <= fi).astype(np.float32)
    sh["c_mtri"] = (pi >= fi).astype(np.float32)
    sh["c_mtriT"] = (pi <= fi).astype(np.float32)
    sh["c_mstr"] = (pi > fi).astype(np.float32)
    sh["ropeP"] = _rope_tables(np.arange(SEQ))
    sh["ropeS"] = _rope_tables(PAST + np.arange(16))
    xs = np.asarray(inp["x_sample"]); cs = np.asarray(inp["c_sample"]); cp = np.asarray(inp["c_prompt"])
    in_maps = []
    for c in range(NCORES):
        sl = slice(c * NS, (c + 1) * NS)
        m = dict(sh)
        m["xsT"] = f(xs[sl].reshape(NS * 16, D).T)
        m["cT"] = f(np.concatenate([cp[0:1], cs[sl]], axis=0).T)
        m["clatT"] = f(np.asarray(inp["cache_mla_latent"])[:, sl].transpose(0, 1, 3, 2))
        m["ckpeT"] = f(np.asarray(inp["cache_mla_kpe"])[:, sl].transpose(0, 1, 3, 2))
        m["hA"] = f(np.asarray(inp["state_conv_a"])[:, sl].transpose(0, 1, 3, 2))
        m["hB"] = f(np.asarray(inp["state_gdn_conv"])[:, sl].transpose(0, 1, 3, 2))
        m["sG"] = f(np.asarray(inp["state_gdn"])[:, sl])
        m["hF"] = f(np.asarray(inp["state_ffn_conv"])[:, sl].transpose(0, 1, 3, 2))
        in_maps.append(m)
    return in_maps


def assemble(res, cfg):
    NS = cfg["NS"]
    r0 = res[0]
    T = lambda a: np.ascontiguousarray(np.swapaxes(a, -1, -2))
    y_p = T(r0["ypT"])[None]
    y_s = np.concatenate([T(r["ysT"]).reshape(NS, 16, D) for r in res], axis=0)
    p_lat = T(r0["o_platT"])[:, None]; p_kpe = T(r0["o_pkpeT"])[:, None]
    p_ca = T(r0["o_pcaT"])[:, None]; p_gc = T(r0["o_pgcT"])[:, None]; p_g = r0["o_pg"][:, None]; p_f = T(r0["o_pfT"])[:, None]
    cat = lambda k, tr: np.concatenate([(T(r[k]) if tr else r[k]) for r in res], axis=1)
    outs = (y_p, y_s, p_lat, p_kpe, p_ca, p_gc, p_g, p_f,
            cat("o_slatT", True), cat("o_skpeT", True), cat("o_scaT", True), cat("o_sgcT", True), cat("o_sg", False),
            cat("o_sfT", True))
    return tuple(np.ascontiguousarray(o, dtype=np.float32) for o in outs)


def run(inputs, cfg, debug=()):
    nc, dbg_names = build_program(cfg, debug)
    in_maps = prep_inputs(inputs, cfg)
    res = run_bass_kernel_spmd(nc, in_maps, core_ids=list(range(NCORES)))
    return assemble(res.results, cfg), res.results


def kernel(**inputs):
    outs, _ = run(inputs, FULL_CFG)
    return outs
```
